# Optimizing a Trainium2 kernel written in Bass

```python
import math
import jax, jax.numpy as jnp
from jax import lax
import numpy as np

D_MODEL = 1024
BATCH = 4
SEQ = 8192
DEPTH = 1

SB_HEADS = 8
SB_HEAD_DIM = 64
SB_WIDTH = SB_HEADS * SB_HEAD_DIM
SB_BLOCK = 128
SW_HEADS = 8
SW_KV_HEADS = 2
SW_GROUP = SW_HEADS // SW_KV_HEADS
SW_HEAD_DIM = 64
SW_Q_WIDTH = SW_HEADS * SW_HEAD_DIM
SW_KV_WIDTH = SW_KV_HEADS * SW_HEAD_DIM
WINDOW = 128
NUM_BUCKETS = 32
MAX_DISTANCE = 128
D_FF = 2816
CONV_WIDTH = 3
EPS = 1e-6
NEG_INF = -1e30

IN_WIDTHS = (SB_WIDTH, SB_WIDTH, SB_WIDTH, SW_Q_WIDTH, SW_KV_WIDTH, SW_KV_WIDTH, 2 * D_MODEL)
IN_TOTAL = sum(IN_WIDTHS)
IN_SPLITS = tuple(int(s) for s in np.cumsum(IN_WIDTHS)[:-1])

kernel_name = "hybrid_stickbreak_swa_convffn_block"


def rms_norm(x, g):
    xf = x.astype(jnp.float32)
    y = xf * lax.rsqrt(jnp.mean(xf * xf, axis=-1, keepdims=True) + EPS)
    return (y * g.astype(jnp.float32)).astype(x.dtype)


def t5_causal_bucket(dist):
    max_exact = NUM_BUCKETS // 2
    is_small = dist < max_exact
    d = jnp.maximum(dist, 1).astype(jnp.float32)
    large = max_exact + (jnp.log(d / max_exact) / math.log(MAX_DISTANCE / max_exact)
                         * (NUM_BUCKETS - max_exact)).astype(jnp.int32)
    large = jnp.minimum(large, NUM_BUCKETS - 1)
    return jnp.where(is_small, dist, large)


def stick_breaking_attention(q, k, v):
    B, S, H, dh = q.shape
    nb = S // SB_BLOCK
    scale = 1.0 / math.sqrt(dh)
    to_blocks = lambda a: a.reshape(B, nb, SB_BLOCK, H, dh).transpose(1, 0, 3, 2, 4)
    qb_all, kb_all, vb_all = to_blocks(q), to_blocks(k), to_blocks(v)
    idx = jnp.arange(SB_BLOCK)

    def per_query_block(args):
        qi, qblk = args

        def body(carry, j):
            acc, log_rem = carry
            ki_raw = qi - j
            block_live = ki_raw >= 0
            ki = jnp.maximum(ki_raw, 0)
            kblk = kb_all[ki]
            vblk = vb_all[ki]
            z = jnp.einsum('bhqd,bhkd->bhqk', qblk, kblk).astype(jnp.float32) * scale
            valid = ((ki * SB_BLOCK + idx[None, :]) < (qi * SB_BLOCK + idx[:, None])) & block_live
            log_beta = jax.nn.log_sigmoid(z)
            log_1mb = jnp.where(valid, jax.nn.log_sigmoid(-z), 0.0)
            later = lax.cumsum(log_1mb, axis=3, reverse=True) - log_1mb
            w = jnp.where(valid, jnp.exp(log_beta + later + log_rem[..., None]), 0.0)
            acc = acc + jnp.einsum('bhqk,bhkd->bhqd', w, vblk.astype(jnp.float32))
            log_rem = log_rem + jnp.sum(log_1mb, axis=-1)
            return (acc, log_rem), None

        init = (jnp.zeros((B, H, SB_BLOCK, dh), jnp.float32),
                jnp.zeros((B, H, SB_BLOCK), jnp.float32))
        (acc, _), _ = lax.scan(body, init, jnp.arange(nb, dtype=jnp.int32))
        return acc

    out = lax.map(per_query_block, (jnp.arange(nb, dtype=jnp.int32), qb_all))
    return out.transpose(1, 0, 3, 2, 4).reshape(B, S, H * dh).astype(q.dtype)


def sliding_window_gqa(q, k, v, sinks, rel_bias):
    B, S, H, dh = q.shape
    nb = S // WINDOW
    scale = 1.0 / math.sqrt(dh)
    qb = q.reshape(B, nb, WINDOW, SW_KV_HEADS, SW_GROUP, dh)
    kb = k.reshape(B, nb, WINDOW, SW_KV_HEADS, dh)
    vb = v.reshape(B, nb, WINDOW, SW_KV_HEADS, dh)
    pad = ((0, 0), (1, 0), (0, 0), (0, 0), (0, 0))
    kk = jnp.concatenate([jnp.pad(kb[:, :-1], pad), kb], axis=2)
    vv = jnp.concatenate([jnp.pad(vb[:, :-1], pad), vb], axis=2)

    qpos = jnp.arange(WINDOW)[:, None] + WINDOW
    kpos = jnp.arange(2 * WINDOW)[None, :]
    dist = qpos - kpos
    valid_local = (dist >= 0) & (dist < WINDOW)
    block_ok = (jnp.arange(nb)[:, None] > 0) | (jnp.arange(2 * WINDOW)[None, :] >= WINDOW)
    valid = valid_local[None] & block_ok[:, None, :]

    bucket = t5_causal_bucket(jnp.maximum(dist, 0))
    bias = rel_bias.astype(jnp.float32)[bucket]
    bias = bias.transpose(2, 0, 1).reshape(SW_KV_HEADS, SW_GROUP, WINDOW, 2 * WINDOW)

    logits = jnp.einsum('bnqhgd,bnkhd->bnhgqk', qb, kk).astype(jnp.float32) * scale + bias
    logits = jnp.where(valid[None, :, None, None], logits, NEG_INF)
    sink = sinks.astype(jnp.float32).reshape(SW_KV_HEADS, SW_GROUP)[None, None, :, :, None, None]
    m = jnp.maximum(jnp.max(logits, axis=-1, keepdims=True), sink)
    p = jnp.exp(logits - m)
    probs = p / (jnp.sum(p, axis=-1, keepdims=True) + jnp.exp(sink - m))
    out = jnp.einsum('bnhgqk,bnkhd->bnqhgd', probs.astype(v.dtype), vv)
    return out.reshape(B, S, H * dh)


def causal_depthwise_conv(u, w, b):
    C = u.shape[-1]
    y = lax.conv_general_dilated(u, w[:, None, :].astype(u.dtype), window_strides=(1,),
                                 padding=[(CONV_WIDTH - 1, 0)],
                                 dimension_numbers=('NWC', 'WIO', 'NWC'),
                                 feature_group_count=C)
    return y + b.astype(u.dtype)


def setup_inputs(seed: int = 0) -> dict:
    key = jax.random.key(seed)
    ks = jax.random.split(key, 16)
    f32 = jnp.float32
    nrm = lambda k, shape, s: jax.random.normal(k, shape, f32) * s
    return {
        "x": nrm(ks[0], (BATCH, SEQ, D_MODEL), 1.0),
        "g_mix": 1.0 + nrm(ks[1], (D_MODEL,), 0.01),
        "w_in": nrm(ks[2], (D_MODEL, IN_TOTAL), D_MODEL ** -0.5),
        "w_sb_proj": nrm(ks[3], (SB_WIDTH, D_MODEL), SB_WIDTH ** -0.5),
        "w_sw_proj": nrm(ks[4], (SW_Q_WIDTH, D_MODEL), SW_Q_WIDTH ** -0.5),
        "w_out": nrm(ks[5], (D_MODEL, D_MODEL), D_MODEL ** -0.5),
        "rel_bias": nrm(ks[6], (NUM_BUCKETS, SW_HEADS), 0.2),
        "sinks": nrm(ks[7], (SW_HEADS,), 1.0),
        "g_ffn": 1.0 + nrm(ks[8], (D_MODEL,), 0.01),
        "w_up": nrm(ks[9], (D_MODEL, 2 * D_FF), D_MODEL ** -0.5),
        "conv_w": nrm(ks[10], (CONV_WIDTH, 2 * D_FF), CONV_WIDTH ** -0.5),
        "conv_b": nrm(ks[11], (2 * D_FF,), 0.01),
        "w_down": nrm(ks[12], (D_FF, D_MODEL), D_FF ** -0.5),
        "g_final": 1.0 + nrm(ks[13], (D_MODEL,), 0.01),
    }


def reference(x, g_mix, w_in, w_sb_proj, w_sw_proj, w_out, rel_bias, sinks,
              g_ffn, w_up, conv_w, conv_b, w_down, g_final):
    B, S, _ = x.shape
    for _layer in range(DEPTH):
        h = rms_norm(x, g_mix)
        proj = h @ w_in
        q_sb, k_sb, v_sb, q_sw, k_sw, v_sw, gate_logits = jnp.split(proj, IN_SPLITS, axis=-1)
        y_sb = stick_breaking_attention(
            q_sb.reshape(B, S, SB_HEADS, SB_HEAD_DIM),
            k_sb.reshape(B, S, SB_HEADS, SB_HEAD_DIM),
            v_sb.reshape(B, S, SB_HEADS, SB_HEAD_DIM))
        y_sw = sliding_window_gqa(
            q_sw.reshape(B, S, SW_HEADS, SW_HEAD_DIM),
            k_sw.reshape(B, S, SW_KV_HEADS, SW_HEAD_DIM),
            v_sw.reshape(B, S, SW_KV_HEADS, SW_HEAD_DIM),
            sinks, rel_bias)
        gates = jax.nn.sigmoid(gate_logits.astype(jnp.float32))
        gate_sb, gate_sw = gates[..., :D_MODEL], gates[..., D_MODEL:]
        merged = (gate_sb * (y_sb @ w_sb_proj).astype(jnp.float32)
                  + gate_sw * (y_sw @ w_sw_proj).astype(jnp.float32)).astype(x.dtype)
        x = x + merged @ w_out
        h2 = rms_norm(x, g_ffn)
        u = causal_depthwise_conv(h2 @ w_up, conv_w, conv_b)
        gate_ff, value_ff = u[..., :D_FF], u[..., D_FF:]
        x = x + (jax.nn.silu(gate_ff) * value_ff) @ w_down
    return rms_norm(x, g_final)
```

```python
import numpy as np
from contextlib import ExitStack
import concourse.bass as bass
import concourse.mybir as mybir
from concourse.bass_utils import run_bass_kernel_spmd

F32 = mybir.dt.float32
BF16 = mybir.dt.bfloat16
AF = mybir.ActivationFunctionType
ALU = mybir.AluOpType

D = 1024
KT = 8
DFF = 2816
NEGM = -100.0
ENGS = ("pe", "act", "dve", "pool", "sp")
NDMA = 24
NDMA_HW = 16


class Buf:
    __slots__ = ("w", "r", "name")

    def __init__(self, name=""):
        self.w = None
        self.r = {}
        self.name = name


class V:
    __slots__ = ("buf", "ap")

    def __init__(self, buf, ap):
        self.buf = buf
        self.ap = ap


class Tile:
    def __init__(self, handle, name=""):
        self.t = handle
        self.buf = Buf(name)

    def __getitem__(self, idx):
        return V(self.buf, self.t[idx])


class Sched:
    def __init__(self):
        self.keys = list(ENGS) + [("d", i) for i in range(NDMA)]
        self.cnt = {k: 0 for k in self.keys}
        self.seen = {e: {k: 0 for k in self.keys} for e in ENGS}
        self.q = {e: [] for e in ENGS}
        self.dma_rr = 0
        self.dma_rr2 = 0
        self.nops = 0

    def op(self, eng, fns, reads=(), writes=(), dma=False):
        if callable(fns):
            fns = [fns]
        need = {}

        def req(tok):
            if tok is None:
                return
            k, v = tok
            if need.get(k, 0) < v:
                need[k] = v

        for b in reads:
            req(b.w)
        for b in writes:
            req(b.w)
            for k, v in b.r.items():
                req((k, v))
        if dma:
            if eng == "sp":
                key = ("d", self.dma_rr)
                self.dma_rr = (self.dma_rr + 1) % NDMA_HW
            else:
                key = ("d", NDMA_HW + self.dma_rr2)
                self.dma_rr2 = (self.dma_rr2 + 1) % (NDMA - NDMA_HW)
            req((key, self.cnt[key]))
            inc = 16
        else:
            key = eng
            inc = 1
        waits = []
        seen = self.seen[eng]
        for k, v in need.items():
            if v <= 0 or seen[k] >= v:
                continue
            if k == eng and eng == "pe":
                continue
            seen[k] = v
            waits.append((k, v))
        self.cnt[key] += inc
        tok = (key, self.cnt[key])
        for b in writes:
            b.w = tok
            b.r = {}
        for b in reads:
            if b.r.get(key, 0) < tok[1]:
                b.r[key] = tok[1]
        self.q[eng].append((waits, fns, key, inc))
        self.nops += 1
        return tok

    def barrier(self):
        for e in ENGS:
            waits = []
            for k in self.keys:
                v = self.cnt[k]
                if v > 0 and self.seen[e][k] < v and k != e:
                    self.seen[e][k] = v
                    waits.append((k, v))
            if waits:
                self.q[e].append((waits, [], None, 0))

    def replay(self, eng, e, sems):
        for waits, fns, key, inc in self.q[eng]:
            for k, v in waits:
                e.wait_ge(sems[k], v)
            ins = None
            for f in fns:
                ins = f(e)
            if ins is not None and key is not None:
                ins.then_inc(sems[key], inc)


def build_nc(NH, phases="A,B1,B2,C1,C2a,C2b"):
    H = NH * 512
    L2 = 2 * H
    NB2 = L2 // 128
    NBH = H // 128
    NOWN = (NH + 1) * 512
    NY = 2 + NH * 512
    NSWB = NH * 4 + 1

    nc = bass.Bass("TRN2", target_bir_lowering=False)
    S = Sched()

    def din(name, shape):
        return nc.dram_tensor(name, shape, F32, kind="ExternalInput").ap()

    xT = din("xT", [D, L2])
    kbias_d = din("kbias", [128, NB2])
    w_in = din("w_in", [D, 4352])
    w_sbp = din("w_sbp", [512, D])
    w_swp = din("w_swp", [512, D])
    w_out = din("w_out", [D, D])
    w_up = din("w_up", [D, 2 * DFF])
    w_down = din("w_down", [DFF, D])
    gm_d = din("gm", [128, 8])
    gf_d = din("gf", [128, 8])
    gl_d = din("gl", [128, 8])
    cw_d = din("cw", [128, 44 * 3])
    cb_d = din("cb", [128, 44])
    bm_d = din("bm", [128, 2048])
    sk_d = din("sk", [128, 4])
    cst_d = din("cst", [128, 2688])
    out_d = nc.dram_tensor("out", [8, 128, H], F32, kind="ExternalOutput").ap()

    def dscr(name, shape, dt):
        return nc.dram_tensor(name, shape, dt, kind="Internal").ap()

    KTs = dscr("KTs", [4, 128, L2], BF16)
    QTs = dscr("QTs", [4, 128, NOWN], BF16)
    VSs = dscr("VSs", [4, 128, NB2, 256], BF16)
    QWs = dscr("QWs", [4, 128, NOWN], BF16)
    KWs = dscr("KWs", [2, 128, NOWN], BF16)
    VWs = dscr("VWs", [128, (NH + 1) * 4, 512], BF16)
    YSB = dscr("YSB", [128, 4, NY], BF16)
    YSW = dscr("YSW", [128, 4, NSWB * 128], BF16)
    X1s = dscr("X1s", [128, 8, NY], F32)
    ATs = dscr("ATs", [128, 22, NY], BF16)

    groups = [(H - 2, 2, 0)] + [(H + 512 * c, 512, 2 + 512 * c) for c in range(NH)]

    def mm(out, lhsT, rhs, start, stop):
        return lambda e: e.matmul(out.ap, lhsT.ap, rhs.ap, start=start, stop=stop)

    def pe_group(out, pairs, extra_reads=()):
        fns = []
        n = len(pairs)
        rd = []
        for i, (l, r) in enumerate(pairs):
            fns.append(mm(out, l, r, i == 0, i == n - 1))
            rd += [l.buf, r.buf]
        S.op("pe", fns, reads=rd + list(extra_reads), writes=[out.buf])

    def act(eng_unused, out, in_, func, scale=1.0, bias=0.0, extra_reads=()):
        def f(e):
            kw = {}
            if isinstance(bias, V):
                kw["bias"] = bias.ap
            else:
                kw["bias"] = float(bias)
            if isinstance(scale, V):
                kw["scale"] = scale.ap
            else:
                kw["scale"] = float(scale)
            return e.activation(out.ap, in_.ap, func, **kw)

        rd = [in_.buf] + [x.buf for x in (scale, bias) if isinstance(x, V)] + list(extra_reads)
        S.op("act", f, reads=rd, writes=[out.buf])

    def tt(eng, out, a, b, op):
        S.op(eng, lambda e: e.tensor_tensor(out.ap, a.ap, b.ap, op), reads=[a.buf, b.buf], writes=[out.buf])

    def ts(eng, out, a, s1, op0, s2=None, op1=None):
        def f(e):
            sc1 = s1.ap if isinstance(s1, V) else float(s1)
            if s2 is None:
                return e.tensor_scalar(out.ap, a.ap, sc1, None, op0)
            sc2 = s2.ap if isinstance(s2, V) else float(s2)
            return e.tensor_scalar(out.ap, a.ap, sc1, sc2, op0, op1)

        rd = [a.buf] + [x.buf for x in (s1, s2) if isinstance(x, V)]
        S.op(eng, f, reads=rd, writes=[out.buf])

    def stt(out, a, sc, b, op0, op1):
        def f(e):
            s = sc.ap if isinstance(sc, V) else float(sc)
            return e.scalar_tensor_tensor(out.ap, a.ap, s, b.ap, op0, op1)

        rd = [a.buf, b.buf] + ([sc.buf] if isinstance(sc, V) else [])
        S.op("dve", f, reads=rd, writes=[out.buf])

    def cp(eng, out, a):
        if eng == "act":
            act(None, out, a, AF.Copy)
        else:
            S.op(eng, lambda e: e.tensor_copy(out.ap, a.ap), reads=[a.buf], writes=[out.buf])

    def memset(eng, out, val):
        S.op(eng, lambda e: e.memset(out.ap, val), writes=[out.buf])

    def load(out, src_ap, eng="sp"):
        S.op(eng, lambda e: e.dma_start(out=out.ap, in_=src_ap), writes=[out.buf], dma=True)

    def store(dst_ap, src, eng="pool"):
        S.op(eng, lambda e: e.dma_start(out=dst_ap, in_=src.ap), reads=[src.buf], dma=True)

    class Pool_:
        def __init__(self):
            self.es = ExitStack()

        def sb(self, name, shape, dt):
            return Tile(self.es.enter_context(nc.sbuf_tensor(name, shape, dt)), name)

        def ps(self, name, shape):
            return Tile(self.es.enter_context(nc.psum_tensor(name, shape, F32)), name)

        def close(self):
            self.es.close()

    gl = Pool_()
    ones = gl.sb("ones", [128, 128], BF16)
    tri = gl.sb("tri", [128, 128], BF16)
    ident = gl.sb("ident", [128, 128], BF16)
    dmask = gl.sb("dmask", [128, 4, 512], BF16)
    opad = gl.sb("opad", [128, 2, 128], BF16)
    kbias = gl.sb("kbias_s", [128, NB2], F32)
    gtmp = Pool_()
    cstf = gtmp.sb("cstf", [128, 2688], F32)

    load(cstf[:], cst_d[:, :])
    load(kbias[:], kbias_d[:, :])
    cp("dve", ones[:], cstf[:, 0:128])
    cp("dve", tri[:], cstf[:, 128:256])
    cp("dve", ident[:], cstf[:, 256:384])
    for j in range(4):
        cp("dve", dmask[:, j, :], cstf[:, 384 + 512 * j:384 + 512 * (j + 1)])
    for j in range(2):
        cp("dve", opad[:, j, :], cstf[:, 2432 + 128 * j:2432 + 128 * (j + 1)])

    def rms_stats(P, xs, n, sq, ssps, lnv, rs):
        act(None, sq[:, :, 0:n], xs[:, :, 0:n], AF.Square)
        pe_group(ssps[:, 0:n], [(ones[:], sq[:, k, 0:n]) for k in range(8)])
        act(None, lnv[:, 0:n], ssps[:, 0:n], AF.Ln, scale=1.0 / D, bias=1e-6)
        act(None, rs[:, 0:n], lnv[:, 0:n], AF.Exp, scale=-0.5)

    def phase_A():
        P = Pool_()
        WA = P.sb("WA", [128, 8, 3328], BF16)
        stg = [P.sb(f"wstg{i}", [128, 2304], F32) for i in range(2)]
        gm = P.sb("gm_s", [128, 8], F32)
        xs = [P.sb(f"xsA{i}", [128, 8, 512], F32) for i in range(2)]
        sq = P.sb("sqA", [128, 8, 512], BF16)
        hT = [P.sb(f"hTA{i}", [128, 8, 512], BF16) for i in range(2)]
        lnv = P.sb("lnvA", [128, 512], F32)
        rs = [P.sb(f"rsA{i}", [128, 512], F32) for i in range(2)]
        stF = [P.sb(f"stF{i}", [128, 14, 512], BF16) for i in range(2)]
        stV = [P.sb(f"stV{i}", [128, 4, 1536], BF16) for i in range(2)]
        ssps = P.ps("ssA", [128, 512])
        PS = [P.ps(f"psA{i}", [128, 512]) for i in range(6)]

        load(gm[:], gm_d[:, :])
        memset("pool", WA[:, :, 1792:3328], 0.0)
        for kt in range(8):
            st = stg[kt % 2]
            load(st[:], w_in[kt * 128:(kt + 1) * 128, 0:2304])
            g = gm[:, kt:kt + 1]
            e1 = "dve" if kt % 2 == 0 else "pool"
            ts(e1, WA[:, kt, 0:1024], st[:, 0:1024], g, ALU.mult)
            ts(e1, WA[:, kt, 1024:1536], st[:, 1536:2048], g, ALU.mult)
            for gg in range(2):
                for r in range(2):
                    ts(e1, WA[:, kt, 1536 + 128 * gg + 64 * r:1536 + 128 * gg + 64 * (r + 1)],
                       st[:, 2048 + 64 * gg:2048 + 64 * (gg + 1)], g, ALU.mult)
            for h in range(8):
                o = 1792 + 128 * h + 64 * (h % 2)
                ts(e1, WA[:, kt, o:o + 64], st[:, 1024 + 64 * h:1024 + 64 * (h + 1)], g, ALU.mult)
            for i in range(4):
                o = 2816 + 128 * i + 64 * (i % 2)
                gg = i // 2
                ts(e1, WA[:, kt, o:o + 64], st[:, 2176 + 64 * gg:2176 + 64 * (gg + 1)], g, ALU.mult)

        psi = [0]

        def nextps():
            psi[0] = (psi[0] + 1) % 6
            return PS[psi[0]]

        evi = [0]

        def evac(out, in_):
            evi[0] += 1
            cp("act" if evi[0] % 2 == 0 else "dve", out, in_)

        def stage1(tc):
            s = tc % 2
            load(xs[s][:], xT[:, tc * 512:(tc + 1) * 512].rearrange("(k p) n -> p k n", p=128))
            rms_stats(P, xs[s], 512, sq, ssps, lnv, rs[s])
            for k in range(8):
                tt("dve" if k % 2 == 0 else "pool", hT[s][:, k, :], xs[s][:, k, :], rs[s][:], ALU.mult)

        def stage2(tc):
            s = tc % 2
            own = tc >= NH - 1
            h = hT[s]
            sf = stF[s]
            sv = stV[s]

            def fm(slot, col0):
                ps = nextps()
                pe_group(ps[:], [(WA[:, k, col0:col0 + 128], h[:, k, :]) for k in range(8)])
                evac(sf[:, slot, :], ps[:])

            for hp in range(4):
                fm(hp, 512 + 128 * hp)
            if own:
                for hp in range(4):
                    fm(4 + hp, 128 * hp)
                for hp in range(4):
                    fm(8 + hp, 1024 + 128 * hp)
                for gg in range(2):
                    fm(12 + gg, 1536 + 128 * gg)
            for blk in range(4):
                for half in range(2):
                    ps = nextps()
                    pe_group(ps[:], [(h[:, k, blk * 128:(blk + 1) * 128],
                                      WA[:, k, 1792 + 512 * half:1792 + 512 * (half + 1)]) for k in range(8)])
                    evac(sv[:, blk, 512 * half:512 * (half + 1)], ps[:])
                if own:
                    ps = nextps()
                    pe_group(ps[:], [(h[:, k, blk * 128:(blk + 1) * 128], WA[:, k, 2816:3328]) for k in range(8)])
                    evac(sv[:, blk, 1024:1536], ps[:])
            c0 = tc * 512
            store(KTs[:, :, c0:c0 + 512].rearrange("h p n -> p h n"), sf[:, 0:4, :])
            for hp in range(4):
                store(VSs[hp, :, 4 * tc:4 * tc + 4, :], sv[:, :, 256 * hp:256 * (hp + 1)])
            if own:
                o0 = (tc - (NH - 1)) * 512
                store(QTs[:, :, o0:o0 + 512].rearrange("h p n -> p h n"), sf[:, 4:8, :])
                store(QWs[:, :, o0:o0 + 512].rearrange("h p n -> p h n"), sf[:, 8:12, :])
                store(KWs[:, :, o0:o0 + 512].rearrange("h p n -> p h n"), sf[:, 12:14, :])
                b0 = (tc - (NH - 1)) * 4
                store(VWs[:, b0:b0 + 4, :], sv[:, :, 1024:1536])

        NT = 2 * NH
        stage1(0)
        for tc in range(NT):
            if tc + 1 < NT:
                stage1(tc + 1)
            stage2(tc)
        S.barrier()
        P.close()

    def phase_B1():
        P = Pool_()
        QW = P.sb("QW", [128, 4, NOWN], BF16)
        KW = P.sb("KW", [128, 2, NOWN], BF16)
        VW = P.sb("VW", [128, (NH + 1) * 4, 512], BF16)
        BM = P.sb("BM", [128, 2048], F32)
        sk = P.sb("sk_s", [128, 4], F32)
        esk = P.sb("esk", [128, 4], F32)
        ESB = P.sb("ESB", [128, 4, 128], F32)
        zer = P.sb("zerB1", [128, 128], F32)
        LG = [P.sb(f"LG{i}", [128, 2048], F32) for i in range(2)]
        PB = [P.sb(f"PB{i}", [128, 2, 2, 4, 128], BF16) for i in range(2)]
        den = P.sb("den", [128, 512], F32)
        rec = P.sb("rec", [128, 512], F32)
        yst = [P.sb(f"yst{i}", [128, 4, 128], BF16) for i in range(2)]
        ZS = P.ps("ZS", [128, 2, 2, 4, 128])
        OP = P.ps("OPs", [128, 4, 128])
        DN = P.ps("DNs", [128, 4, 128])

        load(QW[:], QWs.rearrange("h p n -> p h n"))
        load(KW[:], KWs.rearrange("h p n -> p h n"))
        load(VW[:], VWs[:, :, :])
        load(BM[:], bm_d[:, :])
        load(sk[:], sk_d[:, :])
        act(None, esk[:], sk[:], AF.Exp)
        memset("dve", zer[:], 0.0)
        for hp in range(4):
            ts("dve", ESB[:, hp, :], zer[:], esk[:, hp:hp + 1], ALU.add)

        for qi, i in enumerate(range(3, (NH + 1) * 4)):
            s = qi % 2
            qcols = slice(i * 128, (i + 1) * 128)
            for kbsel in range(2):
                ib = i - 1 + kbsel
                for h in range(8):
                    po = (h % 2) * 64
                    g = h // 4
                    hp = h // 2
                    S.op("pe", mm(ZS[:, kbsel, h % 2, hp, :], KW[po:po + 64, g, ib * 128:(ib + 1) * 128],
                                  QW[po:po + 64, hp, qcols], True, True),
                         reads=[KW.buf, QW.buf], writes=[ZS.buf])
            stt(LG[s][:], V(ZS.buf, ZS.t[:].rearrange("p a r h q -> p (a r h q)")), 0.125, BM[:], ALU.mult, ALU.add)
            for kbsel in range(2):
                lb = (NBH - 4) + i - 1 + kbsel
                act(None, V(PB[s].buf, PB[s].t[:, kbsel, :, :, :].rearrange("p r h q -> p (r h q)")),
                    LG[s][:, kbsel * 1024:(kbsel + 1) * 1024], AF.Exp, bias=kbias[:, lb:lb + 1])
            for hp in range(4):
                g = hp // 2
                prs = []
                prd = []
                for kbsel in range(2):
                    ib = i - 1 + kbsel
                    for r in range(2):
                        h = 2 * hp + r
                        var = 2 * g + r
                        prs.append((VW[:, ib, 128 * var:128 * (var + 1)], PB[s][:, kbsel, r, hp, :]))
                        prd.append((opad[:, r, :], PB[s][:, kbsel, r, hp, :]))
                pe_group(OP[:, hp, :], prs)
                pe_group(DN[:, hp, :], prd)
            tt("dve", den[:], V(DN.buf, DN.t[:].rearrange("p a q -> p (a q)")),
               V(ESB.buf, ESB.t[:].rearrange("p a q -> p (a q)")), ALU.add)
            S.op("dve", lambda e: e.reciprocal(rec.t[:], den.t[:]), reads=[den.buf], writes=[rec.buf])
            tt("dve", V(yst[s].buf, yst[s].t[:].rearrange("p a q -> p (a q)")),
               V(OP.buf, OP.t[:].rearrange("p a q -> p (a q)")), rec[:], ALU.mult)
            store(YSW[:, :, qi * 128:(qi + 1) * 128], yst[s][:])
        S.barrier()
        P.close()

    def phase_B2():
        P = Pool_()
        KTt = [P.sb(f"KTt{i}", [128, L2], BF16) for i in range(2)]
        Vt = [P.sb(f"Vt{i}", [128, NB2, 256], BF16) for i in range(2)]
        QTt = [P.sb(f"QTt{i}", [128, NOWN], BF16) for i in range(2)]
        NBUF = 3
        Eb = [P.sb(f"Eb{i}", [128, 2, 512], BF16) for i in range(NBUF)]
        Lb = [P.sb(f"Lb{i}", [128, 2, 512], BF16) for i in range(NBUF)]
        Gb = [P.sb(f"Gb{i}", [128, 2, 512], BF16) for i in range(NBUF)]
        Wb = [P.sb(f"Wb{i}", [128, 2, 512], BF16) for i in range(NBUF)]
        Sb = [P.sb(f"Sb{i}", [128, 2, 512], F32) for i in range(NBUF)]
        Racc = P.sb("Racc", [128, 2, 512], F32)
        yev = [P.sb(f"yev{i}", [128, 512], BF16) for i in range(2)]
        Z = P.ps("Zp", [128, 2, 512])
        PQ = P.ps("PQp", [128, 2, 2, 512])
        Y = [P.ps(f"Yp{i}", [128, 512]) for i in range(2)]

        def load_hp(hp):
            s = hp % 2
            hl = L2 // 2
            load(KTt[s][:, 0:hl], KTs[hp, :, 0:hl])
            load(KTt[s][:, hl:L2], KTs[hp, :, hl:L2])
            load(Vt[s][:, 0:NB2 // 2, :], VSs[hp, :, 0:NB2 // 2, :])
            load(Vt[s][:, NB2 // 2:NB2, :], VSs[hp, :, NB2 // 2:NB2, :])
            load(QTt[s][:], QTs[hp, :, :])

        gi = [0]
        load_hp(0)
        for hp in range(4):
            s = hp % 2
            if hp + 1 < 4:
                load_hp(hp + 1)
            Kt, Vv, Qt = KTt[s], Vt[s], QTt[s]
            for (qs, n, ycol) in groups:
                qc = qs - (NH - 1) * 512
                kbmax = (qs + n - 2) // 128
                kbs = list(range(kbmax, -1, -1))
                U = len(kbs)
                Yp = Y[gi[0] % 2]
                ye = yev[gi[0] % 2]
                gi[0] += 1
                memset("dve", Racc[:, :, 0:n], 0.0)

                def s0(u):
                    kb = kbs[u]
                    j = kb - qs // 128
                    for r in range(2):
                        po = 64 * r
                        prs = [(Kt[po:po + 64, kb * 128:(kb + 1) * 128], Qt[po:po + 64, qc:qc + n])]
                        if j >= 0:
                            if n == 512:
                                prs.append((ident[:], dmask[:, j, :]))
                            else:
                                prs.append((ident[:], dmask[:, 0, 126:128]))
                        pe_group(Z[:, r, 0:n], prs)

                def s1(u):
                    kb = kbs[u]
                    b = u % NBUF
                    act(None, Eb[b][:, :, 0:n], Z[:, :, 0:n], AF.Exp, scale=0.125, bias=kbias[:, kb:kb + 1])
                    act(None, Lb[b][:, :, 0:n], Eb[b][:, :, 0:n], AF.Ln, bias=1.0)

                def s2(u):
                    b = u % NBUF
                    for r in range(2):
                        pe_group(PQ[:, 0, r, 0:n], [(tri[:], Lb[b][:, r, 0:n])])
                        pe_group(PQ[:, 1, r, 0:n], [(ones[:], Lb[b][:, r, 0:n])])
                    tt("dve", Sb[b][:, :, 0:n], PQ[:, 0, :, 0:n], Racc[:, :, 0:n], ALU.add)
                    tt("dve", Racc[:, :, 0:n], PQ[:, 1, :, 0:n], Racc[:, :, 0:n], ALU.add)

                def s345(u):
                    kb = kbs[u]
                    b = u % NBUF
                    act(None, Gb[b][:, :, 0:n], Sb[b][:, :, 0:n], AF.Exp, scale=-1.0)
                    tt("pool", Wb[b][:, :, 0:n], Eb[b][:, :, 0:n], Gb[b][:, :, 0:n], ALU.mult)
                    fns = []
                    for r in range(2):
                        fns.append(mm(Yp[:, 0:n], Vv[:, kb, 128 * r:128 * (r + 1)], Wb[b][:, r, 0:n],
                                      (u == 0 and r == 0), (u == U - 1 and r == 1)))
                    S.op("pe", fns, reads=[Vv.buf, Wb[b].buf], writes=[Yp.buf])

                s0(0)
                for i in range(U + 1):
                    if i < U:
                        s1(i)
                    if i + 1 < U:
                        s0(i + 1)
                    if i < U:
                        s2(i)
                    if i >= 1:
                        s345(i - 1)
                cp("dve", ye[:, 0:n], Yp[:, 0:n])
                store(YSB[:, hp, ycol:ycol + n], ye[:, 0:n])
        S.barrier()
        P.close()

    def load_w_bf16(P, W, src, nk, ncol, gvec, stg, piece):
        i = 0
        for k in range(nk):
            for c0 in range(0, ncol, piece):
                c1 = min(ncol, c0 + piece)
                st = stg[i % 2]
                eng = "dve" if i % 2 == 0 else "pool"
                i += 1
                load(st[:, 0:c1 - c0], src[k * 128:(k + 1) * 128, c0:c1])
                if gvec is None:
                    cp(eng, W[:, k, c0:c1], st[:, 0:c1 - c0])
                else:
                    ts(eng, W[:, k, c0:c1], st[:, 0:c1 - c0], gvec[:, k:k + 1], ALU.mult)

    def phase_C1():
        P = Pool_()
        WG = P.sb("WG", [128, 8, 2048], BF16)
        WSB = P.sb("WSB", [128, 4, 1024], BF16)
        WSW = P.sb("WSW", [128, 4, 1024], BF16)
        WO = P.sb("WO", [128, 8, 1024], BF16)
        stg = [P.sb(f"stgC1{i}", [128, 1024], F32) for i in range(2)]
        gm = P.sb("gm_c1", [128, 8], F32)
        xs = [P.sb(f"xsC{i}", [128, 8, 512], F32) for i in range(1)]
        sq = P.sb("sqC", [128, 8, 512], BF16)
        hT = P.sb("hTC", [128, 8, 512], BF16)
        lnv = P.sb("lnvC", [128, 512], F32)
        rs = P.sb("rsC", [128, 512], F32)
        ysb = [P.sb(f"ysbC{i}", [128, 4, 512], BF16) for i in range(2)]
        ysw = [P.sb(f"yswC{i}", [128, 4, 512], BF16) for i in range(2)]
        gate = P.sb("gate", [128, 16, 512], F32)
        t1 = [P.sb(f"t1C{i}", [128, 512], F32) for i in range(2)]
        t2 = [P.sb(f"t2C{i}", [128, 512], F32) for i in range(2)]
        mT = P.sb("mT", [128, 8, 512], BF16)
        x1 = [P.sb(f"x1C{i}", [128, 8, 512], F32) for i in range(1)]
        ssps = P.ps("ssC", [128, 512])
        PS = [P.ps(f"psC{i}", [128, 512]) for i in range(6)]

        load(gm[:], gm_d[:, :])
        load_w_bf16(P, WG, w_in[:, 2304:4352], 8, 2048, gm, stg, 1024)
        load_w_bf16(P, WSB, w_sbp, 4, 1024, None, stg, 1024)
        load_w_bf16(P, WSW, w_swp, 4, 1024, None, stg, 1024)
        load_w_bf16(P, WO, w_out, 8, 1024, None, stg, 1024)
        psi = [0]

        def nextps():
            psi[0] = (psi[0] + 1) % 6
            return PS[psi[0]]

        for gi_, (qs, n, oc) in enumerate(groups):
            s = gi_ % 2
            x = xs[0]
            load(x[:, :, 0:n], xT[:, qs:qs + n].rearrange("(k p) n -> p k n", p=128))
            load(ysb[s][:, :, 0:n], YSB[:, :, oc:oc + n])
            swc = qs - (H - 128)
            load(ysw[s][:, :, 0:n], YSW[:, :, swc:swc + n])
            rms_stats(P, x, n, sq, ssps, lnv, rs)
            for k in range(8):
                tt("dve" if k % 2 == 0 else "pool", hT[:, k, 0:n], x[:, k, 0:n], rs[:, 0:n], ALU.mult)
            for o in range(16):
                ps = nextps()
                pe_group(ps[:, 0:n], [(WG[:, k, o * 128:(o + 1) * 128], hT[:, k, 0:n]) for k in range(8)])
                act(None, gate[:, o, 0:n], ps[:, 0:n], AF.Sigmoid)
            for o in range(8):
                pa = nextps()
                pe_group(pa[:, 0:n], [(WSB[:, k, o * 128:(o + 1) * 128], ysb[s][:, k, 0:n]) for k in range(4)])
                pb = nextps()
                pe_group(pb[:, 0:n], [(WSW[:, k, o * 128:(o + 1) * 128], ysw[s][:, k, 0:n]) for k in range(4)])
                tt("dve", t1[o % 2][:, 0:n], pa[:, 0:n], gate[:, o, 0:n], ALU.mult)
                tt("dve", t2[o % 2][:, 0:n], pb[:, 0:n], gate[:, 8 + o, 0:n], ALU.mult)
                tt("pool", mT[:, o, 0:n], t1[o % 2][:, 0:n], t2[o % 2][:, 0:n], ALU.add)
            for o in range(8):
                ps = nextps()
                pe_group(ps[:, 0:n], [(WO[:, k, o * 128:(o + 1) * 128], mT[:, k, 0:n]) for k in range(8)])
                tt("dve", x1[0][:, o, 0:n], ps[:, 0:n], x[:, o, 0:n], ALU.add)
            store(X1s[:, :, oc:oc + n], x1[0][:, :, 0:n])
        S.barrier()
        P.close()

    def phase_C2a():
        P = Pool_()
        WU = P.sb("WU", [128, 8, 2 * DFF], BF16)
        stg = [P.sb(f"stgU{i}", [128, 1408], F32) for i in range(2)]
        gf = P.sb("gf_s", [128, 8], F32)
        cw = P.sb("cw_s", [128, 44, 3], F32)
        cb = P.sb("cb_s", [128, 44], F32)
        carry = P.sb("carry", [128, 44, 2], F32)
        xs = [P.sb(f"xsU{i}", [128, 8, 512], F32) for i in range(1)]
        sq = P.sb("sqU", [128, 8, 512], BF16)
        hT = P.sb("hTU", [128, 8, 512], BF16)
        lnv = P.sb("lnvU", [128, 512], F32)
        rs = P.sb("rsU", [128, 512], F32)
        yb = [P.sb(f"ybU{i}", [128, 512], F32) for i in range(4)]
        sg = [P.sb(f"sgU{i}", [128, 512], F32) for i in range(2)]
        aT = [P.sb(f"aTU{i}", [128, 22, 512], BF16) for i in range(2)]
        ssps = P.ps("ssU", [128, 512])
        PS = [P.ps(f"psU{i}", [128, 512]) for i in range(6)]

        load(gf[:], gf_d[:, :])
        load(cw[:], cw_d.rearrange("p (j k) -> p j k", k=3))
        load(cb[:], cb_d[:, :])
        memset("dve", carry[:], 0.0)
        load_w_bf16(P, WU, w_up, 8, 2 * DFF, gf, stg, 1408)
        psi = [0]

        def nextps():
            psi[0] = (psi[0] + 1) % 6
            return PS[psi[0]]

        ybi = [0]
        for gi_, (qs, n, oc) in enumerate(groups):
            s = gi_ % 2
            x = xs[0]
            load(x[:, :, 0:n], X1s[:, :, oc:oc + n])
            rms_stats(P, x, n, sq, ssps, lnv, rs)
            for k in range(8):
                tt("dve" if k % 2 == 0 else "pool", hT[:, k, 0:n], x[:, k, 0:n], rs[:, 0:n], ALU.mult)
            for j in range(22):
                ys = []
                for t in (j, 22 + j):
                    ps = nextps()
                    pe_group(ps[:, 0:n], [(WU[:, k, t * 128:(t + 1) * 128], hT[:, k, 0:n]) for k in range(8)])
                    y = yb[ybi[0] % 4]
                    ybi[0] += 1
                    ys.append(y)
                    act(None, y[:, 0:n], ps[:, 0:n], AF.Identity, scale=cw[:, t, 2:3], bias=cb[:, t:t + 1])
                    stt(y[:, 1:n], ps[:, 0:n - 1], cw[:, t, 1:2], y[:, 1:n], ALU.mult, ALU.add)
                    if n > 2:
                        stt(y[:, 2:n], ps[:, 0:n - 2], cw[:, t, 0:1], y[:, 2:n], ALU.mult, ALU.add)
                    stt(y[:, 0:1], carry[:, t, 1:2], cw[:, t, 1:2], y[:, 0:1], ALU.mult, ALU.add)
                    stt(y[:, 0:2], carry[:, t, 0:2], cw[:, t, 0:1], y[:, 0:2], ALU.mult, ALU.add)
                    cp("dve", carry[:, t, 0:2], ps[:, n - 2:n])
                sgt = sg[j % 2]
                act(None, sgt[:, 0:n], ys[0][:, 0:n], AF.Silu)
                tt("pool", aT[s][:, j, 0:n], sgt[:, 0:n], ys[1][:, 0:n], ALU.mult)
            store(ATs[:, :, oc:oc + n], aT[s][:, :, 0:n])
        S.barrier()
        P.close()

    def phase_C2b():
        P = Pool_()
        WD = P.sb("WD", [128, 22, 1024], BF16)
        stg = [P.sb(f"stgD{i}", [128, 1024], F32) for i in range(2)]
        glf = P.sb("gl_s", [128, 8], F32)
        xs = [P.sb(f"xsD{i}", [128, 8, 512], F32) for i in range(2)]
        aT = [P.sb(f"aTD{i}", [128, 22, 512], BF16) for i in range(2)]
        x2 = P.sb("x2D", [128, 8, 512], F32)
        sq = P.sb("sqD", [128, 8, 512], BF16)
        lnv = P.sb("lnvD", [128, 512], F32)
        rs = P.sb("rsD", [128, 512], F32)
        ob = [P.sb(f"obD{i}", [128, 8, 512], F32) for i in range(2)]
        ssps = P.ps("ssD", [128, 512])
        PS = [P.ps(f"psD{i}", [128, 512]) for i in range(6)]

        load(glf[:], gl_d[:, :])
        load_w_bf16(P, WD, w_down, 22, 1024, None, stg, 1024)
        psi = [0]

        def nextps():
            psi[0] = (psi[0] + 1) % 6
            return PS[psi[0]]

        for gi_, (qs, n, oc) in enumerate(groups[1:]):
            s = gi_ % 2
            x = xs[s]
            load(x[:], X1s[:, :, oc:oc + n])
            load(aT[s][:], ATs[:, :, oc:oc + n])
            for o in range(8):
                ps = nextps()
                pe_group(ps[:], [(WD[:, k, o * 128:(o + 1) * 128], aT[s][:, k, :]) for k in range(22)])
                tt("dve", x2[:, o, :], ps[:], x[:, o, :], ALU.add)
            rms_stats(P, x2, 512, sq, ssps, lnv, rs)
            for o in range(8):
                stt(ob[s][:, o, :], x2[:, o, :], glf[:, o:o + 1], rs[:], ALU.mult, ALU.mult)
            store(out_d[:, :, oc - 2:oc - 2 + n].rearrange("k p n -> p k n"), ob[s][:])
        S.barrier()
        P.close()

    S.barrier()
    gtmp.close()
    ph = phases.split(",")
    if "A" in ph:
        phase_A()
    if "B1" in ph:
        phase_B1()
    if "B2" in ph:
        phase_B2()
    if "C1" in ph:
        phase_C1()
    if "C2a" in ph:
        phase_C2a()
    if "C2b" in ph:
        phase_C2b()
    gl.close()

    with ExitStack() as es:
        sems = {}
        for k in S.keys:
            nm = k if isinstance(k, str) else f"d{k[1]}"
            sems[k] = es.enter_context(nc.semaphore("s_" + nm))
        block = es.enter_context(nc.Block())

        @block.tensor
        def _(e):
            S.replay("pe", e, sems)

        @block.scalar
        def _(e):
            S.replay("act", e, sems)

        @block.vector
        def _(e):
            S.replay("dve", e, sems)

        @block.gpsimd
        def _(e):
            S.replay("pool", e, sems)

        @block.sync
        def _(e):
            S.replay("sp", e, sems)

    return nc


def _t5_bucket(dist):
    dist = np.asarray(dist, np.int32)
    max_exact = 16
    d = np.maximum(dist, 1).astype(np.float32)
    large = max_exact + (np.log(d / np.float32(max_exact)) / np.float32(np.log(128 / max_exact))
                         * np.float32(32 - max_exact)).astype(np.int32)
    large = np.minimum(large, 31)
    return np.where(dist < max_exact, dist, large)


def _consts():
    c = np.zeros((128, 2688), np.float32)
    c[:, 0:128] = 1.0
    j = np.arange(128)[:, None]
    s = np.arange(128)[None, :]
    c[:, 128:256] = (j >= s).astype(np.float32)
    c[:, 256:384] = np.eye(128, dtype=np.float32)
    q = np.arange(512)[None, :]
    for jj in range(4):
        valid = (128 * jj + j) < q
        c[:, 384 + 512 * jj:384 + 512 * (jj + 1)] = np.where(valid, 0.0, 8.0 * NEGM)
    c[:, 2432:2432 + 64] = 1.0
    c[:, 2432 + 128 + 64:2432 + 256] = 1.0
    return c


def prep_inputs(NH, x, g_mix, w_in, w_sb_proj, w_sw_proj, w_out, rel_bias, sinks,
                g_ffn, w_up, conv_w, conv_b, w_down, g_final):
    H = NH * 512
    B = x.shape[0]
    f = lambda a: np.ascontiguousarray(np.asarray(a, dtype=np.float32))
    x = f(x)
    lay8 = lambda g: f(np.asarray(g, np.float32).reshape(8, 128).T)
    rel_bias = np.asarray(rel_bias, np.float32)
    k = np.arange(128)[:, None]
    q = np.arange(128)[None, :]
    bm = np.zeros((128, 2, 2, 4, 128), np.float32)
    d0 = 128 + q - k
    d1 = q - k
    b0 = _t5_bucket(np.clip(d0, 0, 255))
    b1 = _t5_bucket(np.clip(d1, 0, 255))
    for h in range(8):
        bm[:, 0, h % 2, h // 2, :] = np.where(d0 <= 127, rel_bias[b0, h], NEGM)
        bm[:, 1, h % 2, h // 2, :] = np.where(d1 >= 0, rel_bias[b1, h], NEGM)
    sinks = np.asarray(sinks, np.float32)
    sk = np.zeros((128, 4), np.float32)
    for hp in range(4):
        sk[0:64, hp] = sinks[2 * hp]
        sk[64:128, hp] = sinks[2 * hp + 1]
    cw = np.asarray(conv_w, np.float32)
    cwl = np.ascontiguousarray(cw.reshape(3, 44, 128).transpose(2, 1, 0)).reshape(128, 132)
    cbl = f(np.asarray(conv_b, np.float32).reshape(44, 128).T)
    shared = {
        "w_in": f(w_in), "w_sbp": f(w_sb_proj), "w_swp": f(w_sw_proj), "w_out": f(w_out),
        "w_up": f(w_up), "w_down": f(w_down), "gm": lay8(g_mix), "gf": lay8(g_ffn), "gl": lay8(g_final),
        "cw": f(cwl), "cb": cbl, "bm": f(bm.reshape(128, 2048)), "sk": sk, "cst": _consts(),
    }
    in_maps = []
    for c in range(2 * B):
        b, p = c // 2, c % 2
        xl = np.zeros((2 * H, D), np.float32)
        kb = np.zeros((128, 2 * H // 128), np.float32)
        if p == 1:
            xl[:] = x[b]
        else:
            xl[H:] = x[b, :H]
            kb[:, :H // 128] = NEGM
        m = dict(shared)
        m["xT"] = np.ascontiguousarray(xl.T)
        m["kbias"] = kb
        in_maps.append(m)
    return in_maps


_NC_CACHE = {}


def run(NH, phases="A,B1,B2,C1,C2a,C2b", **inputs):
    x = np.asarray(inputs["x"])
    B = x.shape[0]
    H = NH * 512
    in_maps = prep_inputs(NH, **inputs)
    if (NH, phases) not in _NC_CACHE:
        _NC_CACHE[(NH, phases)] = build_nc(NH, phases)
    nc = _NC_CACHE[(NH, phases)]
    res = run_bass_kernel_spmd(nc, in_maps, core_ids=list(range(2 * B)))
    out = np.zeros((B, 2 * H, D), np.float32)
    for c in range(2 * B):
        b, p = c // 2, c % 2
        o = np.asarray(res.results[c]["out"]).reshape(D, H)
        out[b, p * H:(p + 1) * H, :] = o.T
    return out


def kernel(**inputs):
    return run(8, **inputs)
```

```python
import numpy as np
from contextlib import ExitStack
import concourse.bass as bass
import concourse.mybir as mybir
from concourse.bass_utils import run_bass_kernel_spmd

F32 = mybir.dt.float32
BF16 = mybir.dt.bfloat16
AF = mybir.ActivationFunctionType
ALU = mybir.AluOpType

D = 1024
KT = 8
DFF = 2816
NEGM = -100.0
ENGS = ("pe", "act", "dve", "pool", "sp")
NDMA = 24
NDMA_HW = 16


class Buf:
    __slots__ = ("w", "r", "name")

    def __init__(self, name=""):
        self.w = None
        self.r = {}
        self.name = name


class V:
    __slots__ = ("buf", "ap")

    def __init__(self, buf, ap):
        self.buf = buf
        self.ap = ap


class Tile:
    def __init__(self, handle, name=""):
        self.t = handle
        self.buf = Buf(name)

    def __getitem__(self, idx):
        return V(self.buf, self.t[idx])


class Sched:
    def __init__(self):
        self.keys = list(ENGS) + [("d", i) for i in range(NDMA)]
        self.cnt = {k: 0 for k in self.keys}
        self.seen = {e: {k: 0 for k in self.keys} for e in ENGS}
        self.q = {e: [] for e in ENGS}
        self.dma_rr = 0
        self.dma_rr2 = 0
        self.nops = 0

    def op(self, eng, fns, reads=(), writes=(), dma=False):
        if callable(fns):
            fns = [fns]
        need = {}

        def req(tok):
            if tok is None:
                return
            k, v = tok
            if need.get(k, 0) < v:
                need[k] = v

        for b in reads:
            req(b.w)
        for b in writes:
            req(b.w)
            for k, v in b.r.items():
                req((k, v))
        if dma:
            if eng == "sp":
                key = ("d", self.dma_rr)
                self.dma_rr = (self.dma_rr + 1) % NDMA_HW
            else:
                key = ("d", NDMA_HW + self.dma_rr2)
                self.dma_rr2 = (self.dma_rr2 + 1) % (NDMA - NDMA_HW)
            req((key, self.cnt[key]))
            inc = 16
        else:
            key = eng
            inc = 1
        waits = []
        seen = self.seen[eng]
        for k, v in need.items():
            if v <= 0 or seen[k] >= v:
                continue
            if k == eng and eng == "pe":
                continue
            seen[k] = v
            waits.append((k, v))
        self.cnt[key] += inc
        tok = (key, self.cnt[key])
        for b in writes:
            b.w = tok
            b.r = {}
        for b in reads:
            if b.r.get(key, 0) < tok[1]:
                b.r[key] = tok[1]
        self.q[eng].append((waits, fns, key, inc))
        self.nops += 1
        return tok

    def barrier(self):
        for e in ENGS:
            waits = []
            for k in self.keys:
                v = self.cnt[k]
                if v > 0 and self.seen[e][k] < v and k != e:
                    self.seen[e][k] = v
                    waits.append((k, v))
            if waits:
                self.q[e].append((waits, [], None, 0))

    def replay(self, eng, e, sems):
        for waits, fns, key, inc in self.q[eng]:
            for k, v in waits:
                e.wait_ge(sems[k], v)
            ins = None
            for f in fns:
                ins = f(e)
            if ins is not None and key is not None:
                ins.then_inc(sems[key], inc)


def build_nc(NH, phases="A,B1,B2,C1,C2a,C2b"):
    H = NH * 512
    L2 = 2 * H
    NB2 = L2 // 128
    NBH = H // 128
    NOWN = (NH + 1) * 512
    NY = 2 + NH * 512
    NSWB = NH * 4 + 1

    nc = bass.Bass("TRN2", target_bir_lowering=False)
    S = Sched()

    def din(name, shape):
        return nc.dram_tensor(name, shape, F32, kind="ExternalInput").ap()

    xT = din("xT", [D, L2])
    kbias_d = din("kbias", [128, NB2])
    w_in = din("w_in", [D, 4352])
    w_sbp = din("w_sbp", [512, D])
    w_swp = din("w_swp", [512, D])
    w_out = din("w_out", [D, D])
    w_up = din("w_up", [D, 2 * DFF])
    w_down = din("w_down", [DFF, D])
    gm_d = din("gm", [128, 8])
    gf_d = din("gf", [128, 8])
    gl_d = din("gl", [128, 8])
    cw_d = din("cw", [128, 44 * 3])
    cb_d = din("cb", [128, 44])
    bm_d = din("bm", [128, 2048])
    sk_d = din("sk", [128, 4])
    cst_d = din("cst", [128, 2688])
    out_d = nc.dram_tensor("out", [8, 128, H], F32, kind="ExternalOutput").ap()

    def dscr(name, shape, dt):
        return nc.dram_tensor(name, shape, dt, kind="Internal").ap()

    KTs = dscr("KTs", [4, 128, L2], BF16)
    QTs = dscr("QTs", [4, 128, NOWN], BF16)
    VSs = dscr("VSs", [4, 128, NB2, 256], BF16)
    QWs = dscr("QWs", [4, 128, NOWN], BF16)
    KWs = dscr("KWs", [2, 128, NOWN], BF16)
    VWs = dscr("VWs", [128, (NH + 1) * 4, 512], BF16)
    YSB = dscr("YSB", [128, 4, NY], BF16)
    YSW = dscr("YSW", [128, 4, NSWB * 128], BF16)
    X1s = dscr("X1s", [128, 8, NY], F32)
    ATs = dscr("ATs", [128, 22, NY], BF16)

    groups = [(H - 2, 2, 0)] + [(H + 512 * c, 512, 2 + 512 * c) for c in range(NH)]

    def mm(out, lhsT, rhs, start, stop):
        return lambda e: e.matmul(out.ap, lhsT.ap, rhs.ap, start=start, stop=stop)

    def pe_group(out, pairs, extra_reads=()):
        fns = []
        n = len(pairs)
        rd = []
        for i, (l, r) in enumerate(pairs):
            fns.append(mm(out, l, r, i == 0, i == n - 1))
            rd += [l.buf, r.buf]
        S.op("pe", fns, reads=rd + list(extra_reads), writes=[out.buf])

    def act(eng_unused, out, in_, func, scale=1.0, bias=0.0, extra_reads=()):
        def f(e):
            kw = {}
            if isinstance(bias, V):
                kw["bias"] = bias.ap
            else:
                kw["bias"] = float(bias)
            if isinstance(scale, V):
                kw["scale"] = scale.ap
            else:
                kw["scale"] = float(scale)
            return e.activation(out.ap, in_.ap, func, **kw)

        rd = [in_.buf] + [x.buf for x in (scale, bias) if isinstance(x, V)] + list(extra_reads)
        S.op("act", f, reads=rd, writes=[out.buf])

    def tt(eng, out, a, b, op):
        S.op(eng, lambda e: e.tensor_tensor(out.ap, a.ap, b.ap, op), reads=[a.buf, b.buf], writes=[out.buf])

    def ts(eng, out, a, s1, op0, s2=None, op1=None):
        def f(e):
            sc1 = s1.ap if isinstance(s1, V) else float(s1)
            if s2 is None:
                return e.tensor_scalar(out.ap, a.ap, sc1, None, op0)
            sc2 = s2.ap if isinstance(s2, V) else float(s2)
            return e.tensor_scalar(out.ap, a.ap, sc1, sc2, op0, op1)

        rd = [a.buf] + [x.buf for x in (s1, s2) if isinstance(x, V)]
        S.op(eng, f, reads=rd, writes=[out.buf])

    def stt(out, a, sc, b, op0, op1):
        def f(e):
            s = sc.ap if isinstance(sc, V) else float(sc)
            return e.scalar_tensor_tensor(out.ap, a.ap, s, b.ap, op0, op1)

        rd = [a.buf, b.buf] + ([sc.buf] if isinstance(sc, V) else [])
        S.op("dve", f, reads=rd, writes=[out.buf])

    def cp(eng, out, a):
        if eng == "act":
            act(None, out, a, AF.Copy)
        else:
            S.op(eng, lambda e: e.tensor_copy(out.ap, a.ap), reads=[a.buf], writes=[out.buf])

    def memset(eng, out, val):
        S.op(eng, lambda e: e.memset(out.ap, val), writes=[out.buf])

    def load(out, src_ap, eng="sp"):
        S.op(eng, lambda e: e.dma_start(out=out.ap, in_=src_ap), writes=[out.buf], dma=True)

    def store(dst_ap, src, eng="pool"):
        S.op(eng, lambda e: e.dma_start(out=dst_ap, in_=src.ap), reads=[src.buf], dma=True)

    class Pool_:
        def __init__(self):
            self.es = ExitStack()

        def sb(self, name, shape, dt):
            return Tile(self.es.enter_context(nc.sbuf_tensor(name, shape, dt)), name)

        def ps(self, name, shape):
            return Tile(self.es.enter_context(nc.psum_tensor(name, shape, F32)), name)

        def close(self):
            self.es.close()

    gl = Pool_()
    ones = gl.sb("ones", [128, 128], BF16)
    tri = gl.sb("tri", [128, 128], BF16)
    ident = gl.sb("ident", [128, 128], BF16)
    dmask = gl.sb("dmask", [128, 4, 512], BF16)
    opad = gl.sb("opad", [128, 2, 128], BF16)
    kbias = gl.sb("kbias_s", [128, NB2], F32)
    gtmp = Pool_()
    cstf = gtmp.sb("cstf", [128, 2688], F32)

    load(cstf[:], cst_d[:, :])
    load(kbias[:], kbias_d[:, :])
    cp("dve", ones[:], cstf[:, 0:128])
    cp("dve", tri[:], cstf[:, 128:256])
    cp("dve", ident[:], cstf[:, 256:384])
    for j in range(4):
        cp("dve", dmask[:, j, :], cstf[:, 384 + 512 * j:384 + 512 * (j + 1)])
    for j in range(2):
        cp("dve", opad[:, j, :], cstf[:, 2432 + 128 * j:2432 + 128 * (j + 1)])

    def rms_stats(P, xs, n, sq, ssps, lnv, rs):
        act(None, sq[:, :, 0:n], xs[:, :, 0:n], AF.Square)
        pe_group(ssps[:, 0:n], [(ones[:], sq[:, k, 0:n]) for k in range(8)])
        act(None, lnv[:, 0:n], ssps[:, 0:n], AF.Ln, scale=1.0 / D, bias=1e-6)
        act(None, rs[:, 0:n], lnv[:, 0:n], AF.Exp, scale=-0.5)

    def phase_A():
        P = Pool_()
        WA = P.sb("WA", [128, 8, 3328], BF16)
        stg = [P.sb(f"wstg{i}", [128, 2304], F32) for i in range(2)]
        gm = P.sb("gm_s", [128, 8], F32)
        xs = [P.sb(f"xsA{i}", [128, 8, 512], F32) for i in range(2)]
        sq = P.sb("sqA", [128, 8, 512], BF16)
        hT = [P.sb(f"hTA{i}", [128, 8, 512], BF16) for i in range(2)]
        lnv = P.sb("lnvA", [128, 512], F32)
        rs = [P.sb(f"rsA{i}", [128, 512], F32) for i in range(2)]
        stF = [P.sb(f"stF{i}", [128, 14, 512], BF16) for i in range(2)]
        stV = [P.sb(f"stV{i}", [128, 4, 1536], BF16) for i in range(2)]
        ssps = P.ps("ssA", [128, 512])
        PS = [P.ps(f"psA{i}", [128, 512]) for i in range(6)]

        load(gm[:], gm_d[:, :])
        memset("pool", WA[:, :, 1792:3328], 0.0)
        for kt in range(8):
            st = stg[kt % 2]
            load(st[:], w_in[kt * 128:(kt + 1) * 128, 0:2304])
            g = gm[:, kt:kt + 1]
            e1 = "dve"
            act(None, WA[:, kt, 0:1024], st[:, 0:1024], AF.Identity, scale=g)
            ts(e1, WA[:, kt, 1024:1536], st[:, 1536:2048], g, ALU.mult)
            for gg in range(2):
                for r in range(2):
                    ts(e1, WA[:, kt, 1536 + 128 * gg + 64 * r:1536 + 128 * gg + 64 * (r + 1)],
                       st[:, 2048 + 64 * gg:2048 + 64 * (gg + 1)], g, ALU.mult)
            for h in range(8):
                o = 1792 + 128 * h + 64 * (h % 2)
                ts(e1, WA[:, kt, o:o + 64], st[:, 1024 + 64 * h:1024 + 64 * (h + 1)], g, ALU.mult)
            for i in range(4):
                o = 2816 + 128 * i + 64 * (i % 2)
                gg = i // 2
                ts(e1, WA[:, kt, o:o + 64], st[:, 2176 + 64 * gg:2176 + 64 * (gg + 1)], g, ALU.mult)

        psi = [0]

        def nextps():
            psi[0] = (psi[0] + 1) % 6
            return PS[psi[0]]

        evi = [0]

        def evac(out, in_):
            evi[0] += 1
            cp("act" if evi[0] % 2 == 0 else "dve", out, in_)

        def stage1(tc):
            s = tc % 2
            load(xs[s][:], xT[:, tc * 512:(tc + 1) * 512].rearrange("(k p) n -> p k n", p=128))
            rms_stats(P, xs[s], 512, sq, ssps, lnv, rs[s])
            for k in range(8):
                tt("dve" if k % 2 == 0 else "pool", hT[s][:, k, :], xs[s][:, k, :], rs[s][:], ALU.mult)

        def stage2(tc):
            s = tc % 2
            own = tc >= NH - 1
            h = hT[s]
            sf = stF[s]
            sv = stV[s]

            def fm(slot, col0):
                ps = nextps()
                pe_group(ps[:], [(WA[:, k, col0:col0 + 128], h[:, k, :]) for k in range(8)])
                evac(sf[:, slot, :], ps[:])

            for hp in range(4):
                fm(hp, 512 + 128 * hp)
            if own:
                for hp in range(4):
                    fm(4 + hp, 128 * hp)
                for hp in range(4):
                    fm(8 + hp, 1024 + 128 * hp)
                for gg in range(2):
                    fm(12 + gg, 1536 + 128 * gg)
            for blk in range(4):
                for half in range(2):
                    ps = nextps()
                    pe_group(ps[:], [(h[:, k, blk * 128:(blk + 1) * 128],
                                      WA[:, k, 1792 + 512 * half:1792 + 512 * (half + 1)]) for k in range(8)])
                    evac(sv[:, blk, 512 * half:512 * (half + 1)], ps[:])
                if own:
                    ps = nextps()
                    pe_group(ps[:], [(h[:, k, blk * 128:(blk + 1) * 128], WA[:, k, 2816:3328]) for k in range(8)])
                    evac(sv[:, blk, 1024:1536], ps[:])
            c0 = tc * 512
            store(KTs[:, :, c0:c0 + 512].rearrange("h p n -> p h n"), sf[:, 0:4, :])
            for hp in range(4):
                store(VSs[hp, :, 4 * tc:4 * tc + 4, :], sv[:, :, 256 * hp:256 * (hp + 1)])
            if own:
                o0 = (tc - (NH - 1)) * 512
                store(QTs[:, :, o0:o0 + 512].rearrange("h p n -> p h n"), sf[:, 4:8, :])
                store(QWs[:, :, o0:o0 + 512].rearrange("h p n -> p h n"), sf[:, 8:12, :])
                store(KWs[:, :, o0:o0 + 512].rearrange("h p n -> p h n"), sf[:, 12:14, :])
                b0 = (tc - (NH - 1)) * 4
                store(VWs[:, b0:b0 + 4, :], sv[:, :, 1024:1536])

        NT = 2 * NH
        stage1(0)
        for tc in range(NT):
            if tc + 1 < NT:
                stage1(tc + 1)
            stage2(tc)
        S.barrier()
        P.close()

    def phase_B1():
        P = Pool_()
        QW = P.sb("QW", [128, 4, NOWN], BF16)
        KW = P.sb("KW", [128, 2, NOWN], BF16)
        VW = P.sb("VW", [128, (NH + 1) * 4, 512], BF16)
        BM = P.sb("BM", [128, 2048], F32)
        sk = P.sb("sk_s", [128, 4], F32)
        esk = P.sb("esk", [128, 4], F32)
        ESB = P.sb("ESB", [128, 4, 128], F32)
        zer = P.sb("zerB1", [128, 128], F32)
        LG = [P.sb(f"LG{i}", [128, 2048], F32) for i in range(2)]
        PB = [P.sb(f"PB{i}", [128, 2, 2, 4, 128], BF16) for i in range(2)]
        den = P.sb("den", [128, 512], F32)
        rec = P.sb("rec", [128, 512], F32)
        yst = [P.sb(f"yst{i}", [128, 4, 128], BF16) for i in range(2)]
        ZS = P.ps("ZS", [128, 2, 2, 4, 128])
        OP = P.ps("OPs", [128, 4, 128])
        DN = P.ps("DNs", [128, 4, 128])

        load(QW[:], QWs.rearrange("h p n -> p h n"))
        load(KW[:], KWs.rearrange("h p n -> p h n"))
        load(VW[:], VWs[:, :, :])
        load(BM[:], bm_d[:, :])
        load(sk[:], sk_d[:, :])
        act(None, esk[:], sk[:], AF.Exp)
        memset("dve", zer[:], 0.0)
        for hp in range(4):
            ts("dve", ESB[:, hp, :], zer[:], esk[:, hp:hp + 1], ALU.add)

        for qi, i in enumerate(range(3, (NH + 1) * 4)):
            s = qi % 2
            qcols = slice(i * 128, (i + 1) * 128)
            for kbsel in range(2):
                ib = i - 1 + kbsel
                for h in range(8):
                    po = (h % 2) * 64
                    g = h // 4
                    hp = h // 2
                    S.op("pe", mm(ZS[:, kbsel, h % 2, hp, :], KW[po:po + 64, g, ib * 128:(ib + 1) * 128],
                                  QW[po:po + 64, hp, qcols], True, True),
                         reads=[KW.buf, QW.buf], writes=[ZS.buf])
            stt(LG[s][:], V(ZS.buf, ZS.t[:].rearrange("p a r h q -> p (a r h q)")), 0.125, BM[:], ALU.mult, ALU.add)
            for kbsel in range(2):
                lb = (NBH - 4) + i - 1 + kbsel
                act(None, V(PB[s].buf, PB[s].t[:, kbsel, :, :, :].rearrange("p r h q -> p (r h q)")),
                    LG[s][:, kbsel * 1024:(kbsel + 1) * 1024], AF.Exp, bias=kbias[:, lb:lb + 1])
            for hp in range(4):
                g = hp // 2
                prs = []
                prd = []
                for kbsel in range(2):
                    ib = i - 1 + kbsel
                    for r in range(2):
                        h = 2 * hp + r
                        var = 2 * g + r
                        prs.append((VW[:, ib, 128 * var:128 * (var + 1)], PB[s][:, kbsel, r, hp, :]))
                        prd.append((opad[:, r, :], PB[s][:, kbsel, r, hp, :]))
                pe_group(OP[:, hp, :], prs)
                pe_group(DN[:, hp, :], prd)
            tt("dve", den[:], V(DN.buf, DN.t[:].rearrange("p a q -> p (a q)")),
               V(ESB.buf, ESB.t[:].rearrange("p a q -> p (a q)")), ALU.add)
            S.op("dve", lambda e: e.reciprocal(rec.t[:], den.t[:]), reads=[den.buf], writes=[rec.buf])
            tt("dve", V(yst[s].buf, yst[s].t[:].rearrange("p a q -> p (a q)")),
               V(OP.buf, OP.t[:].rearrange("p a q -> p (a q)")), rec[:], ALU.mult)
            store(YSW[:, :, qi * 128:(qi + 1) * 128], yst[s][:])
        S.barrier()
        P.close()

    def phase_B2():
        P = Pool_()
        KTt = [P.sb(f"KTt{i}", [128, L2], BF16) for i in range(2)]
        Vt = [P.sb(f"Vt{i}", [128, NB2, 256], BF16) for i in range(2)]
        QTt = [P.sb(f"QTt{i}", [128, NOWN], BF16) for i in range(2)]
        NBUF = 4
        Eb = [P.sb(f"Eb{i}", [128, 2, 512], BF16) for i in range(NBUF)]
        Lb = [P.sb(f"Lb{i}", [128, 2, 512], BF16) for i in range(NBUF)]
        Gb = [P.sb(f"Gb{i}", [128, 2, 512], BF16) for i in range(NBUF)]
        Wb = [P.sb(f"Wb{i}", [128, 2, 512], BF16) for i in range(NBUF)]
        Sb = [P.sb(f"Sb{i}", [128, 2, 512], F32) for i in range(NBUF)]
        Racc = P.sb("Racc", [128, 2, 512], F32)
        yev = [P.sb(f"yev{i}", [128, 512], BF16) for i in range(2)]
        Z = P.ps("Zp", [128, 2, 512])
        PQ = P.ps("PQp", [128, 2, 2, 512])
        Y = [P.ps(f"Yp{i}", [128, 512]) for i in range(2)]

        def load_hp(hp):
            s = hp % 2
            hl = L2 // 2
            load(KTt[s][:, 0:hl], KTs[hp, :, 0:hl])
            load(KTt[s][:, hl:L2], KTs[hp, :, hl:L2])
            load(Vt[s][:, 0:NB2 // 2, :], VSs[hp, :, 0:NB2 // 2, :])
            load(Vt[s][:, NB2 // 2:NB2, :], VSs[hp, :, NB2 // 2:NB2, :])
            load(QTt[s][:], QTs[hp, :, :])

        gi = [0]
        load_hp(0)
        for hp in range(4):
            s = hp % 2
            if hp + 1 < 4:
                load_hp(hp + 1)
            Kt, Vv, Qt = KTt[s], Vt[s], QTt[s]
            for (qs, n, ycol) in groups:
                qc = qs - (NH - 1) * 512
                kbmax = (qs + n - 2) // 128
                kbs = list(range(kbmax, -1, -1))
                U = len(kbs)
                Yp = Y[gi[0] % 2]
                ye = yev[gi[0] % 2]
                gi[0] += 1
                memset("dve", Racc[:, :, 0:n], 0.0)

                def s0(u):
                    kb = kbs[u]
                    j = kb - qs // 128
                    for r in range(2):
                        po = 64 * r
                        prs = [(Kt[po:po + 64, kb * 128:(kb + 1) * 128], Qt[po:po + 64, qc:qc + n])]
                        if j >= 0:
                            if n == 512:
                                prs.append((ident[:], dmask[:, j, :]))
                            else:
                                prs.append((ident[:], dmask[:, 0, 126:128]))
                        pe_group(Z[:, r, 0:n], prs)

                def s1(u):
                    kb = kbs[u]
                    b = u % NBUF
                    act(None, Eb[b][:, :, 0:n], Z[:, :, 0:n], AF.Exp, scale=0.125, bias=kbias[:, kb:kb + 1])
                    act(None, Lb[b][:, :, 0:n], Eb[b][:, :, 0:n], AF.Ln, bias=1.0)

                def s2(u):
                    b = u % NBUF
                    for r in range(2):
                        pe_group(PQ[:, 0, r, 0:n], [(tri[:], Lb[b][:, r, 0:n])])
                        pe_group(PQ[:, 1, r, 0:n], [(ones[:], Lb[b][:, r, 0:n])])
                    tt("dve", Sb[b][:, :, 0:n], PQ[:, 0, :, 0:n], Racc[:, :, 0:n], ALU.add)
                    tt("dve", Racc[:, :, 0:n], PQ[:, 1, :, 0:n], Racc[:, :, 0:n], ALU.add)

                def s34(u):
                    b = u % NBUF
                    act(None, Gb[b][:, :, 0:n], Sb[b][:, :, 0:n], AF.Exp, scale=-1.0)
                    tt("pool", Wb[b][:, :, 0:n], Eb[b][:, :, 0:n], Gb[b][:, :, 0:n], ALU.mult)

                def s5(u):
                    kb = kbs[u]
                    b = u % NBUF
                    fns = []
                    for r in range(2):
                        fns.append(mm(Yp[:, 0:n], Vv[:, kb, 128 * r:128 * (r + 1)], Wb[b][:, r, 0:n],
                                      (u == 0 and r == 0), (u == U - 1 and r == 1)))
                    S.op("pe", fns, reads=[Vv.buf, Wb[b].buf], writes=[Yp.buf])

                s0(0)
                for i in range(U + 3):
                    if i < U:
                        s1(i)
                    if i + 1 < U:
                        s0(i + 1)
                    if i < U:
                        s2(i)
                    if 0 <= i - 2 < U:
                        s34(i - 2)
                    if 0 <= i - 3 < U:
                        s5(i - 3)
                cp("dve", ye[:, 0:n], Yp[:, 0:n])
                store(YSB[:, hp, ycol:ycol + n], ye[:, 0:n])
        S.barrier()
        P.close()

    def load_w_bf16(P, W, src, nk, ncol, gvec, stg, piece):
        i = 0
        for k in range(nk):
            for c0 in range(0, ncol, piece):
                c1 = min(ncol, c0 + piece)
                st = stg[i % 2]
                eng = "dve" if i % 2 == 0 else "act"
                i += 1
                load(st[:, 0:c1 - c0], src[k * 128:(k + 1) * 128, c0:c1])
                if gvec is None:
                    cp(eng, W[:, k, c0:c1], st[:, 0:c1 - c0])
                elif eng == "act":
                    act(None, W[:, k, c0:c1], st[:, 0:c1 - c0], AF.Identity, scale=gvec[:, k:k + 1])
                else:
                    ts(eng, W[:, k, c0:c1], st[:, 0:c1 - c0], gvec[:, k:k + 1], ALU.mult)

    def phase_C1():
        P = Pool_()
        WG = P.sb("WG", [128, 8, 2048], BF16)
        WSB = P.sb("WSB", [128, 4, 1024], BF16)
        WSW = P.sb("WSW", [128, 4, 1024], BF16)
        WO = P.sb("WO", [128, 8, 1024], BF16)
        stg = [P.sb(f"stgC1{i}", [128, 1024], F32) for i in range(2)]
        gm = P.sb("gm_c1", [128, 8], F32)
        xs = [P.sb(f"xsC{i}", [128, 8, 512], F32) for i in range(1)]
        sq = P.sb("sqC", [128, 8, 512], BF16)
        hT = P.sb("hTC", [128, 8, 512], BF16)
        lnv = P.sb("lnvC", [128, 512], F32)
        rs = P.sb("rsC", [128, 512], F32)
        ysb = [P.sb(f"ysbC{i}", [128, 4, 512], BF16) for i in range(2)]
        ysw = [P.sb(f"yswC{i}", [128, 4, 512], BF16) for i in range(2)]
        gate = P.sb("gate", [128, 16, 512], F32)
        t1 = [P.sb(f"t1C{i}", [128, 512], F32) for i in range(2)]
        t2 = [P.sb(f"t2C{i}", [128, 512], F32) for i in range(2)]
        mT = P.sb("mT", [128, 8, 512], BF16)
        x1 = [P.sb(f"x1C{i}", [128, 8, 512], F32) for i in range(1)]
        ssps = P.ps("ssC", [128, 512])
        PS = [P.ps(f"psC{i}", [128, 512]) for i in range(6)]

        load(gm[:], gm_d[:, :])
        load_w_bf16(P, WG, w_in[:, 2304:4352], 8, 2048, gm, stg, 1024)
        load_w_bf16(P, WSB, w_sbp, 4, 1024, None, stg, 1024)
        load_w_bf16(P, WSW, w_swp, 4, 1024, None, stg, 1024)
        load_w_bf16(P, WO, w_out, 8, 1024, None, stg, 1024)
        psi = [0]

        def nextps():
            psi[0] = (psi[0] + 1) % 6
            return PS[psi[0]]

        for gi_, (qs, n, oc) in enumerate(groups):
            s = gi_ % 2
            x = xs[0]
            load(x[:, :, 0:n], xT[:, qs:qs + n].rearrange("(k p) n -> p k n", p=128))
            load(ysb[s][:, :, 0:n], YSB[:, :, oc:oc + n])
            swc = qs - (H - 128)
            load(ysw[s][:, :, 0:n], YSW[:, :, swc:swc + n])
            rms_stats(P, x, n, sq, ssps, lnv, rs)
            for k in range(8):
                tt("dve" if k % 2 == 0 else "pool", hT[:, k, 0:n], x[:, k, 0:n], rs[:, 0:n], ALU.mult)
            for o in range(16):
                ps = nextps()
                pe_group(ps[:, 0:n], [(WG[:, k, o * 128:(o + 1) * 128], hT[:, k, 0:n]) for k in range(8)])
                act(None, gate[:, o, 0:n], ps[:, 0:n], AF.Sigmoid)
            for o in range(8):
                pa = nextps()
                pe_group(pa[:, 0:n], [(WSB[:, k, o * 128:(o + 1) * 128], ysb[s][:, k, 0:n]) for k in range(4)])
                pb = nextps()
                pe_group(pb[:, 0:n], [(WSW[:, k, o * 128:(o + 1) * 128], ysw[s][:, k, 0:n]) for k in range(4)])
                tt("dve", t1[o % 2][:, 0:n], pa[:, 0:n], gate[:, o, 0:n], ALU.mult)
                tt("dve", t2[o % 2][:, 0:n], pb[:, 0:n], gate[:, 8 + o, 0:n], ALU.mult)
                tt("pool", mT[:, o, 0:n], t1[o % 2][:, 0:n], t2[o % 2][:, 0:n], ALU.add)
            for o in range(8):
                ps = nextps()
                pe_group(ps[:, 0:n], [(WO[:, k, o * 128:(o + 1) * 128], mT[:, k, 0:n]) for k in range(8)])
                tt("dve", x1[0][:, o, 0:n], ps[:, 0:n], x[:, o, 0:n], ALU.add)
            store(X1s[:, :, oc:oc + n], x1[0][:, :, 0:n])
        S.barrier()
        P.close()

    def phase_C2a():
        P = Pool_()
        WU = P.sb("WU", [128, 8, 2 * DFF], BF16)
        stg = [P.sb(f"stgU{i}", [128, 1408], F32) for i in range(2)]
        gf = P.sb("gf_s", [128, 8], F32)
        cw = P.sb("cw_s", [128, 44, 3], F32)
        cb = P.sb("cb_s", [128, 44], F32)
        carry = P.sb("carry", [128, 44, 2], F32)
        xs = [P.sb(f"xsU{i}", [128, 8, 512], F32) for i in range(1)]
        sq = P.sb("sqU", [128, 8, 512], BF16)
        hT = P.sb("hTU", [128, 8, 512], BF16)
        lnv = P.sb("lnvU", [128, 512], F32)
        rs = P.sb("rsU", [128, 512], F32)
        yb = [P.sb(f"ybU{i}", [128, 512], F32) for i in range(4)]
        sg = [P.sb(f"sgU{i}", [128, 512], F32) for i in range(2)]
        aT = [P.sb(f"aTU{i}", [128, 22, 512], BF16) for i in range(2)]
        ssps = P.ps("ssU", [128, 512])
        PS = [P.ps(f"psU{i}", [128, 512]) for i in range(6)]

        load(gf[:], gf_d[:, :])
        load(cw[:], cw_d.rearrange("p (j k) -> p j k", k=3))
        load(cb[:], cb_d[:, :])
        memset("dve", carry[:], 0.0)
        load_w_bf16(P, WU, w_up, 8, 2 * DFF, gf, stg, 1408)
        psi = [0]

        def nextps():
            psi[0] = (psi[0] + 1) % 6
            return PS[psi[0]]

        ybi = [0]
        for gi_, (qs, n, oc) in enumerate(groups):
            s = gi_ % 2
            x = xs[0]
            load(x[:, :, 0:n], X1s[:, :, oc:oc + n])
            rms_stats(P, x, n, sq, ssps, lnv, rs)
            for k in range(8):
                tt("dve" if k % 2 == 0 else "pool", hT[:, k, 0:n], x[:, k, 0:n], rs[:, 0:n], ALU.mult)
            for j in range(22):
                ys = []
                for t in (j, 22 + j):
                    ps = nextps()
                    pe_group(ps[:, 0:n], [(WU[:, k, t * 128:(t + 1) * 128], hT[:, k, 0:n]) for k in range(8)])
                    y = yb[ybi[0] % 4]
                    ybi[0] += 1
                    ys.append(y)
                    act(None, y[:, 0:n], ps[:, 0:n], AF.Identity, scale=cw[:, t, 2:3], bias=cb[:, t:t + 1])
                    stt(y[:, 1:n], ps[:, 0:n - 1], cw[:, t, 1:2], y[:, 1:n], ALU.mult, ALU.add)
                    if n > 2:
                        stt(y[:, 2:n], ps[:, 0:n - 2], cw[:, t, 0:1], y[:, 2:n], ALU.mult, ALU.add)
                    stt(y[:, 0:1], carry[:, t, 1:2], cw[:, t, 1:2], y[:, 0:1], ALU.mult, ALU.add)
                    stt(y[:, 0:2], carry[:, t, 0:2], cw[:, t, 0:1], y[:, 0:2], ALU.mult, ALU.add)
                    cp("act", carry[:, t, 0:2], ps[:, n - 2:n])
                sgt = sg[j % 2]
                act(None, sgt[:, 0:n], ys[0][:, 0:n], AF.Silu)
                tt("pool", aT[s][:, j, 0:n], sgt[:, 0:n], ys[1][:, 0:n], ALU.mult)
            store(ATs[:, :, oc:oc + n], aT[s][:, :, 0:n])
        S.barrier()
        P.close()

    def phase_C2b():
        P = Pool_()
        WD = P.sb("WD", [128, 22, 1024], BF16)
        stg = [P.sb(f"stgD{i}", [128, 1024], F32) for i in range(2)]
        glf = P.sb("gl_s", [128, 8], F32)
        xs = [P.sb(f"xsD{i}", [128, 8, 512], F32) for i in range(2)]
        aT = [P.sb(f"aTD{i}", [128, 22, 512], BF16) for i in range(2)]
        x2 = P.sb("x2D", [128, 8, 512], F32)
        sq = P.sb("sqD", [128, 8, 512], BF16)
        lnv = P.sb("lnvD", [128, 512], F32)
        rs = P.sb("rsD", [128, 512], F32)
        ob = [P.sb(f"obD{i}", [128, 8, 512], F32) for i in range(2)]
        ssps = P.ps("ssD", [128, 512])
        PS = [P.ps(f"psD{i}", [128, 512]) for i in range(6)]

        load(glf[:], gl_d[:, :])
        load_w_bf16(P, WD, w_down, 22, 1024, None, stg, 1024)
        psi = [0]

        def nextps():
            psi[0] = (psi[0] + 1) % 6
            return PS[psi[0]]

        for gi_, (qs, n, oc) in enumerate(groups[1:]):
            s = gi_ % 2
            x = xs[s]
            load(x[:], X1s[:, :, oc:oc + n])
            load(aT[s][:], ATs[:, :, oc:oc + n])
            for o in range(8):
                ps = nextps()
                pe_group(ps[:], [(WD[:, k, o * 128:(o + 1) * 128], aT[s][:, k, :]) for k in range(22)])
                tt("dve", x2[:, o, :], ps[:], x[:, o, :], ALU.add)
            rms_stats(P, x2, 512, sq, ssps, lnv, rs)
            for o in range(8):
                stt(ob[s][:, o, :], x2[:, o, :], glf[:, o:o + 1], rs[:], ALU.mult, ALU.mult)
            store(out_d[:, :, oc - 2:oc - 2 + n].rearrange("k p n -> p k n"), ob[s][:])
        S.barrier()
        P.close()

    S.barrier()
    gtmp.close()
    ph = phases.split(",")
    if "A" in ph:
        phase_A()
    if "B1" in ph:
        phase_B1()
    if "B2" in ph:
        phase_B2()
    if "C1" in ph:
        phase_C1()
    if "C2a" in ph:
        phase_C2a()
    if "C2b" in ph:
        phase_C2b()
    gl.close()

    with ExitStack() as es:
        sems = {}
        for k in S.keys:
            nm = k if isinstance(k, str) else f"d{k[1]}"
            sems[k] = es.enter_context(nc.semaphore("s_" + nm))
        block = es.enter_context(nc.Block())

        @block.tensor
        def _(e):
            S.replay("pe", e, sems)

        @block.scalar
        def _(e):
            S.replay("act", e, sems)

        @block.vector
        def _(e):
            S.replay("dve", e, sems)

        @block.gpsimd
        def _(e):
            S.replay("pool", e, sems)

        @block.sync
        def _(e):
            S.replay("sp", e, sems)

    return nc


def _t5_bucket(dist):
    dist = np.asarray(dist, np.int32)
    max_exact = 16
    d = np.maximum(dist, 1).astype(np.float32)
    large = max_exact + (np.log(d / np.float32(max_exact)) / np.float32(np.log(128 / max_exact))
                         * np.float32(32 - max_exact)).astype(np.int32)
    large = np.minimum(large, 31)
    return np.where(dist < max_exact, dist, large)


def _consts():
    c = np.zeros((128, 2688), np.float32)
    c[:, 0:128] = 1.0
    j = np.arange(128)[:, None]
    s = np.arange(128)[None, :]
    c[:, 128:256] = (j >= s).astype(np.float32)
    c[:, 256:384] = np.eye(128, dtype=np.float32)
    q = np.arange(512)[None, :]
    for jj in range(4):
        valid = (128 * jj + j) < q
        c[:, 384 + 512 * jj:384 + 512 * (jj + 1)] = np.where(valid, 0.0, 8.0 * NEGM)
    c[:, 2432:2432 + 64] = 1.0
    c[:, 2432 + 128 + 64:2432 + 256] = 1.0
    return c


def prep_inputs(NH, x, g_mix, w_in, w_sb_proj, w_sw_proj, w_out, rel_bias, sinks,
                g_ffn, w_up, conv_w, conv_b, w_down, g_final):
    H = NH * 512
    B = x.shape[0]
    f = lambda a: np.ascontiguousarray(np.asarray(a, dtype=np.float32))
    x = f(x)
    lay8 = lambda g: f(np.asarray(g, np.float32).reshape(8, 128).T)
    rel_bias = np.asarray(rel_bias, np.float32)
    k = np.arange(128)[:, None]
    q = np.arange(128)[None, :]
    bm = np.zeros((128, 2, 2, 4, 128), np.float32)
    d0 = 128 + q - k
    d1 = q - k
    b0 = _t5_bucket(np.clip(d0, 0, 255))
    b1 = _t5_bucket(np.clip(d1, 0, 255))
    for h in range(8):
        bm[:, 0, h % 2, h // 2, :] = np.where(d0 <= 127, rel_bias[b0, h], NEGM)
        bm[:, 1, h % 2, h // 2, :] = np.where(d1 >= 0, rel_bias[b1, h], NEGM)
    sinks = np.asarray(sinks, np.float32)
    sk = np.zeros((128, 4), np.float32)
    for hp in range(4):
        sk[0:64, hp] = sinks[2 * hp]
        sk[64:128, hp] = sinks[2 * hp + 1]
    cw = np.asarray(conv_w, np.float32)
    cwl = np.ascontiguousarray(cw.reshape(3, 44, 128).transpose(2, 1, 0)).reshape(128, 132)
    cbl = f(np.asarray(conv_b, np.float32).reshape(44, 128).T)
    shared = {
        "w_in": f(w_in), "w_sbp": f(w_sb_proj), "w_swp": f(w_sw_proj), "w_out": f(w_out),
        "w_up": f(w_up), "w_down": f(w_down), "gm": lay8(g_mix), "gf": lay8(g_ffn), "gl": lay8(g_final),
        "cw": f(cwl), "cb": cbl, "bm": f(bm.reshape(128, 2048)), "sk": sk, "cst": _consts(),
    }
    in_maps = []
    for c in range(2 * B):
        b, p = c // 2, c % 2
        xl = np.zeros((2 * H, D), np.float32)
        kb = np.zeros((128, 2 * H // 128), np.float32)
        if p == 1:
            xl[:] = x[b]
        else:
            xl[H:] = x[b, :H]
            kb[:, :H // 128] = NEGM
        m = dict(shared)
        m["xT"] = np.ascontiguousarray(xl.T)
        m["kbias"] = kb
        in_maps.append(m)
    return in_maps


_NC_CACHE = {}


def run(NH, phases="A,B1,B2,C1,C2a,C2b", **inputs):
    x = np.asarray(inputs["x"])
    B = x.shape[0]
    H = NH * 512
    in_maps = prep_inputs(NH, **inputs)
    if (NH, phases) not in _NC_CACHE:
        _NC_CACHE[(NH, phases)] = build_nc(NH, phases)
    nc = _NC_CACHE[(NH, phases)]
    res = run_bass_kernel_spmd(nc, in_maps, core_ids=list(range(2 * B)))
    out = np.zeros((B, 2 * H, D), np.float32)
    for c in range(2 * B):
        b, p = c // 2, c % 2
        o = np.asarray(res.results[c]["out"]).reshape(D, H)
        out[b, p * H:(p + 1) * H, :] = o.T
    return out


def kernel(**inputs):
    return run(8, **inputs)
```

```python
import numpy as np
from contextlib import ExitStack
import concourse.bass as bass
import concourse.mybir as mybir
from concourse.bass_utils import run_bass_kernel_spmd

F32 = mybir.dt.float32
BF16 = mybir.dt.bfloat16
AF = mybir.ActivationFunctionType
ALU = mybir.AluOpType

D = 1024
KT = 8
DFF = 2816
NEGM = -100.0
ENGS = ("pe", "act", "dve", "pool", "sp")
NDMA = 24
NDMA_HW = 16


class Buf:
    __slots__ = ("w", "r", "name")

    def __init__(self, name=""):
        self.w = None
        self.r = {}
        self.name = name


class V:
    __slots__ = ("buf", "ap")

    def __init__(self, buf, ap):
        self.buf = buf
        self.ap = ap


class Tile:
    def __init__(self, handle, name=""):
        self.t = handle
        self.buf = Buf(name)

    def __getitem__(self, idx):
        return V(self.buf, self.t[idx])


class Sched:
    def __init__(self):
        self.keys = list(ENGS) + [("d", i) for i in range(NDMA)]
        self.cnt = {k: 0 for k in self.keys}
        self.seen = {e: {k: 0 for k in self.keys} for e in ENGS}
        self.q = {e: [] for e in ENGS}
        self.dma_rr = 0
        self.dma_rr2 = 0
        self.nops = 0

    def op(self, eng, fns, reads=(), writes=(), dma=False):
        if callable(fns):
            fns = [fns]
        need = {}

        def req(tok):
            if tok is None:
                return
            k, v = tok
            if need.get(k, 0) < v:
                need[k] = v

        for b in reads:
            req(b.w)
        for b in writes:
            req(b.w)
            for k, v in b.r.items():
                req((k, v))
        if dma:
            if eng == "sp":
                key = ("d", self.dma_rr)
                self.dma_rr = (self.dma_rr + 1) % NDMA_HW
            else:
                key = ("d", NDMA_HW + self.dma_rr2)
                self.dma_rr2 = (self.dma_rr2 + 1) % (NDMA - NDMA_HW)
            req((key, self.cnt[key]))
            inc = 16
        else:
            key = eng
            inc = 1
        waits = []
        seen = self.seen[eng]
        for k, v in need.items():
            if v <= 0 or seen[k] >= v:
                continue
            if k == eng and eng == "pe":
                continue
            seen[k] = v
            waits.append((k, v))
        self.cnt[key] += inc
        tok = (key, self.cnt[key])
        for b in writes:
            b.w = tok
            b.r = {}
        for b in reads:
            if b.r.get(key, 0) < tok[1]:
                b.r[key] = tok[1]
        self.q[eng].append((waits, fns, key, inc))
        self.nops += 1
        return tok

    def barrier(self):
        for e in ENGS:
            waits = []
            for k in self.keys:
                v = self.cnt[k]
                if v > 0 and self.seen[e][k] < v and k != e:
                    self.seen[e][k] = v
                    waits.append((k, v))
            if waits:
                self.q[e].append((waits, [], None, 0))

    def replay(self, eng, e, sems):
        for waits, fns, key, inc in self.q[eng]:
            for k, v in waits:
                e.wait_ge(sems[k], v)
            ins = None
            for f in fns:
                ins = f(e)
            if ins is not None and key is not None:
                ins.then_inc(sems[key], inc)


def build_nc(NH, phases="A,B1,B2,C1,C2a,C2b"):
    H = NH * 512
    L2 = 2 * H
    NB2 = L2 // 128
    NBH = H // 128
    NOWN = (NH + 1) * 512
    NY = 2 + NH * 512
    NSWB = NH * 4 + 1

    nc = bass.Bass("TRN2", target_bir_lowering=False)
    S = Sched()

    def din(name, shape):
        return nc.dram_tensor(name, shape, F32, kind="ExternalInput").ap()

    xT = din("xT", [D, L2])
    kbias_d = din("kbias", [128, NB2])
    w_in = din("w_in", [D, 4352])
    w_sbp = din("w_sbp", [512, D])
    w_swp = din("w_swp", [512, D])
    w_out = din("w_out", [D, D])
    w_up = din("w_up", [D, 2 * DFF])
    w_down = din("w_down", [DFF, D])
    gm_d = din("gm", [128, 8])
    gf_d = din("gf", [128, 8])
    gl_d = din("gl", [128, 8])
    cw_d = din("cw", [128, 44 * 3])
    cb_d = din("cb", [128, 44])
    bm_d = din("bm", [128, 2048])
    sk_d = din("sk", [128, 4])
    cst_d = din("cst", [128, 2688])
    out_d = nc.dram_tensor("out", [8, 128, H], F32, kind="ExternalOutput").ap()

    def dscr(name, shape, dt):
        return nc.dram_tensor(name, shape, dt, kind="Internal").ap()

    KTs = dscr("KTs", [4, 128, L2], BF16)
    QTs = dscr("QTs", [4, 128, NOWN], BF16)
    VSs = dscr("VSs", [4, 128, NB2, 256], BF16)
    QWs = dscr("QWs", [4, 128, NOWN], BF16)
    KWs = dscr("KWs", [2, 128, NOWN], BF16)
    VWs = dscr("VWs", [128, (NH + 1) * 4, 512], BF16)
    YSB = dscr("YSB", [128, 4, NY], BF16)
    YSW = dscr("YSW", [128, 4, NSWB * 128], BF16)
    X1s = dscr("X1s", [128, 8, NY], F32)
    ATs = dscr("ATs", [128, 22, NY], BF16)

    groups = [(H - 2, 2, 0)] + [(H + 512 * c, 512, 2 + 512 * c) for c in range(NH)]

    def mm(out, lhsT, rhs, start, stop):
        return lambda e: e.matmul(out.ap, lhsT.ap, rhs.ap, start=start, stop=stop)

    def pe_group(out, pairs, extra_reads=()):
        fns = []
        n = len(pairs)
        rd = []
        for i, (l, r) in enumerate(pairs):
            fns.append(mm(out, l, r, i == 0, i == n - 1))
            rd += [l.buf, r.buf]
        S.op("pe", fns, reads=rd + list(extra_reads), writes=[out.buf])

    def act(eng_unused, out, in_, func, scale=1.0, bias=0.0, extra_reads=()):
        def f(e):
            kw = {}
            if isinstance(bias, V):
                kw["bias"] = bias.ap
            else:
                kw["bias"] = float(bias)
            if isinstance(scale, V):
                kw["scale"] = scale.ap
            else:
                kw["scale"] = float(scale)
            return e.activation(out.ap, in_.ap, func, **kw)

        rd = [in_.buf] + [x.buf for x in (scale, bias) if isinstance(x, V)] + list(extra_reads)
        S.op("act", f, reads=rd, writes=[out.buf])

    def tt(eng, out, a, b, op):
        S.op(eng, lambda e: e.tensor_tensor(out.ap, a.ap, b.ap, op), reads=[a.buf, b.buf], writes=[out.buf])

    def ts(eng, out, a, s1, op0, s2=None, op1=None):
        def f(e):
            sc1 = s1.ap if isinstance(s1, V) else float(s1)
            if s2 is None:
                return e.tensor_scalar(out.ap, a.ap, sc1, None, op0)
            sc2 = s2.ap if isinstance(s2, V) else float(s2)
            return e.tensor_scalar(out.ap, a.ap, sc1, sc2, op0, op1)

        rd = [a.buf] + [x.buf for x in (s1, s2) if isinstance(x, V)]
        S.op(eng, f, reads=rd, writes=[out.buf])

    def stt(out, a, sc, b, op0, op1):
        def f(e):
            s = sc.ap if isinstance(sc, V) else float(sc)
            return e.scalar_tensor_tensor(out.ap, a.ap, s, b.ap, op0, op1)

        rd = [a.buf, b.buf] + ([sc.buf] if isinstance(sc, V) else [])
        S.op("dve", f, reads=rd, writes=[out.buf])

    def cp(eng, out, a):
        if eng == "act":
            act(None, out, a, AF.Copy)
        else:
            S.op(eng, lambda e: e.tensor_copy(out.ap, a.ap), reads=[a.buf], writes=[out.buf])

    def memset(eng, out, val):
        S.op(eng, lambda e: e.memset(out.ap, val), writes=[out.buf])

    def load(out, src_ap, eng="sp"):
        S.op(eng, lambda e: e.dma_start(out=out.ap, in_=src_ap), writes=[out.buf], dma=True)

    def store(dst_ap, src, eng="pool"):
        S.op(eng, lambda e: e.dma_start(out=dst_ap, in_=src.ap), reads=[src.buf], dma=True)

    class Pool_:
        def __init__(self):
            self.es = ExitStack()

        def sb(self, name, shape, dt):
            return Tile(self.es.enter_context(nc.sbuf_tensor(name, shape, dt)), name)

        def ps(self, name, shape):
            return Tile(self.es.enter_context(nc.psum_tensor(name, shape, F32)), name)

        def close(self):
            self.es.close()

    gl = Pool_()
    ones = gl.sb("ones", [128, 128], BF16)
    tri = gl.sb("tri", [128, 128], BF16)
    ident = gl.sb("ident", [128, 128], BF16)
    dmask = gl.sb("dmask", [128, 4, 512], BF16)
    opad = gl.sb("opad", [128, 2, 128], BF16)
    kbias = gl.sb("kbias_s", [128, NB2], F32)
    gtmp = Pool_()
    cstf = gtmp.sb("cstf", [128, 2688], F32)

    load(cstf[:], cst_d[:, :])
    load(kbias[:], kbias_d[:, :])
    cp("dve", ones[:], cstf[:, 0:128])
    cp("dve", tri[:], cstf[:, 128:256])
    cp("dve", ident[:], cstf[:, 256:384])
    for j in range(4):
        cp("dve", dmask[:, j, :], cstf[:, 384 + 512 * j:384 + 512 * (j + 1)])
    for j in range(2):
        cp("dve", opad[:, j, :], cstf[:, 2432 + 128 * j:2432 + 128 * (j + 1)])

    def rms_stats(P, xs, n, sq, ssps, lnv, rs):
        act(None, sq[:, :, 0:n], xs[:, :, 0:n], AF.Square)
        pe_group(ssps[:, 0:n], [(ones[:], sq[:, k, 0:n]) for k in range(8)])
        act(None, lnv[:, 0:n], ssps[:, 0:n], AF.Ln, scale=1.0 / D, bias=1e-6)
        act(None, rs[:, 0:n], lnv[:, 0:n], AF.Exp, scale=-0.5)

    def phase_A():
        P = Pool_()
        WA = P.sb("WA", [128, 8, 3328], BF16)
        stg = [P.sb(f"wstg{i}", [128, 2304], F32) for i in range(2)]
        gm = P.sb("gm_s", [128, 8], F32)
        xs = [P.sb(f"xsA{i}", [128, 8, 512], F32) for i in range(2)]
        sq = P.sb("sqA", [128, 8, 512], BF16)
        hT = [P.sb(f"hTA{i}", [128, 8, 512], BF16) for i in range(2)]
        lnv = P.sb("lnvA", [128, 512], F32)
        rs = [P.sb(f"rsA{i}", [128, 512], F32) for i in range(2)]
        stF = [P.sb(f"stF{i}", [128, 14, 512], BF16) for i in range(2)]
        stV = [P.sb(f"stV{i}", [128, 4, 1536], BF16) for i in range(2)]
        ssps = P.ps("ssA", [128, 512])
        PS = [P.ps(f"psA{i}", [128, 512]) for i in range(6)]

        load(gm[:], gm_d[:, :])
        memset("pool", WA[:, :, 1792:3328], 0.0)
        for kt in range(8):
            st = stg[kt % 2]
            load(st[:], w_in[kt * 128:(kt + 1) * 128, 0:2304])
            g = gm[:, kt:kt + 1]
            e1 = "dve"
            act(None, WA[:, kt, 0:1024], st[:, 0:1024], AF.Identity, scale=g)
            ts(e1, WA[:, kt, 1024:1536], st[:, 1536:2048], g, ALU.mult)
            for gg in range(2):
                for r in range(2):
                    ts(e1, WA[:, kt, 1536 + 128 * gg + 64 * r:1536 + 128 * gg + 64 * (r + 1)],
                       st[:, 2048 + 64 * gg:2048 + 64 * (gg + 1)], g, ALU.mult)
            for h in range(8):
                o = 1792 + 128 * h + 64 * (h % 2)
                ts(e1, WA[:, kt, o:o + 64], st[:, 1024 + 64 * h:1024 + 64 * (h + 1)], g, ALU.mult)
            for i in range(4):
                o = 2816 + 128 * i + 64 * (i % 2)
                gg = i // 2
                ts(e1, WA[:, kt, o:o + 64], st[:, 2176 + 64 * gg:2176 + 64 * (gg + 1)], g, ALU.mult)

        psi = [0]

        def nextps():
            psi[0] = (psi[0] + 1) % 6
            return PS[psi[0]]

        evi = [0]

        def evac(out, in_):
            evi[0] += 1
            cp("act" if evi[0] % 2 == 0 else "dve", out, in_)

        def stage1(tc):
            s = tc % 2
            load(xs[s][:], xT[:, tc * 512:(tc + 1) * 512].rearrange("(k p) n -> p k n", p=128))
            rms_stats(P, xs[s], 512, sq, ssps, lnv, rs[s])
            for k in range(8):
                tt("dve" if k % 2 == 0 else "pool", hT[s][:, k, :], xs[s][:, k, :], rs[s][:], ALU.mult)

        def stage2(tc):
            s = tc % 2
            own = tc >= NH - 1
            h = hT[s]
            sf = stF[s]
            sv = stV[s]

            def fm(slot, col0):
                ps = nextps()
                pe_group(ps[:], [(WA[:, k, col0:col0 + 128], h[:, k, :]) for k in range(8)])
                evac(sf[:, slot, :], ps[:])

            for hp in range(4):
                fm(hp, 512 + 128 * hp)
            if own:
                for hp in range(4):
                    fm(4 + hp, 128 * hp)
                for hp in range(4):
                    fm(8 + hp, 1024 + 128 * hp)
                for gg in range(2):
                    fm(12 + gg, 1536 + 128 * gg)
            for blk in range(4):
                for half in range(2):
                    ps = nextps()
                    pe_group(ps[:], [(h[:, k, blk * 128:(blk + 1) * 128],
                                      WA[:, k, 1792 + 512 * half:1792 + 512 * (half + 1)]) for k in range(8)])
                    evac(sv[:, blk, 512 * half:512 * (half + 1)], ps[:])
                if own:
                    ps = nextps()
                    pe_group(ps[:], [(h[:, k, blk * 128:(blk + 1) * 128], WA[:, k, 2816:3328]) for k in range(8)])
                    evac(sv[:, blk, 1024:1536], ps[:])
            c0 = tc * 512
            store(KTs[:, :, c0:c0 + 512].rearrange("h p n -> p h n"), sf[:, 0:4, :])
            for hp in range(4):
                store(VSs[hp, :, 4 * tc:4 * tc + 4, :], sv[:, :, 256 * hp:256 * (hp + 1)])
            if own:
                o0 = (tc - (NH - 1)) * 512
                store(QTs[:, :, o0:o0 + 512].rearrange("h p n -> p h n"), sf[:, 4:8, :])
                store(QWs[:, :, o0:o0 + 512].rearrange("h p n -> p h n"), sf[:, 8:12, :])
                store(KWs[:, :, o0:o0 + 512].rearrange("h p n -> p h n"), sf[:, 12:14, :])
                b0 = (tc - (NH - 1)) * 4
                store(VWs[:, b0:b0 + 4, :], sv[:, :, 1024:1536])

        NT = 2 * NH
        stage1(0)
        for tc in range(NT):
            if tc + 1 < NT:
                stage1(tc + 1)
            stage2(tc)
        S.barrier()
        P.close()

    def phase_B1():
        P = Pool_()
        QW = P.sb("QW", [128, 4, NOWN], BF16)
        KW = P.sb("KW", [128, 2, NOWN], BF16)
        VW = P.sb("VW", [128, (NH + 1) * 4, 512], BF16)
        BM = P.sb("BM", [128, 2048], F32)
        sk = P.sb("sk_s", [128, 4], F32)
        esk = P.sb("esk", [128, 4], F32)
        ESB = P.sb("ESB", [128, 4, 128], F32)
        zer = P.sb("zerB1", [128, 128], F32)
        LG = [P.sb(f"LG{i}", [128, 2048], F32) for i in range(2)]
        PB = [P.sb(f"PB{i}", [128, 2, 2, 4, 128], BF16) for i in range(2)]
        den = P.sb("den", [128, 512], F32)
        rec = P.sb("rec", [128, 512], F32)
        yst = [P.sb(f"yst{i}", [128, 4, 128], BF16) for i in range(2)]
        ZS = P.ps("ZS", [128, 2, 2, 4, 128])
        OP = P.ps("OPs", [128, 4, 128])
        DN = P.ps("DNs", [128, 4, 128])

        load(QW[:], QWs.rearrange("h p n -> p h n"))
        load(KW[:], KWs.rearrange("h p n -> p h n"))
        load(VW[:], VWs[:, :, :])
        load(BM[:], bm_d[:, :])
        load(sk[:], sk_d[:, :])
        act(None, esk[:], sk[:], AF.Exp)
        memset("dve", zer[:], 0.0)
        for hp in range(4):
            ts("dve", ESB[:, hp, :], zer[:], esk[:, hp:hp + 1], ALU.add)

        for qi, i in enumerate(range(3, (NH + 1) * 4)):
            s = qi % 2
            qcols = slice(i * 128, (i + 1) * 128)
            for kbsel in range(2):
                ib = i - 1 + kbsel
                for h in range(8):
                    po = (h % 2) * 64
                    g = h // 4
                    hp = h // 2
                    S.op("pe", mm(ZS[:, kbsel, h % 2, hp, :], KW[po:po + 64, g, ib * 128:(ib + 1) * 128],
                                  QW[po:po + 64, hp, qcols], True, True),
                         reads=[KW.buf, QW.buf], writes=[ZS.buf])
            stt(LG[s][:], V(ZS.buf, ZS.t[:].rearrange("p a r h q -> p (a r h q)")), 0.125, BM[:], ALU.mult, ALU.add)
            for kbsel in range(2):
                lb = (NBH - 4) + i - 1 + kbsel
                act(None, V(PB[s].buf, PB[s].t[:, kbsel, :, :, :].rearrange("p r h q -> p (r h q)")),
                    LG[s][:, kbsel * 1024:(kbsel + 1) * 1024], AF.Exp, bias=kbias[:, lb:lb + 1])
            for hp in range(4):
                g = hp // 2
                prs = []
                prd = []
                for kbsel in range(2):
                    ib = i - 1 + kbsel
                    for r in range(2):
                        h = 2 * hp + r
                        var = 2 * g + r
                        prs.append((VW[:, ib, 128 * var:128 * (var + 1)], PB[s][:, kbsel, r, hp, :]))
                        prd.append((opad[:, r, :], PB[s][:, kbsel, r, hp, :]))
                pe_group(OP[:, hp, :], prs)
                pe_group(DN[:, hp, :], prd)
            tt("dve", den[:], V(DN.buf, DN.t[:].rearrange("p a q -> p (a q)")),
               V(ESB.buf, ESB.t[:].rearrange("p a q -> p (a q)")), ALU.add)
            S.op("dve", lambda e: e.reciprocal(rec.t[:], den.t[:]), reads=[den.buf], writes=[rec.buf])
            tt("dve", V(yst[s].buf, yst[s].t[:].rearrange("p a q -> p (a q)")),
               V(OP.buf, OP.t[:].rearrange("p a q -> p (a q)")), rec[:], ALU.mult)
            store(YSW[:, :, qi * 128:(qi + 1) * 128], yst[s][:])
        S.barrier()
        P.close()

    def phase_B2():
        P = Pool_()
        KTt = [P.sb(f"KTt{i}", [128, L2], BF16) for i in range(2)]
        Vt = [P.sb(f"Vt{i}", [128, NB2, 256], BF16) for i in range(2)]
        QTt = [P.sb(f"QTt{i}", [128, NOWN], BF16) for i in range(2)]
        NBUF = 4
        Eb = [P.sb(f"Eb{i}", [128, 2, 512], BF16) for i in range(NBUF)]
        Lb = [P.sb(f"Lb{i}", [128, 2, 512], BF16) for i in range(NBUF)]
        Gb = [P.sb(f"Gb{i}", [128, 2, 512], BF16) for i in range(NBUF)]
        Wb = [P.sb(f"Wb{i}", [128, 2, 512], BF16) for i in range(NBUF)]
        triC = P.sb("triC", [128, 128], BF16)
        yev = [P.sb(f"yev{i}", [128, 512], BF16) for i in range(2)]
        Z = [P.ps(f"Zp{i}", [128, 2, 512]) for i in range(2)]
        X = P.ps("Xp", [128, 2, 512])
        Y = [P.ps(f"Yp{i}", [128, 512]) for i in range(2)]
        tt("dve", triC[:], ones[:], tri[:], ALU.subtract)

        def load_hp(hp):
            s = hp % 2
            hl = L2 // 2
            load(KTt[s][:, 0:hl], KTs[hp, :, 0:hl])
            load(KTt[s][:, hl:L2], KTs[hp, :, hl:L2])
            load(Vt[s][:, 0:NB2 // 2, :], VSs[hp, :, 0:NB2 // 2, :])
            load(Vt[s][:, NB2 // 2:NB2, :], VSs[hp, :, NB2 // 2:NB2, :])
            load(QTt[s][:], QTs[hp, :, :])

        gi = [0]
        load_hp(0)
        for hp in range(4):
            s = hp % 2
            if hp + 1 < 4:
                load_hp(hp + 1)
            Kt, Vv, Qt = KTt[s], Vt[s], QTt[s]
            for (qs, n, ycol) in groups:
                qc = qs - (NH - 1) * 512
                kbmax = (qs + n - 2) // 128
                kbs = list(range(kbmax, -1, -1))
                U = len(kbs)
                Yp = Y[gi[0] % 2]
                ye = yev[gi[0] % 2]
                gi[0] += 1

                def s_Z(u):
                    kb = kbs[u]
                    j = kb - qs // 128
                    Zt = Z[u % 2]
                    for r in range(2):
                        po = 64 * r
                        prs = [(Kt[po:po + 64, kb * 128:(kb + 1) * 128], Qt[po:po + 64, qc:qc + n])]
                        if j >= 0:
                            if n == 512:
                                prs.append((ident[:], dmask[:, j, :]))
                            else:
                                prs.append((ident[:], dmask[:, 0, 126:128]))
                        pe_group(Zt[:, r, 0:n], prs)

                def s_EL(u):
                    kb = kbs[u]
                    b = u % NBUF
                    act(None, Eb[b][:, :, 0:n], Z[u % 2][:, :, 0:n], AF.Exp, scale=0.125, bias=kbias[:, kb:kb + 1])
                    act(None, Lb[b][:, :, 0:n], Eb[b][:, :, 0:n], AF.Ln, bias=1.0)

                def mmx(out, lhsT, rhs, start):
                    return lambda e: e.matmul(out.ap, lhsT.ap, rhs.ap, start=start, stop=True,
                                              skip_group_check=True)

                def s_A(u):
                    b = u % NBUF
                    fns = [mmx(X[:, r, 0:n], tri[:], Lb[b][:, r, 0:n], u == 0) for r in range(2)]
                    S.op("pe", fns, reads=[tri.buf, Lb[b].buf], writes=[X.buf])

                def s_B(u):
                    b = u % NBUF
                    fns = [mmx(X[:, r, 0:n], triC[:], Lb[b][:, r, 0:n], False) for r in range(2)]
                    S.op("pe", fns, reads=[triC.buf, Lb[b].buf], writes=[X.buf])

                def s_G(u):
                    b = u % NBUF
                    act(None, Gb[b][:, :, 0:n], X[:, :, 0:n], AF.Exp, scale=-1.0)
                    tt("dve", Wb[b][:, :, 0:n], Eb[b][:, :, 0:n], Gb[b][:, :, 0:n], ALU.mult)

                def s_Y(u):
                    kb = kbs[u]
                    b = u % NBUF
                    fns = []
                    for r in range(2):
                        fns.append(mm(Yp[:, 0:n], Vv[:, kb, 128 * r:128 * (r + 1)], Wb[b][:, r, 0:n],
                                      (u == 0 and r == 0), (u == U - 1 and r == 1)))
                    S.op("pe", fns, reads=[Vv.buf, Wb[b].buf], writes=[Yp.buf])

                s_Z(0)
                if U > 1:
                    s_Z(1)
                s_EL(0)
                if U > 2:
                    s_Z(2)
                s_A(0)
                for t in range(U):
                    if t + 1 < U:
                        s_EL(t + 1)
                    s_G(t)
                    if t + 1 < U:
                        s_B(t)
                        s_A(t + 1)
                    if t + 3 < U:
                        s_Z(t + 3)
                    if t >= 1:
                        s_Y(t - 1)
                s_Y(U - 1)
                cp("dve", ye[:, 0:n], Yp[:, 0:n])
                store(YSB[:, hp, ycol:ycol + n], ye[:, 0:n])
        S.barrier()
        P.close()

    def load_w_bf16(P, W, src, nk, ncol, gvec, stg, piece):
        i = 0
        for k in range(nk):
            for c0 in range(0, ncol, piece):
                c1 = min(ncol, c0 + piece)
                st = stg[i % 2]
                eng = "dve" if i % 2 == 0 else "act"
                i += 1
                load(st[:, 0:c1 - c0], src[k * 128:(k + 1) * 128, c0:c1])
                if gvec is None:
                    cp(eng, W[:, k, c0:c1], st[:, 0:c1 - c0])
                elif eng == "act":
                    act(None, W[:, k, c0:c1], st[:, 0:c1 - c0], AF.Identity, scale=gvec[:, k:k + 1])
                else:
                    ts(eng, W[:, k, c0:c1], st[:, 0:c1 - c0], gvec[:, k:k + 1], ALU.mult)

    def phase_C1():
        P = Pool_()
        WG = P.sb("WG", [128, 8, 2048], BF16)
        WSB = P.sb("WSB", [128, 4, 1024], BF16)
        WSW = P.sb("WSW", [128, 4, 1024], BF16)
        WO = P.sb("WO", [128, 8, 1024], BF16)
        stg = [P.sb(f"stgC1{i}", [128, 1024], F32) for i in range(2)]
        gm = P.sb("gm_c1", [128, 8], F32)
        xs = [P.sb(f"xsC{i}", [128, 8, 512], F32) for i in range(1)]
        sq = P.sb("sqC", [128, 8, 512], BF16)
        hT = P.sb("hTC", [128, 8, 512], BF16)
        lnv = P.sb("lnvC", [128, 512], F32)
        rs = P.sb("rsC", [128, 512], F32)
        ysb = [P.sb(f"ysbC{i}", [128, 4, 512], BF16) for i in range(2)]
        ysw = [P.sb(f"yswC{i}", [128, 4, 512], BF16) for i in range(2)]
        gate = P.sb("gate", [128, 16, 512], F32)
        t1 = [P.sb(f"t1C{i}", [128, 512], F32) for i in range(2)]
        t2 = [P.sb(f"t2C{i}", [128, 512], F32) for i in range(2)]
        mT = P.sb("mT", [128, 8, 512], BF16)
        x1 = [P.sb(f"x1C{i}", [128, 8, 512], F32) for i in range(1)]
        ssps = P.ps("ssC", [128, 512])
        PS = [P.ps(f"psC{i}", [128, 512]) for i in range(6)]

        load(gm[:], gm_d[:, :])
        load_w_bf16(P, WG, w_in[:, 2304:4352], 8, 2048, gm, stg, 1024)
        load_w_bf16(P, WSB, w_sbp, 4, 1024, None, stg, 1024)
        load_w_bf16(P, WSW, w_swp, 4, 1024, None, stg, 1024)
        load_w_bf16(P, WO, w_out, 8, 1024, None, stg, 1024)
        psi = [0]

        def nextps():
            psi[0] = (psi[0] + 1) % 6
            return PS[psi[0]]

        for gi_, (qs, n, oc) in enumerate(groups):
            s = gi_ % 2
            x = xs[0]
            load(x[:, :, 0:n], xT[:, qs:qs + n].rearrange("(k p) n -> p k n", p=128))
            load(ysb[s][:, :, 0:n], YSB[:, :, oc:oc + n])
            swc = qs - (H - 128)
            load(ysw[s][:, :, 0:n], YSW[:, :, swc:swc + n])
            rms_stats(P, x, n, sq, ssps, lnv, rs)
            for k in range(8):
                tt("dve" if k % 2 == 0 else "pool", hT[:, k, 0:n], x[:, k, 0:n], rs[:, 0:n], ALU.mult)
            for o in range(16):
                ps = nextps()
                pe_group(ps[:, 0:n], [(WG[:, k, o * 128:(o + 1) * 128], hT[:, k, 0:n]) for k in range(8)])
                act(None, gate[:, o, 0:n], ps[:, 0:n], AF.Sigmoid)
            for o in range(8):
                pa = nextps()
                pe_group(pa[:, 0:n], [(WSB[:, k, o * 128:(o + 1) * 128], ysb[s][:, k, 0:n]) for k in range(4)])
                pb = nextps()
                pe_group(pb[:, 0:n], [(WSW[:, k, o * 128:(o + 1) * 128], ysw[s][:, k, 0:n]) for k in range(4)])
                tt("dve", t1[o % 2][:, 0:n], pa[:, 0:n], gate[:, o, 0:n], ALU.mult)
                tt("dve", t2[o % 2][:, 0:n], pb[:, 0:n], gate[:, 8 + o, 0:n], ALU.mult)
                tt("pool", mT[:, o, 0:n], t1[o % 2][:, 0:n], t2[o % 2][:, 0:n], ALU.add)
            for o in range(8):
                ps = nextps()
                pe_group(ps[:, 0:n], [(WO[:, k, o * 128:(o + 1) * 128], mT[:, k, 0:n]) for k in range(8)])
                tt("dve", x1[0][:, o, 0:n], ps[:, 0:n], x[:, o, 0:n], ALU.add)
            store(X1s[:, :, oc:oc + n], x1[0][:, :, 0:n])
        S.barrier()
        P.close()

    def phase_C2a():
        P = Pool_()
        WU = P.sb("WU", [128, 8, 2 * DFF], BF16)
        stg = [P.sb(f"stgU{i}", [128, 1408], F32) for i in range(2)]
        gf = P.sb("gf_s", [128, 8], F32)
        cw = P.sb("cw_s", [128, 44, 3], F32)
        cb = P.sb("cb_s", [128, 44], F32)
        carry = P.sb("carry", [128, 44, 2], F32)
        xs = [P.sb(f"xsU{i}", [128, 8, 512], F32) for i in range(1)]
        sq = P.sb("sqU", [128, 8, 512], BF16)
        hT = P.sb("hTU", [128, 8, 512], BF16)
        lnv = P.sb("lnvU", [128, 512], F32)
        rs = P.sb("rsU", [128, 512], F32)
        yb = [P.sb(f"ybU{i}", [128, 512], F32) for i in range(4)]
        sg = [P.sb(f"sgU{i}", [128, 512], F32) for i in range(2)]
        aT = [P.sb(f"aTU{i}", [128, 22, 512], BF16) for i in range(2)]
        ssps = P.ps("ssU", [128, 512])
        PS = [P.ps(f"psU{i}", [128, 512]) for i in range(6)]

        load(gf[:], gf_d[:, :])
        load(cw[:], cw_d.rearrange("p (j k) -> p j k", k=3))
        load(cb[:], cb_d[:, :])
        memset("dve", carry[:], 0.0)
        load_w_bf16(P, WU, w_up, 8, 2 * DFF, gf, stg, 1408)
        psi = [0]

        def nextps():
            psi[0] = (psi[0] + 1) % 6
            return PS[psi[0]]

        ybi = [0]
        for gi_, (qs, n, oc) in enumerate(groups):
            s = gi_ % 2
            x = xs[0]
            load(x[:, :, 0:n], X1s[:, :, oc:oc + n])
            rms_stats(P, x, n, sq, ssps, lnv, rs)
            for k in range(8):
                tt("dve" if k % 2 == 0 else "pool", hT[:, k, 0:n], x[:, k, 0:n], rs[:, 0:n], ALU.mult)
            for j in range(22):
                ys = []
                for t in (j, 22 + j):
                    ps = nextps()
                    pe_group(ps[:, 0:n], [(WU[:, k, t * 128:(t + 1) * 128], hT[:, k, 0:n]) for k in range(8)])
                    y = yb[ybi[0] % 4]
                    ybi[0] += 1
                    ys.append(y)
                    act(None, y[:, 0:n], ps[:, 0:n], AF.Identity, scale=cw[:, t, 2:3], bias=cb[:, t:t + 1])
                    stt(y[:, 1:n], ps[:, 0:n - 1], cw[:, t, 1:2], y[:, 1:n], ALU.mult, ALU.add)
                    if n > 2:
                        stt(y[:, 2:n], ps[:, 0:n - 2], cw[:, t, 0:1], y[:, 2:n], ALU.mult, ALU.add)
                    stt(y[:, 0:1], carry[:, t, 1:2], cw[:, t, 1:2], y[:, 0:1], ALU.mult, ALU.add)
                    stt(y[:, 0:2], carry[:, t, 0:2], cw[:, t, 0:1], y[:, 0:2], ALU.mult, ALU.add)
                    cp("act", carry[:, t, 0:2], ps[:, n - 2:n])
                sgt = sg[j % 2]
                act(None, sgt[:, 0:n], ys[0][:, 0:n], AF.Silu)
                tt("pool", aT[s][:, j, 0:n], sgt[:, 0:n], ys[1][:, 0:n], ALU.mult)
            store(ATs[:, :, oc:oc + n], aT[s][:, :, 0:n])
        S.barrier()
        P.close()

    def phase_C2b():
        P = Pool_()
        WD = P.sb("WD", [128, 22, 1024], BF16)
        stg = [P.sb(f"stgD{i}", [128, 1024], F32) for i in range(2)]
        glf = P.sb("gl_s", [128, 8], F32)
        xs = [P.sb(f"xsD{i}", [128, 8, 512], F32) for i in range(2)]
        aT = [P.sb(f"aTD{i}", [128, 22, 512], BF16) for i in range(2)]
        x2 = P.sb("x2D", [128, 8, 512], F32)
        sq = P.sb("sqD", [128, 8, 512], BF16)
        lnv = P.sb("lnvD", [128, 512], F32)
        rs = P.sb("rsD", [128, 512], F32)
        ob = [P.sb(f"obD{i}", [128, 8, 512], F32) for i in range(2)]
        ssps = P.ps("ssD", [128, 512])
        PS = [P.ps(f"psD{i}", [128, 512]) for i in range(6)]

        load(glf[:], gl_d[:, :])
        load_w_bf16(P, WD, w_down, 22, 1024, None, stg, 1024)
        psi = [0]

        def nextps():
            psi[0] = (psi[0] + 1) % 6
            return PS[psi[0]]

        for gi_, (qs, n, oc) in enumerate(groups[1:]):
            s = gi_ % 2
            x = xs[s]
            load(x[:], X1s[:, :, oc:oc + n])
            load(aT[s][:], ATs[:, :, oc:oc + n])
            for o in range(8):
                ps = nextps()
                pe_group(ps[:], [(WD[:, k, o * 128:(o + 1) * 128], aT[s][:, k, :]) for k in range(22)])
                tt("dve", x2[:, o, :], ps[:], x[:, o, :], ALU.add)
            rms_stats(P, x2, 512, sq, ssps, lnv, rs)
            for o in range(8):
                stt(ob[s][:, o, :], x2[:, o, :], glf[:, o:o + 1], rs[:], ALU.mult, ALU.mult)
            store(out_d[:, :, oc - 2:oc - 2 + n].rearrange("k p n -> p k n"), ob[s][:])
        S.barrier()
        P.close()

    S.barrier()
    gtmp.close()
    ph = phases.split(",")
    if "A" in ph:
        phase_A()
    if "B1" in ph:
        phase_B1()
    if "B2" in ph:
        phase_B2()
    if "C1" in ph:
        phase_C1()
    if "C2a" in ph:
        phase_C2a()
    if "C2b" in ph:
        phase_C2b()
    gl.close()

    with ExitStack() as es:
        sems = {}
        for k in S.keys:
            nm = k if isinstance(k, str) else f"d{k[1]}"
            sems[k] = es.enter_context(nc.semaphore("s_" + nm))
        block = es.enter_context(nc.Block())

        @block.tensor
        def _(e):
            S.replay("pe", e, sems)

        @block.scalar
        def _(e):
            S.replay("act", e, sems)

        @block.vector
        def _(e):
            S.replay("dve", e, sems)

        @block.gpsimd
        def _(e):
            S.replay("pool", e, sems)

        @block.sync
        def _(e):
            S.replay("sp", e, sems)

    return nc


def _t5_bucket(dist):
    dist = np.asarray(dist, np.int32)
    max_exact = 16
    d = np.maximum(dist, 1).astype(np.float32)
    large = max_exact + (np.log(d / np.float32(max_exact)) / np.float32(np.log(128 / max_exact))
                         * np.float32(32 - max_exact)).astype(np.int32)
    large = np.minimum(large, 31)
    return np.where(dist < max_exact, dist, large)


def _consts():
    c = np.zeros((128, 2688), np.float32)
    c[:, 0:128] = 1.0
    j = np.arange(128)[:, None]
    s = np.arange(128)[None, :]
    c[:, 128:256] = (j >= s).astype(np.float32)
    c[:, 256:384] = np.eye(128, dtype=np.float32)
    q = np.arange(512)[None, :]
    for jj in range(4):
        valid = (128 * jj + j) < q
        c[:, 384 + 512 * jj:384 + 512 * (jj + 1)] = np.where(valid, 0.0, 8.0 * NEGM)
    c[:, 2432:2432 + 64] = 1.0
    c[:, 2432 + 128 + 64:2432 + 256] = 1.0
    return c


def prep_inputs(NH, x, g_mix, w_in, w_sb_proj, w_sw_proj, w_out, rel_bias, sinks,
                g_ffn, w_up, conv_w, conv_b, w_down, g_final):
    H = NH * 512
    B = x.shape[0]
    f = lambda a: np.ascontiguousarray(np.asarray(a, dtype=np.float32))
    x = f(x)
    lay8 = lambda g: f(np.asarray(g, np.float32).reshape(8, 128).T)
    rel_bias = np.asarray(rel_bias, np.float32)
    k = np.arange(128)[:, None]
    q = np.arange(128)[None, :]
    bm = np.zeros((128, 2, 2, 4, 128), np.float32)
    d0 = 128 + q - k
    d1 = q - k
    b0 = _t5_bucket(np.clip(d0, 0, 255))
    b1 = _t5_bucket(np.clip(d1, 0, 255))
    for h in range(8):
        bm[:, 0, h % 2, h // 2, :] = np.where(d0 <= 127, rel_bias[b0, h], NEGM)
        bm[:, 1, h % 2, h // 2, :] = np.where(d1 >= 0, rel_bias[b1, h], NEGM)
    sinks = np.asarray(sinks, np.float32)
    sk = np.zeros((128, 4), np.float32)
    for hp in range(4):
        sk[0:64, hp] = sinks[2 * hp]
        sk[64:128, hp] = sinks[2 * hp + 1]
    cw = np.asarray(conv_w, np.float32)
    cwl = np.ascontiguousarray(cw.reshape(3, 44, 128).transpose(2, 1, 0)).reshape(128, 132)
    cbl = f(np.asarray(conv_b, np.float32).reshape(44, 128).T)
    shared = {
        "w_in": f(w_in), "w_sbp": f(w_sb_proj), "w_swp": f(w_sw_proj), "w_out": f(w_out),
        "w_up": f(w_up), "w_down": f(w_down), "gm": lay8(g_mix), "gf": lay8(g_ffn), "gl": lay8(g_final),
        "cw": f(cwl), "cb": cbl, "bm": f(bm.reshape(128, 2048)), "sk": sk, "cst": _consts(),
    }
    in_maps = []
    for c in range(2 * B):
        b, p = c // 2, c % 2
        xl = np.zeros((2 * H, D), np.float32)
        kb = np.zeros((128, 2 * H // 128), np.float32)
        if p == 1:
            xl[:] = x[b]
        else:
            xl[H:] = x[b, :H]
            kb[:, :H // 128] = NEGM
        m = dict(shared)
        m["xT"] = np.ascontiguousarray(xl.T)
        m["kbias"] = kb
        in_maps.append(m)
    return in_maps


_NC_CACHE = {}


def run(NH, phases="A,B1,B2,C1,C2a,C2b", **inputs):
    x = np.asarray(inputs["x"])
    B = x.shape[0]
    H = NH * 512
    in_maps = prep_inputs(NH, **inputs)
    if (NH, phases) not in _NC_CACHE:
        _NC_CACHE[(NH, phases)] = build_nc(NH, phases)
    nc = _NC_CACHE[(NH, phases)]
    res = run_bass_kernel_spmd(nc, in_maps, core_ids=list(range(2 * B)))
    out = np.zeros((B, 2 * H, D), np.float32)
    for c in range(2 * B):
        b, p = c // 2, c % 2
        o = np.asarray(res.results[c]["out"]).reshape(D, H)
        out[b, p * H:(p + 1) * H, :] = o.T
    return out


def kernel(**inputs):
    return run(8, **inputs)
```

```python
import numpy as np
from contextlib import ExitStack
import concourse.bass as bass
import concourse.mybir as mybir
from concourse.bass_utils import run_bass_kernel_spmd

F32 = mybir.dt.float32
BF16 = mybir.dt.bfloat16
AF = mybir.ActivationFunctionType
ALU = mybir.AluOpType

D = 1024
KT = 8
DFF = 2816
NEGM = -100.0
ENGS = ("pe", "act", "dve", "pool", "sp")
NDMA = 24
NDMA_HW = 16


class Buf:
    __slots__ = ("w", "r", "name")

    def __init__(self, name=""):
        self.w = None
        self.r = {}
        self.name = name


class V:
    __slots__ = ("buf", "ap")

    def __init__(self, buf, ap):
        self.buf = buf
        self.ap = ap


class Tile:
    def __init__(self, handle, name=""):
        self.t = handle
        self.buf = Buf(name)

    def __getitem__(self, idx):
        return V(self.buf, self.t[idx])


class Sched:
    def __init__(self):
        self.keys = list(ENGS) + [("d", i) for i in range(NDMA)]
        self.cnt = {k: 0 for k in self.keys}
        self.seen = {e: {k: 0 for k in self.keys} for e in ENGS}
        self.q = {e: [] for e in ENGS}
        self.dma_rr = 0
        self.dma_rr2 = 0
        self.nops = 0

    def op(self, eng, fns, reads=(), writes=(), dma=False):
        if callable(fns):
            fns = [fns]
        need = {}

        def req(tok):
            if tok is None:
                return
            k, v = tok
            if need.get(k, 0) < v:
                need[k] = v

        for b in reads:
            req(b.w)
        for b in writes:
            req(b.w)
            for k, v in b.r.items():
                req((k, v))
        if dma:
            if eng == "sp":
                key = ("d", self.dma_rr)
                self.dma_rr = (self.dma_rr + 1) % NDMA_HW
            else:
                key = ("d", NDMA_HW + self.dma_rr2)
                self.dma_rr2 = (self.dma_rr2 + 1) % (NDMA - NDMA_HW)
            req((key, self.cnt[key]))
            inc = 16
        else:
            key = eng
            inc = 1
        waits = []
        seen = self.seen[eng]
        for k, v in need.items():
            if v <= 0 or seen[k] >= v:
                continue
            if k == eng and eng == "pe":
                continue
            seen[k] = v
            waits.append((k, v))
        self.cnt[key] += inc
        tok = (key, self.cnt[key])
        for b in writes:
            b.w = tok
            b.r = {}
        for b in reads:
            if b.r.get(key, 0) < tok[1]:
                b.r[key] = tok[1]
        self.q[eng].append((waits, fns, key, inc))
        self.nops += 1
        return tok

    def barrier(self):
        for e in ENGS:
            waits = []
            for k in self.keys:
                v = self.cnt[k]
                if v > 0 and self.seen[e][k] < v and k != e:
                    self.seen[e][k] = v
                    waits.append((k, v))
            if waits:
                self.q[e].append((waits, [], None, 0))

    def replay(self, eng, e, sems):
        for waits, fns, key, inc in self.q[eng]:
            for k, v in waits:
                e.wait_ge(sems[k], v)
            ins = None
            for f in fns:
                ins = f(e)
            if ins is not None and key is not None:
                ins.then_inc(sems[key], inc)


def build_nc(NH, phases="A,B1,B2,C1,C2a,C2b"):
    H = NH * 512
    L2 = 2 * H
    NB2 = L2 // 128
    NBH = H // 128
    NOWN = (NH + 1) * 512
    NY = 2 + NH * 512
    NSWB = NH * 4 + 1

    nc = bass.Bass("TRN2", target_bir_lowering=False)
    S = Sched()

    def din(name, shape):
        return nc.dram_tensor(name, shape, F32, kind="ExternalInput").ap()

    xT = din("xT", [D, L2])
    kbias_d = din("kbias", [128, NB2])
    w_in = din("w_in", [D, 4352])
    w_sbp = din("w_sbp", [512, D])
    w_swp = din("w_swp", [512, D])
    w_out = din("w_out", [D, D])
    w_up = din("w_up", [D, 2 * DFF])
    w_down = din("w_down", [DFF, D])
    gm_d = din("gm", [128, 8])
    gf_d = din("gf", [128, 8])
    gl_d = din("gl", [128, 8])
    cw_d = din("cw", [128, 44 * 3])
    cb_d = din("cb", [128, 44])
    bm_d = din("bm", [128, 2048])
    sk_d = din("sk", [128, 4])
    cst_d = din("cst", [128, 2688])
    out_d = nc.dram_tensor("out", [8, 128, H], F32, kind="ExternalOutput").ap()

    def dscr(name, shape, dt):
        return nc.dram_tensor(name, shape, dt, kind="Internal").ap()

    KTs = dscr("KTs", [4, 128, L2], BF16)
    QTs = dscr("QTs", [4, 128, NOWN], BF16)
    VSs = dscr("VSs", [4, 128, NB2, 256], BF16)
    QWs = dscr("QWs", [4, 128, NOWN], BF16)
    KWs = dscr("KWs", [2, 128, NOWN], BF16)
    VWs = dscr("VWs", [128, (NH + 1) * 4, 512], BF16)
    YSB = dscr("YSB", [128, 4, NY], BF16)
    YSW = dscr("YSW", [128, 4, NSWB * 128], BF16)
    X1s = dscr("X1s", [128, 8, NY], F32)
    ATs = dscr("ATs", [128, 22, NY], BF16)

    groups = [(H - 2, 2, 0)] + [(H + 512 * c, 512, 2 + 512 * c) for c in range(NH)]

    def mm(out, lhsT, rhs, start, stop):
        return lambda e: e.matmul(out.ap, lhsT.ap, rhs.ap, start=start, stop=stop)

    def pe_group(out, pairs, extra_reads=()):
        fns = []
        n = len(pairs)
        rd = []
        for i, (l, r) in enumerate(pairs):
            fns.append(mm(out, l, r, i == 0, i == n - 1))
            rd += [l.buf, r.buf]
        S.op("pe", fns, reads=rd + list(extra_reads), writes=[out.buf])

    def act(eng_unused, out, in_, func, scale=1.0, bias=0.0, extra_reads=()):
        def f(e):
            kw = {}
            if isinstance(bias, V):
                kw["bias"] = bias.ap
            else:
                kw["bias"] = float(bias)
            if isinstance(scale, V):
                kw["scale"] = scale.ap
            else:
                kw["scale"] = float(scale)
            return e.activation(out.ap, in_.ap, func, **kw)

        rd = [in_.buf] + [x.buf for x in (scale, bias) if isinstance(x, V)] + list(extra_reads)
        S.op("act", f, reads=rd, writes=[out.buf])

    def tt(eng, out, a, b, op):
        S.op(eng, lambda e: e.tensor_tensor(out.ap, a.ap, b.ap, op), reads=[a.buf, b.buf], writes=[out.buf])

    def ts(eng, out, a, s1, op0, s2=None, op1=None):
        def f(e):
            sc1 = s1.ap if isinstance(s1, V) else float(s1)
            if s2 is None:
                return e.tensor_scalar(out.ap, a.ap, sc1, None, op0)
            sc2 = s2.ap if isinstance(s2, V) else float(s2)
            return e.tensor_scalar(out.ap, a.ap, sc1, sc2, op0, op1)

        rd = [a.buf] + [x.buf for x in (s1, s2) if isinstance(x, V)]
        S.op(eng, f, reads=rd, writes=[out.buf])

    def stt(out, a, sc, b, op0, op1):
        def f(e):
            s = sc.ap if isinstance(sc, V) else float(sc)
            return e.scalar_tensor_tensor(out.ap, a.ap, s, b.ap, op0, op1)

        rd = [a.buf, b.buf] + ([sc.buf] if isinstance(sc, V) else [])
        S.op("dve", f, reads=rd, writes=[out.buf])

    def cp(eng, out, a):
        if eng == "act":
            act(None, out, a, AF.Copy)
        else:
            S.op(eng, lambda e: e.tensor_copy(out.ap, a.ap), reads=[a.buf], writes=[out.buf])

    def memset(eng, out, val):
        S.op(eng, lambda e: e.memset(out.ap, val), writes=[out.buf])

    def load(out, src_ap, eng="sp"):
        S.op(eng, lambda e: e.dma_start(out=out.ap, in_=src_ap), writes=[out.buf], dma=True)

    def store(dst_ap, src, eng="pool"):
        S.op(eng, lambda e: e.dma_start(out=dst_ap, in_=src.ap), reads=[src.buf], dma=True)

    class Pool_:
        def __init__(self):
            self.es = ExitStack()

        def sb(self, name, shape, dt):
            return Tile(self.es.enter_context(nc.sbuf_tensor(name, shape, dt)), name)

        def ps(self, name, shape):
            return Tile(self.es.enter_context(nc.psum_tensor(name, shape, F32)), name)

        def close(self):
            self.es.close()

    gl = Pool_()
    ones = gl.sb("ones", [128, 128], BF16)
    tri = gl.sb("tri", [128, 128], BF16)
    ident = gl.sb("ident", [128, 128], BF16)
    dmask = gl.sb("dmask", [128, 4, 512], BF16)
    opad = gl.sb("opad", [128, 2, 128], BF16)
    kbias = gl.sb("kbias_s", [128, NB2], F32)
    gtmp = Pool_()
    cstf = gtmp.sb("cstf", [128, 2688], F32)

    load(cstf[:], cst_d[:, :])
    load(kbias[:], kbias_d[:, :])
    cp("dve", ones[:], cstf[:, 0:128])
    cp("dve", tri[:], cstf[:, 128:256])
    cp("dve", ident[:], cstf[:, 256:384])
    for j in range(4):
        cp("dve", dmask[:, j, :], cstf[:, 384 + 512 * j:384 + 512 * (j + 1)])
    for j in range(2):
        cp("dve", opad[:, j, :], cstf[:, 2432 + 128 * j:2432 + 128 * (j + 1)])

    def rms_stats(P, xs, n, sq, ssps, lnv, rs):
        act(None, sq[:, :, 0:n], xs[:, :, 0:n], AF.Square)
        pe_group(ssps[:, 0:n], [(ones[:], sq[:, k, 0:n]) for k in range(8)])
        act(None, lnv[:, 0:n], ssps[:, 0:n], AF.Ln, scale=1.0 / D, bias=1e-6)
        act(None, rs[:, 0:n], lnv[:, 0:n], AF.Exp, scale=-0.5)

    def phase_A():
        P = Pool_()
        WA = P.sb("WA", [128, 8, 3328], BF16)
        stg = [P.sb(f"wstg{i}", [128, 2304], F32) for i in range(2)]
        gm = P.sb("gm_s", [128, 8], F32)
        xs = [P.sb(f"xsA{i}", [128, 8, 512], F32) for i in range(2)]
        sq = P.sb("sqA", [128, 8, 512], BF16)
        hT = [P.sb(f"hTA{i}", [128, 8, 512], BF16) for i in range(2)]
        lnv = P.sb("lnvA", [128, 512], F32)
        rs = [P.sb(f"rsA{i}", [128, 512], F32) for i in range(2)]
        stF = [P.sb(f"stF{i}", [128, 14, 512], BF16) for i in range(2)]
        stV = [P.sb(f"stV{i}", [128, 4, 1536], BF16) for i in range(2)]
        ssps = P.ps("ssA", [128, 512])
        PS = [P.ps(f"psA{i}", [128, 512]) for i in range(6)]

        load(gm[:], gm_d[:, :])
        memset("pool", WA[:, :, 1792:3328], 0.0)
        for kt in range(8):
            st = stg[kt % 2]
            load(st[:], w_in[kt * 128:(kt + 1) * 128, 0:2304])
            g = gm[:, kt:kt + 1]
            e1 = "dve"
            act(None, WA[:, kt, 0:1024], st[:, 0:1024], AF.Identity, scale=g)
            ts(e1, WA[:, kt, 1024:1536], st[:, 1536:2048], g, ALU.mult)
            for gg in range(2):
                for r in range(2):
                    ts(e1, WA[:, kt, 1536 + 128 * gg + 64 * r:1536 + 128 * gg + 64 * (r + 1)],
                       st[:, 2048 + 64 * gg:2048 + 64 * (gg + 1)], g, ALU.mult)
            for h in range(8):
                o = 1792 + 128 * h + 64 * (h % 2)
                ts(e1, WA[:, kt, o:o + 64], st[:, 1024 + 64 * h:1024 + 64 * (h + 1)], g, ALU.mult)
            for i in range(4):
                o = 2816 + 128 * i + 64 * (i % 2)
                gg = i // 2
                ts(e1, WA[:, kt, o:o + 64], st[:, 2176 + 64 * gg:2176 + 64 * (gg + 1)], g, ALU.mult)

        psi = [0]

        def nextps():
            psi[0] = (psi[0] + 1) % 6
            return PS[psi[0]]

        evi = [0]

        def evac(out, in_):
            evi[0] += 1
            cp("act" if evi[0] % 2 == 0 else "dve", out, in_)

        def stage1(tc):
            s = tc % 2
            load(xs[s][:], xT[:, tc * 512:(tc + 1) * 512].rearrange("(k p) n -> p k n", p=128))
            rms_stats(P, xs[s], 512, sq, ssps, lnv, rs[s])
            for k in range(8):
                tt("dve" if k % 2 == 0 else "pool", hT[s][:, k, :], xs[s][:, k, :], rs[s][:], ALU.mult)

        def stage2(tc):
            s = tc % 2
            own = tc >= NH - 1
            h = hT[s]
            sf = stF[s]
            sv = stV[s]

            def fm(slot, col0):
                ps = nextps()
                pe_group(ps[:], [(WA[:, k, col0:col0 + 128], h[:, k, :]) for k in range(8)])
                evac(sf[:, slot, :], ps[:])

            for hp in range(4):
                fm(hp, 512 + 128 * hp)
            if own:
                for hp in range(4):
                    fm(4 + hp, 128 * hp)
                for hp in range(4):
                    fm(8 + hp, 1024 + 128 * hp)
                for gg in range(2):
                    fm(12 + gg, 1536 + 128 * gg)
            for blk in range(4):
                for half in range(2):
                    ps = nextps()
                    pe_group(ps[:], [(h[:, k, blk * 128:(blk + 1) * 128],
                                      WA[:, k, 1792 + 512 * half:1792 + 512 * (half + 1)]) for k in range(8)])
                    evac(sv[:, blk, 512 * half:512 * (half + 1)], ps[:])
                if own:
                    ps = nextps()
                    pe_group(ps[:], [(h[:, k, blk * 128:(blk + 1) * 128], WA[:, k, 2816:3328]) for k in range(8)])
                    evac(sv[:, blk, 1024:1536], ps[:])
            c0 = tc * 512
            store(KTs[:, :, c0:c0 + 512].rearrange("h p n -> p h n"), sf[:, 0:4, :])
            for hp in range(4):
                store(VSs[hp, :, 4 * tc:4 * tc + 4, :], sv[:, :, 256 * hp:256 * (hp + 1)])
            if own:
                o0 = (tc - (NH - 1)) * 512
                store(QTs[:, :, o0:o0 + 512].rearrange("h p n -> p h n"), sf[:, 4:8, :])
                store(QWs[:, :, o0:o0 + 512].rearrange("h p n -> p h n"), sf[:, 8:12, :])
                store(KWs[:, :, o0:o0 + 512].rearrange("h p n -> p h n"), sf[:, 12:14, :])
                b0 = (tc - (NH - 1)) * 4
                store(VWs[:, b0:b0 + 4, :], sv[:, :, 1024:1536])

        NT = 2 * NH
        stage1(0)
        for tc in range(NT):
            if tc + 1 < NT:
                stage1(tc + 1)
            stage2(tc)
        S.barrier()
        P.close()

    def phase_B1():
        P = Pool_()
        QW = P.sb("QW", [128, 4, NOWN], BF16)
        KW = P.sb("KW", [128, 2, NOWN], BF16)
        VW = P.sb("VW", [128, (NH + 1) * 4, 512], BF16)
        BM = P.sb("BM", [128, 2048], F32)
        sk = P.sb("sk_s", [128, 4], F32)
        esk = P.sb("esk", [128, 4], F32)
        ESB = P.sb("ESB", [128, 4, 128], F32)
        zer = P.sb("zerB1", [128, 128], F32)
        LG = [P.sb(f"LG{i}", [128, 2048], F32) for i in range(2)]
        PB = [P.sb(f"PB{i}", [128, 2, 2, 4, 128], BF16) for i in range(2)]
        den = P.sb("den", [128, 512], F32)
        rec = P.sb("rec", [128, 512], F32)
        yst = [P.sb(f"yst{i}", [128, 4, 128], BF16) for i in range(2)]
        ZS = P.ps("ZS", [128, 2, 2, 4, 128])
        OP = P.ps("OPs", [128, 4, 128])
        DN = P.ps("DNs", [128, 4, 128])

        load(QW[:], QWs.rearrange("h p n -> p h n"))
        load(KW[:], KWs.rearrange("h p n -> p h n"))
        load(VW[:], VWs[:, :, :])
        load(BM[:], bm_d[:, :])
        load(sk[:], sk_d[:, :])
        act(None, esk[:], sk[:], AF.Exp)
        memset("dve", zer[:], 0.0)
        for hp in range(4):
            ts("dve", ESB[:, hp, :], zer[:], esk[:, hp:hp + 1], ALU.add)

        for qi, i in enumerate(range(3, (NH + 1) * 4)):
            s = qi % 2
            qcols = slice(i * 128, (i + 1) * 128)
            for kbsel in range(2):
                ib = i - 1 + kbsel
                for h in range(8):
                    po = (h % 2) * 64
                    g = h // 4
                    hp = h // 2
                    S.op("pe", mm(ZS[:, kbsel, h % 2, hp, :], KW[po:po + 64, g, ib * 128:(ib + 1) * 128],
                                  QW[po:po + 64, hp, qcols], True, True),
                         reads=[KW.buf, QW.buf], writes=[ZS.buf])
            stt(LG[s][:], V(ZS.buf, ZS.t[:].rearrange("p a r h q -> p (a r h q)")), 0.125, BM[:], ALU.mult, ALU.add)
            for kbsel in range(2):
                lb = (NBH - 4) + i - 1 + kbsel
                act(None, V(PB[s].buf, PB[s].t[:, kbsel, :, :, :].rearrange("p r h q -> p (r h q)")),
                    LG[s][:, kbsel * 1024:(kbsel + 1) * 1024], AF.Exp, bias=kbias[:, lb:lb + 1])
            for hp in range(4):
                g = hp // 2
                prs = []
                prd = []
                for kbsel in range(2):
                    ib = i - 1 + kbsel
                    for r in range(2):
                        h = 2 * hp + r
                        var = 2 * g + r
                        prs.append((VW[:, ib, 128 * var:128 * (var + 1)], PB[s][:, kbsel, r, hp, :]))
                        prd.append((opad[:, r, :], PB[s][:, kbsel, r, hp, :]))
                pe_group(OP[:, hp, :], prs)
                pe_group(DN[:, hp, :], prd)
            tt("dve", den[:], V(DN.buf, DN.t[:].rearrange("p a q -> p (a q)")),
               V(ESB.buf, ESB.t[:].rearrange("p a q -> p (a q)")), ALU.add)
            S.op("dve", lambda e: e.reciprocal(rec.t[:], den.t[:]), reads=[den.buf], writes=[rec.buf])
            tt("dve", V(yst[s].buf, yst[s].t[:].rearrange("p a q -> p (a q)")),
               V(OP.buf, OP.t[:].rearrange("p a q -> p (a q)")), rec[:], ALU.mult)
            store(YSW[:, :, qi * 128:(qi + 1) * 128], yst[s][:])
        S.barrier()
        P.close()

    def phase_B2():
        P = Pool_()
        KTt = [P.sb(f"KTt{i}", [128, L2], BF16) for i in range(2)]
        Vt = [P.sb(f"Vt{i}", [128, NB2, 256], BF16) for i in range(2)]
        QTt = [P.sb(f"QTt{i}", [128, NOWN], BF16) for i in range(2)]
        NBUF = 4
        Eb = [P.sb(f"Eb{i}", [128, 2, 512], BF16) for i in range(NBUF)]
        Lb = [P.sb(f"Lb{i}", [128, 2, 512], BF16) for i in range(NBUF)]
        Gb = [P.sb(f"Gb{i}", [128, 2, 512], BF16) for i in range(NBUF)]
        Wb = [P.sb(f"Wb{i}", [128, 2, 512], BF16) for i in range(NBUF)]
        triC = P.sb("triC", [128, 128], BF16)
        yev = [P.sb(f"yev{i}", [128, 512], BF16) for i in range(2)]
        Z = [P.ps(f"Zp{i}", [128, 2, 512]) for i in range(2)]
        X = P.ps("Xp", [128, 2, 512])
        Y = [P.ps(f"Yp{i}", [128, 512]) for i in range(2)]
        tt("dve", triC[:], ones[:], tri[:], ALU.subtract)

        def load_hp(hp):
            s = hp % 2
            hl = L2 // 2
            load(KTt[s][:, 0:hl], KTs[hp, :, 0:hl])
            load(KTt[s][:, hl:L2], KTs[hp, :, hl:L2])
            load(Vt[s][:, 0:NB2 // 2, :], VSs[hp, :, 0:NB2 // 2, :])
            load(Vt[s][:, NB2 // 2:NB2, :], VSs[hp, :, NB2 // 2:NB2, :])
            load(QTt[s][:], QTs[hp, :, :])

        gi = [0]
        load_hp(0)
        for hp in range(4):
            s = hp % 2
            if hp + 1 < 4:
                load_hp(hp + 1)
            Kt, Vv, Qt = KTt[s], Vt[s], QTt[s]
            for (qs, n, ycol) in groups:
                qc = qs - (NH - 1) * 512
                kbmax = (qs + n - 2) // 128
                kbs = list(range(kbmax, -1, -1))
                U = len(kbs)
                Yp = Y[gi[0] % 2]
                ye = yev[gi[0] % 2]
                gi[0] += 1

                def s_Z(u):
                    kb = kbs[u]
                    j = kb - qs // 128
                    Zt = Z[u % 2]
                    for r in range(2):
                        po = 64 * r
                        prs = [(Kt[po:po + 64, kb * 128:(kb + 1) * 128], Qt[po:po + 64, qc:qc + n])]
                        if j >= 0:
                            if n == 512:
                                prs.append((ident[:], dmask[:, j, :]))
                            else:
                                prs.append((ident[:], dmask[:, 0, 126:128]))
                        pe_group(Zt[:, r, 0:n], prs)

                def s_EL(u):
                    kb = kbs[u]
                    b = u % NBUF
                    act(None, Eb[b][:, :, 0:n], Z[u % 2][:, :, 0:n], AF.Exp, scale=0.125, bias=kbias[:, kb:kb + 1])
                    act(None, Lb[b][:, :, 0:n], Eb[b][:, :, 0:n], AF.Ln, bias=1.0)

                def mmx(out, lhsT, rhs, start):
                    return lambda e: e.matmul(out.ap, lhsT.ap, rhs.ap, start=start, stop=True,
                                              skip_group_check=True)

                def s_A(u):
                    b = u % NBUF
                    fns = [mmx(X[:, r, 0:n], tri[:], Lb[b][:, r, 0:n], u == 0) for r in range(2)]
                    S.op("pe", fns, reads=[tri.buf, Lb[b].buf], writes=[X.buf])

                def s_B(u):
                    b = u % NBUF
                    fns = [mmx(X[:, r, 0:n], triC[:], Lb[b][:, r, 0:n], False) for r in range(2)]
                    S.op("pe", fns, reads=[triC.buf, Lb[b].buf], writes=[X.buf])

                def s_G(u):
                    b = u % NBUF
                    act(None, Gb[b][:, :, 0:n], X[:, :, 0:n], AF.Exp, scale=-1.0)
                    tt("dve", Wb[b][:, :, 0:n], Eb[b][:, :, 0:n], Gb[b][:, :, 0:n], ALU.mult)

                def s_Y(u):
                    kb = kbs[u]
                    b = u % NBUF
                    fns = []
                    for r in range(2):
                        fns.append(mm(Yp[:, 0:n], Vv[:, kb, 128 * r:128 * (r + 1)], Wb[b][:, r, 0:n],
                                      (u == 0 and r == 0), (u == U - 1 and r == 1)))
                    S.op("pe", fns, reads=[Vv.buf, Wb[b].buf], writes=[Yp.buf])

                s_Z(0)
                if U > 1:
                    s_Z(1)
                s_EL(0)
                if U > 2:
                    s_Z(2)
                s_A(0)
                for t in range(U):
                    if t + 1 < U:
                        s_EL(t + 1)
                    s_G(t)
                    if t + 1 < U:
                        s_B(t)
                        s_A(t + 1)
                    if t + 3 < U:
                        s_Z(t + 3)
                    if t >= 1:
                        s_Y(t - 1)
                s_Y(U - 1)
                cp("dve", ye[:, 0:n], Yp[:, 0:n])
                store(YSB[:, hp, ycol:ycol + n], ye[:, 0:n])
        S.barrier()
        P.close()

    def load_w_bf16(P, W, src, nk, ncol, gvec, stg, piece):
        i = 0
        for k in range(nk):
            for c0 in range(0, ncol, piece):
                c1 = min(ncol, c0 + piece)
                st = stg[i % len(stg)]
                eng = "dve" if i % 2 == 0 else "act"
                i += 1
                load(st[:, 0:c1 - c0], src[k * 128:(k + 1) * 128, c0:c1])
                if gvec is None:
                    cp(eng, W[:, k, c0:c1], st[:, 0:c1 - c0])
                elif eng == "act":
                    act(None, W[:, k, c0:c1], st[:, 0:c1 - c0], AF.Identity, scale=gvec[:, k:k + 1])
                else:
                    ts(eng, W[:, k, c0:c1], st[:, 0:c1 - c0], gvec[:, k:k + 1], ALU.mult)

    def phase_C1():
        P = Pool_()
        WG = P.sb("WG", [128, 8, 2048], BF16)
        WSB = P.sb("WSB", [128, 4, 1024], BF16)
        WSW = P.sb("WSW", [128, 4, 1024], BF16)
        WO = P.sb("WO", [128, 8, 1024], BF16)
        stg = [P.sb(f"stgC1{i}", [128, 1024], F32) for i in range(4)]
        gm = P.sb("gm_c1", [128, 8], F32)
        xs = [P.sb(f"xsC{i}", [128, 8, 512], F32) for i in range(2)]
        sq = P.sb("sqC", [128, 8, 512], BF16)
        hT = [P.sb(f"hTC{i}", [128, 8, 512], BF16) for i in range(2)]
        lnv = P.sb("lnvC", [128, 512], F32)
        rs = P.sb("rsC", [128, 512], F32)
        ysb = [P.sb(f"ysbC{i}", [128, 4, 512], BF16) for i in range(2)]
        ysw = [P.sb(f"yswC{i}", [128, 4, 512], BF16) for i in range(2)]
        g1 = [P.sb(f"g1C{i}", [128, 512], F32) for i in range(2)]
        g2 = [P.sb(f"g2C{i}", [128, 512], F32) for i in range(2)]
        t1 = [P.sb(f"t1C{i}", [128, 512], F32) for i in range(2)]
        t2 = [P.sb(f"t2C{i}", [128, 512], F32) for i in range(2)]
        mT = P.sb("mT", [128, 8, 512], BF16)
        x1 = P.sb("x1C", [128, 8, 512], F32)
        ssps = P.ps("ssC", [128, 512])
        PS = [P.ps(f"psC{i}", [128, 512]) for i in range(7)]

        load(gm[:], gm_d[:, :])
        load_w_bf16(P, WG, w_in[:, 2304:4352], 8, 2048, gm, stg, 1024)
        load_w_bf16(P, WSB, w_sbp, 4, 1024, None, stg, 1024)
        load_w_bf16(P, WSW, w_swp, 4, 1024, None, stg, 1024)
        load_w_bf16(P, WO, w_out, 8, 1024, None, stg, 1024)
        psi = [0]

        def nextps():
            psi[0] = (psi[0] + 1) % 7
            return PS[psi[0]]

        def prologue(gi_):
            qs, n, oc = groups[gi_]
            s = gi_ % 2
            x = xs[s]
            load(x[:, :, 0:n], xT[:, qs:qs + n].rearrange("(k p) n -> p k n", p=128))
            load(ysb[s][:, :, 0:n], YSB[:, :, oc:oc + n])
            swc = qs - (H - 128)
            load(ysw[s][:, :, 0:n], YSW[:, :, swc:swc + n])
            rms_stats(P, x, n, sq, ssps, lnv, rs)
            for k in range(8):
                tt("dve" if k % 2 == 0 else "pool", hT[s][:, k, 0:n], x[:, k, 0:n], rs[:, 0:n], ALU.mult)

        def main(gi_):
            qs, n, oc = groups[gi_]
            s = gi_ % 2
            x = xs[s]
            h = hT[s]
            for o in range(8):
                pg1 = nextps()
                pe_group(pg1[:, 0:n], [(WG[:, k, o * 128:(o + 1) * 128], h[:, k, 0:n]) for k in range(8)])
                act(None, g1[o % 2][:, 0:n], pg1[:, 0:n], AF.Sigmoid)
                pg2 = nextps()
                pe_group(pg2[:, 0:n], [(WG[:, k, (8 + o) * 128:(9 + o) * 128], h[:, k, 0:n]) for k in range(8)])
                act(None, g2[o % 2][:, 0:n], pg2[:, 0:n], AF.Sigmoid)
                pa = nextps()
                pe_group(pa[:, 0:n], [(WSB[:, k, o * 128:(o + 1) * 128], ysb[s][:, k, 0:n]) for k in range(4)])
                pb = nextps()
                pe_group(pb[:, 0:n], [(WSW[:, k, o * 128:(o + 1) * 128], ysw[s][:, k, 0:n]) for k in range(4)])
                tt("dve", t1[o % 2][:, 0:n], pa[:, 0:n], g1[o % 2][:, 0:n], ALU.mult)
                tt("dve", t2[o % 2][:, 0:n], pb[:, 0:n], g2[o % 2][:, 0:n], ALU.mult)
                tt("pool", mT[:, o, 0:n], t1[o % 2][:, 0:n], t2[o % 2][:, 0:n], ALU.add)
            for o in range(8):
                ps = nextps()
                pe_group(ps[:, 0:n], [(WO[:, k, o * 128:(o + 1) * 128], mT[:, k, 0:n]) for k in range(8)])
                tt("dve", x1[:, o, 0:n], ps[:, 0:n], x[:, o, 0:n], ALU.add)
            store(X1s[:, :, oc:oc + n], x1[:, :, 0:n])

        prologue(0)
        for gi_ in range(len(groups)):
            if gi_ + 1 < len(groups):
                prologue(gi_ + 1)
            main(gi_)
        S.barrier()
        P.close()

    def phase_C2a():
        P = Pool_()
        WU = P.sb("WU", [128, 8, 2 * DFF], BF16)
        stg = [P.sb(f"stgU{i}", [128, 1408], F32) for i in range(4)]
        gf = P.sb("gf_s", [128, 8], F32)
        cw = P.sb("cw_s", [128, 44, 3], F32)
        cb = P.sb("cb_s", [128, 44], F32)
        xs = P.sb("xsU", [128, 8, 512], F32)
        sq = P.sb("sqU", [128, 8, 512], BF16)
        hT = [P.sb(f"hTU{i}", [128, 8, 512], BF16) for i in range(2)]
        lnv = P.sb("lnvU", [128, 512], F32)
        rs = P.sb("rsU", [128, 512], F32)
        yb = [P.sb(f"ybU{i}", [128, 512], F32) for i in range(4)]
        sg = [P.sb(f"sgU{i}", [128, 512], F32) for i in range(2)]
        aT = P.sb("aTU", [128, 22, 512], BF16)
        ssps = P.ps("ssU", [128, 512])
        PS = [P.ps(f"psU{i}", [128, 512]) for i in range(7)]

        load(gf[:], gf_d[:, :])
        load(cw[:], cw_d.rearrange("p (j k) -> p j k", k=3))
        load(cb[:], cb_d[:, :])
        load_w_bf16(P, WU, w_up, 8, 2 * DFF, gf, stg, 1408)
        psi = [0]

        def nextps():
            psi[0] = (psi[0] + 1) % 7
            return PS[psi[0]]

        GW = 510
        NG = (H + GW - 1) // GW

        def geo(g):
            a0 = g * GW
            w = min(GW, H - a0)
            return a0, w, w + 2

        def prologue(g):
            a0, w, n = geo(g)
            load(xs[:, :, 0:n], X1s[:, :, a0:a0 + n])
            rms_stats(P, xs, n, sq, ssps, lnv, rs)
            for k in range(8):
                tt("dve" if k % 2 == 0 else "pool", hT[g % 2][:, k, 0:n], xs[:, k, 0:n], rs[:, 0:n], ALU.mult)

        ybi = [0]

        def tiles(g):
            a0, w, n = geo(g)
            h = hT[g % 2]
            for j in range(22):
                ys = []
                for t in (j, 22 + j):
                    ps = nextps()
                    pe_group(ps[:, 0:n], [(WU[:, k, t * 128:(t + 1) * 128], h[:, k, 0:n]) for k in range(8)])
                    y = yb[ybi[0] % 4]
                    ybi[0] += 1
                    ys.append(y)
                    act(None, y[:, 0:w], ps[:, 2:n], AF.Identity, scale=cw[:, t, 2:3], bias=cb[:, t:t + 1])
                    stt(y[:, 0:w], ps[:, 1:n - 1], cw[:, t, 1:2], y[:, 0:w], ALU.mult, ALU.add)
                    stt(y[:, 0:w], ps[:, 0:w], cw[:, t, 0:1], y[:, 0:w], ALU.mult, ALU.add)
                sgt = sg[j % 2]
                act(None, sgt[:, 0:w], ys[0][:, 0:w], AF.Silu)
                tt("pool", aT[:, j, 0:w], sgt[:, 0:w], ys[1][:, 0:w], ALU.mult)
            store(ATs[:, :, 2 + a0:2 + a0 + w], aT[:, :, 0:w])

        prologue(0)
        for g in range(NG):
            if g + 1 < NG:
                prologue(g + 1)
            tiles(g)
        S.barrier()
        P.close()

    def phase_C2b():
        P = Pool_()
        WD = P.sb("WD", [128, 22, 1024], BF16)
        stg = [P.sb(f"stgD{i}", [128, 1024], F32) for i in range(4)]
        glf = P.sb("gl_s", [128, 8], F32)
        xs = [P.sb(f"xsD{i}", [128, 8, 512], F32) for i in range(2)]
        aT = [P.sb(f"aTD{i}", [128, 22, 512], BF16) for i in range(2)]
        x2 = P.sb("x2D", [128, 8, 512], F32)
        sq = P.sb("sqD", [128, 8, 512], BF16)
        lnv = P.sb("lnvD", [128, 512], F32)
        rs = P.sb("rsD", [128, 512], F32)
        ob = [P.sb(f"obD{i}", [128, 8, 512], F32) for i in range(2)]
        ssps = P.ps("ssD", [128, 512])
        PS = [P.ps(f"psD{i}", [128, 512]) for i in range(6)]

        load(glf[:], gl_d[:, :])
        load_w_bf16(P, WD, w_down, 22, 1024, None, stg, 1024)
        psi = [0]

        def nextps():
            psi[0] = (psi[0] + 1) % 6
            return PS[psi[0]]

        for gi_, (qs, n, oc) in enumerate(groups[1:]):
            s = gi_ % 2
            x = xs[s]
            load(x[:], X1s[:, :, oc:oc + n])
            load(aT[s][:], ATs[:, :, oc:oc + n])
            for o in range(8):
                ps = nextps()
                pe_group(ps[:], [(WD[:, k, o * 128:(o + 1) * 128], aT[s][:, k, :]) for k in range(22)])
                tt("dve", x2[:, o, :], ps[:], x[:, o, :], ALU.add)
            rms_stats(P, x2, 512, sq, ssps, lnv, rs)
            for o in range(8):
                stt(ob[s][:, o, :], x2[:, o, :], glf[:, o:o + 1], rs[:], ALU.mult, ALU.mult)
            store(out_d[:, :, oc - 2:oc - 2 + n].rearrange("k p n -> p k n"), ob[s][:])
        S.barrier()
        P.close()

    S.barrier()
    gtmp.close()
    ph = phases.split(",")
    if "A" in ph:
        phase_A()
    if "B1" in ph:
        phase_B1()
    if "B2" in ph:
        phase_B2()
    if "C1" in ph:
        phase_C1()
    if "C2a" in ph:
        phase_C2a()
    if "C2b" in ph:
        phase_C2b()
    gl.close()

    with ExitStack() as es:
        sems = {}
        for k in S.keys:
            nm = k if isinstance(k, str) else f"d{k[1]}"
            sems[k] = es.enter_context(nc.semaphore("s_" + nm))
        block = es.enter_context(nc.Block())

        @block.tensor
        def _(e):
            S.replay("pe", e, sems)

        @block.scalar
        def _(e):
            S.replay("act", e, sems)

        @block.vector
        def _(e):
            S.replay("dve", e, sems)

        @block.gpsimd
        def _(e):
            S.replay("pool", e, sems)

        @block.sync
        def _(e):
            S.replay("sp", e, sems)

    return nc


def _t5_bucket(dist):
    dist = np.asarray(dist, np.int32)
    max_exact = 16
    d = np.maximum(dist, 1).astype(np.float32)
    large = max_exact + (np.log(d / np.float32(max_exact)) / np.float32(np.log(128 / max_exact))
                         * np.float32(32 - max_exact)).astype(np.int32)
    large = np.minimum(large, 31)
    return np.where(dist < max_exact, dist, large)


def _consts():
    c = np.zeros((128, 2688), np.float32)
    c[:, 0:128] = 1.0
    j = np.arange(128)[:, None]
    s = np.arange(128)[None, :]
    c[:, 128:256] = (j >= s).astype(np.float32)
    c[:, 256:384] = np.eye(128, dtype=np.float32)
    q = np.arange(512)[None, :]
    for jj in range(4):
        valid = (128 * jj + j) < q
        c[:, 384 + 512 * jj:384 + 512 * (jj + 1)] = np.where(valid, 0.0, 8.0 * NEGM)
    c[:, 2432:2432 + 64] = 1.0
    c[:, 2432 + 128 + 64:2432 + 256] = 1.0
    return c


def prep_inputs(NH, x, g_mix, w_in, w_sb_proj, w_sw_proj, w_out, rel_bias, sinks,
                g_ffn, w_up, conv_w, conv_b, w_down, g_final):
    H = NH * 512
    B = x.shape[0]
    f = lambda a: np.ascontiguousarray(np.asarray(a, dtype=np.float32))
    x = f(x)
    lay8 = lambda g: f(np.asarray(g, np.float32).reshape(8, 128).T)
    rel_bias = np.asarray(rel_bias, np.float32)
    k = np.arange(128)[:, None]
    q = np.arange(128)[None, :]
    bm = np.zeros((128, 2, 2, 4, 128), np.float32)
    d0 = 128 + q - k
    d1 = q - k
    b0 = _t5_bucket(np.clip(d0, 0, 255))
    b1 = _t5_bucket(np.clip(d1, 0, 255))
    for h in range(8):
        bm[:, 0, h % 2, h // 2, :] = np.where(d0 <= 127, rel_bias[b0, h], NEGM)
        bm[:, 1, h % 2, h // 2, :] = np.where(d1 >= 0, rel_bias[b1, h], NEGM)
    sinks = np.asarray(sinks, np.float32)
    sk = np.zeros((128, 4), np.float32)
    for hp in range(4):
        sk[0:64, hp] = sinks[2 * hp]
        sk[64:128, hp] = sinks[2 * hp + 1]
    cw = np.asarray(conv_w, np.float32)
    cwl = np.ascontiguousarray(cw.reshape(3, 44, 128).transpose(2, 1, 0)).reshape(128, 132)
    cbl = f(np.asarray(conv_b, np.float32).reshape(44, 128).T)
    shared = {
        "w_in": f(w_in), "w_sbp": f(w_sb_proj), "w_swp": f(w_sw_proj), "w_out": f(w_out),
        "w_up": f(w_up), "w_down": f(w_down), "gm": lay8(g_mix), "gf": lay8(g_ffn), "gl": lay8(g_final),
        "cw": f(cwl), "cb": cbl, "bm": f(bm.reshape(128, 2048)), "sk": sk, "cst": _consts(),
    }
    in_maps = []
    for c in range(2 * B):
        b, p = c // 2, c % 2
        xl = np.zeros((2 * H, D), np.float32)
        kb = np.zeros((128, 2 * H // 128), np.float32)
        if p == 1:
            xl[:] = x[b]
        else:
            xl[H:] = x[b, :H]
            kb[:, :H // 128] = NEGM
        m = dict(shared)
        m["xT"] = np.ascontiguousarray(xl.T)
        m["kbias"] = kb
        in_maps.append(m)
    return in_maps


_NC_CACHE = {}


def run(NH, phases="A,B1,B2,C1,C2a,C2b", **inputs):
    x = np.asarray(inputs["x"])
    B = x.shape[0]
    H = NH * 512
    in_maps = prep_inputs(NH, **inputs)
    if (NH, phases) not in _NC_CACHE:
        _NC_CACHE[(NH, phases)] = build_nc(NH, phases)
    nc = _NC_CACHE[(NH, phases)]
    res = run_bass_kernel_spmd(nc, in_maps, core_ids=list(range(2 * B)))
    out = np.zeros((B, 2 * H, D), np.float32)
    for c in range(2 * B):
        b, p = c // 2, c % 2
        o = np.asarray(res.results[c]["out"]).reshape(D, H)
        out[b, p * H:(p + 1) * H, :] = o.T
    return out


def kernel(**inputs):
    return run(8, **inputs)
```

```python
import numpy as np
from contextlib import ExitStack
import concourse.bass as bass
import concourse.mybir as mybir
from concourse.bass_utils import run_bass_kernel_spmd

F32 = mybir.dt.float32
BF16 = mybir.dt.bfloat16
AF = mybir.ActivationFunctionType
ALU = mybir.AluOpType

D = 1024
KT = 8
DFF = 2816
NEGM = -100.0
ENGS = ("pe", "act", "dve", "pool", "sp")
NDMA = 24
NDMA_HW = 16


class Buf:
    __slots__ = ("w", "r", "name")

    def __init__(self, name=""):
        self.w = None
        self.r = {}
        self.name = name


class V:
    __slots__ = ("buf", "ap")

    def __init__(self, buf, ap):
        self.buf = buf
        self.ap = ap


class Tile:
    def __init__(self, handle, name=""):
        self.t = handle
        self.buf = Buf(name)

    def __getitem__(self, idx):
        return V(self.buf, self.t[idx])


class Sched:
    def __init__(self):
        self.keys = list(ENGS) + [("d", i) for i in range(NDMA)]
        self.cnt = {k: 0 for k in self.keys}
        self.seen = {e: {k: 0 for k in self.keys} for e in ENGS}
        self.q = {e: [] for e in ENGS}
        self.dma_rr = 0
        self.dma_rr2 = 0
        self.nops = 0

    def op(self, eng, fns, reads=(), writes=(), dma=False):
        if callable(fns):
            fns = [fns]
        need = {}

        def req(tok):
            if tok is None:
                return
            k, v = tok
            if need.get(k, 0) < v:
                need[k] = v

        for b in reads:
            req(b.w)
        for b in writes:
            req(b.w)
            for k, v in b.r.items():
                req((k, v))
        if dma:
            if eng == "sp":
                key = ("d", self.dma_rr)
                self.dma_rr = (self.dma_rr + 1) % NDMA_HW
            else:
                key = ("d", NDMA_HW + self.dma_rr2)
                self.dma_rr2 = (self.dma_rr2 + 1) % (NDMA - NDMA_HW)
            req((key, self.cnt[key]))
            inc = 16
        else:
            key = eng
            inc = 1
        waits = []
        seen = self.seen[eng]
        for k, v in need.items():
            if v <= 0 or seen[k] >= v:
                continue
            if k == eng and eng == "pe":
                continue
            seen[k] = v
            waits.append((k, v))
        self.cnt[key] += inc
        tok = (key, self.cnt[key])
        for b in writes:
            b.w = tok
            b.r = {}
        for b in reads:
            if b.r.get(key, 0) < tok[1]:
                b.r[key] = tok[1]
        self.q[eng].append((waits, fns, key, inc))
        self.nops += 1
        return tok

    def barrier(self):
        for e in ENGS:
            waits = []
            for k in self.keys:
                v = self.cnt[k]
                if v > 0 and self.seen[e][k] < v and k != e:
                    self.seen[e][k] = v
                    waits.append((k, v))
            if waits:
                self.q[e].append((waits, [], None, 0))

    def replay(self, eng, e, sems):
        for waits, fns, key, inc in self.q[eng]:
            for k, v in waits:
                e.wait_ge(sems[k], v)
            ins = None
            for f in fns:
                ins = f(e)
            if ins is not None and key is not None:
                ins.then_inc(sems[key], inc)


def build_nc(NH, phases="A,B1,B2,C1,C2a,C2b"):
    H = NH * 512
    L2 = 2 * H
    NB2 = L2 // 128
    NBH = H // 128
    NOWN = (NH + 1) * 512
    NY = 2 + NH * 512
    NSWB = NH * 4 + 1

    nc = bass.Bass("TRN2", target_bir_lowering=False)
    S = Sched()

    def din(name, shape):
        return nc.dram_tensor(name, shape, F32, kind="ExternalInput").ap()

    xT = din("xT", [D, L2])
    kbias_d = din("kbias", [128, NB2])
    w_in = din("w_in", [D, 4352])
    w_sbp = din("w_sbp", [512, D])
    w_swp = din("w_swp", [512, D])
    w_out = din("w_out", [D, D])
    w_up = din("w_up", [D, 2 * DFF])
    w_down = din("w_down", [DFF, D])
    gm_d = din("gm", [128, 8])
    gf_d = din("gf", [128, 8])
    gl_d = din("gl", [128, 8])
    cw_d = din("cw", [128, 44 * 3])
    cb_d = din("cb", [128, 44])
    bm_d = din("bm", [128, 2048])
    sk_d = din("sk", [128, 4])
    cst_d = din("cst", [128, 2688])
    out_d = nc.dram_tensor("out", [8, 128, H], F32, kind="ExternalOutput").ap()

    def dscr(name, shape, dt):
        return nc.dram_tensor(name, shape, dt, kind="Internal").ap()

    KTs = dscr("KTs", [4, 128, L2], BF16)
    QTs = dscr("QTs", [4, 128, NOWN], BF16)
    VSs = dscr("VSs", [4, 128, NB2, 256], BF16)
    QWs = dscr("QWs", [4, 128, NOWN], BF16)
    KWs = dscr("KWs", [2, 128, NOWN], BF16)
    VWs = dscr("VWs", [128, (NH + 1) * 4, 512], BF16)
    YSB = dscr("YSB", [128, 4, NY], BF16)
    YSW = dscr("YSW", [128, 4, NSWB * 128], BF16)
    X1s = dscr("X1s", [128, 8, NY], F32)
    ATs = dscr("ATs", [128, 22, NY], BF16)

    groups = [(H - 2, 2, 0)] + [(H + 512 * c, 512, 2 + 512 * c) for c in range(NH)]

    def mm(out, lhsT, rhs, start, stop):
        return lambda e: e.matmul(out.ap, lhsT.ap, rhs.ap, start=start, stop=stop)

    def pe_group(out, pairs, extra_reads=()):
        fns = []
        n = len(pairs)
        rd = []
        for i, (l, r) in enumerate(pairs):
            fns.append(mm(out, l, r, i == 0, i == n - 1))
            rd += [l.buf, r.buf]
        S.op("pe", fns, reads=rd + list(extra_reads), writes=[out.buf])

    def act(eng_unused, out, in_, func, scale=1.0, bias=0.0, extra_reads=()):
        def f(e):
            kw = {}
            if isinstance(bias, V):
                kw["bias"] = bias.ap
            else:
                kw["bias"] = float(bias)
            if isinstance(scale, V):
                kw["scale"] = scale.ap
            else:
                kw["scale"] = float(scale)
            return e.activation(out.ap, in_.ap, func, **kw)

        rd = [in_.buf] + [x.buf for x in (scale, bias) if isinstance(x, V)] + list(extra_reads)
        S.op("act", f, reads=rd, writes=[out.buf])

    def tt(eng, out, a, b, op):
        S.op(eng, lambda e: e.tensor_tensor(out.ap, a.ap, b.ap, op), reads=[a.buf, b.buf], writes=[out.buf])

    def ts(eng, out, a, s1, op0, s2=None, op1=None):
        def f(e):
            sc1 = s1.ap if isinstance(s1, V) else float(s1)
            if s2 is None:
                return e.tensor_scalar(out.ap, a.ap, sc1, None, op0)
            sc2 = s2.ap if isinstance(s2, V) else float(s2)
            return e.tensor_scalar(out.ap, a.ap, sc1, sc2, op0, op1)

        rd = [a.buf] + [x.buf for x in (s1, s2) if isinstance(x, V)]
        S.op(eng, f, reads=rd, writes=[out.buf])

    def stt(out, a, sc, b, op0, op1):
        def f(e):
            s = sc.ap if isinstance(sc, V) else float(sc)
            return e.scalar_tensor_tensor(out.ap, a.ap, s, b.ap, op0, op1)

        rd = [a.buf, b.buf] + ([sc.buf] if isinstance(sc, V) else [])
        S.op("dve", f, reads=rd, writes=[out.buf])

    def cp(eng, out, a):
        if eng == "act":
            act(None, out, a, AF.Copy)
        else:
            S.op(eng, lambda e: e.tensor_copy(out.ap, a.ap), reads=[a.buf], writes=[out.buf])

    def memset(eng, out, val):
        S.op(eng, lambda e: e.memset(out.ap, val), writes=[out.buf])

    def load(out, src_ap, eng="sp"):
        S.op(eng, lambda e: e.dma_start(out=out.ap, in_=src_ap), writes=[out.buf], dma=True)

    def store(dst_ap, src, eng="pool"):
        S.op(eng, lambda e: e.dma_start(out=dst_ap, in_=src.ap), reads=[src.buf], dma=True)

    class Pool_:
        def __init__(self):
            self.es = ExitStack()

        def sb(self, name, shape, dt):
            return Tile(self.es.enter_context(nc.sbuf_tensor(name, shape, dt)), name)

        def ps(self, name, shape):
            return Tile(self.es.enter_context(nc.psum_tensor(name, shape, F32)), name)

        def close(self):
            self.es.close()

    gl = Pool_()
    ones = gl.sb("ones", [128, 128], BF16)
    tri = gl.sb("tri", [128, 128], BF16)
    ident = gl.sb("ident", [128, 128], BF16)
    dmask = gl.sb("dmask", [128, 4, 512], BF16)
    opad = gl.sb("opad", [128, 2, 128], BF16)
    kbias = gl.sb("kbias_s", [128, NB2], F32)
    gtmp = Pool_()
    cstf = gtmp.sb("cstf", [128, 2688], F32)

    load(cstf[:], cst_d[:, :])
    load(kbias[:], kbias_d[:, :])
    cp("dve", ones[:], cstf[:, 0:128])
    cp("dve", tri[:], cstf[:, 128:256])
    cp("dve", ident[:], cstf[:, 256:384])
    for j in range(4):
        cp("dve", dmask[:, j, :], cstf[:, 384 + 512 * j:384 + 512 * (j + 1)])
    for j in range(2):
        cp("dve", opad[:, j, :], cstf[:, 2432 + 128 * j:2432 + 128 * (j + 1)])

    def rms_stats(P, xs, n, sq, ssps, lnv, rs):
        act(None, sq[:, :, 0:n], xs[:, :, 0:n], AF.Square)
        pe_group(ssps[:, 0:n], [(ones[:], sq[:, k, 0:n]) for k in range(8)])
        act(None, lnv[:, 0:n], ssps[:, 0:n], AF.Ln, scale=1.0 / D, bias=1e-6)
        act(None, rs[:, 0:n], lnv[:, 0:n], AF.Exp, scale=-0.5)

    def phase_A():
        P = Pool_()
        WA = P.sb("WA", [128, 8, 3328], BF16)
        stg = [P.sb(f"wstg{i}", [128, 2304], F32) for i in range(2)]
        gm = P.sb("gm_s", [128, 8], F32)
        xs = [P.sb(f"xsA{i}", [128, 8, 512], F32) for i in range(2)]
        sq = P.sb("sqA", [128, 8, 512], BF16)
        hT = [P.sb(f"hTA{i}", [128, 8, 512], BF16) for i in range(2)]
        lnv = P.sb("lnvA", [128, 512], F32)
        rs = [P.sb(f"rsA{i}", [128, 512], F32) for i in range(2)]
        stF = [P.sb(f"stF{i}", [128, 14, 512], BF16) for i in range(2)]
        stV = [P.sb(f"stV{i}", [128, 4, 1536], BF16) for i in range(2)]
        ssps = P.ps("ssA", [128, 512])
        PS = [P.ps(f"psA{i}", [128, 512]) for i in range(6)]

        load(gm[:], gm_d[:, :])
        memset("pool", WA[:, :, 1792:3328], 0.0)
        for kt in range(8):
            st = stg[kt % 2]
            load(st[:], w_in[kt * 128:(kt + 1) * 128, 0:2304])
            g = gm[:, kt:kt + 1]
            e1 = "dve"
            act(None, WA[:, kt, 0:1024], st[:, 0:1024], AF.Identity, scale=g)
            ts(e1, WA[:, kt, 1024:1536], st[:, 1536:2048], g, ALU.mult)
            for gg in range(2):
                for r in range(2):
                    ts(e1, WA[:, kt, 1536 + 128 * gg + 64 * r:1536 + 128 * gg + 64 * (r + 1)],
                       st[:, 2048 + 64 * gg:2048 + 64 * (gg + 1)], g, ALU.mult)
            for h in range(8):
                o = 1792 + 128 * h + 64 * (h % 2)
                ts(e1, WA[:, kt, o:o + 64], st[:, 1024 + 64 * h:1024 + 64 * (h + 1)], g, ALU.mult)
            for i in range(4):
                o = 2816 + 128 * i + 64 * (i % 2)
                gg = i // 2
                ts(e1, WA[:, kt, o:o + 64], st[:, 2176 + 64 * gg:2176 + 64 * (gg + 1)], g, ALU.mult)

        psi = [0]

        def nextps():
            psi[0] = (psi[0] + 1) % 6
            return PS[psi[0]]

        evi = [0]

        def evac(out, in_):
            evi[0] += 1
            cp("act" if evi[0] % 2 == 0 else "dve", out, in_)

        def stage1(tc):
            s = tc % 2
            load(xs[s][:], xT[:, tc * 512:(tc + 1) * 512].rearrange("(k p) n -> p k n", p=128))
            rms_stats(P, xs[s], 512, sq, ssps, lnv, rs[s])
            for k in range(8):
                tt("dve" if k % 2 == 0 else "pool", hT[s][:, k, :], xs[s][:, k, :], rs[s][:], ALU.mult)

        def stage2(tc):
            s = tc % 2
            own = tc >= NH - 1
            h = hT[s]
            sf = stF[s]
            sv = stV[s]

            def fm(slot, col0):
                ps = nextps()
                pe_group(ps[:], [(WA[:, k, col0:col0 + 128], h[:, k, :]) for k in range(8)])
                evac(sf[:, slot, :], ps[:])

            for hp in range(4):
                fm(hp, 512 + 128 * hp)
            if own:
                for hp in range(4):
                    fm(4 + hp, 128 * hp)
                for hp in range(4):
                    fm(8 + hp, 1024 + 128 * hp)
                for gg in range(2):
                    fm(12 + gg, 1536 + 128 * gg)
            for blk in range(4):
                for half in range(2):
                    ps = nextps()
                    pe_group(ps[:], [(h[:, k, blk * 128:(blk + 1) * 128],
                                      WA[:, k, 1792 + 512 * half:1792 + 512 * (half + 1)]) for k in range(8)])
                    evac(sv[:, blk, 512 * half:512 * (half + 1)], ps[:])
                if own:
                    ps = nextps()
                    pe_group(ps[:], [(h[:, k, blk * 128:(blk + 1) * 128], WA[:, k, 2816:3328]) for k in range(8)])
                    evac(sv[:, blk, 1024:1536], ps[:])
            c0 = tc * 512
            store(KTs[:, :, c0:c0 + 512].rearrange("h p n -> p h n"), sf[:, 0:4, :])
            for hp in range(4):
                store(VSs[hp, :, 4 * tc:4 * tc + 4, :], sv[:, :, 256 * hp:256 * (hp + 1)])
            if own:
                o0 = (tc - (NH - 1)) * 512
                store(QTs[:, :, o0:o0 + 512].rearrange("h p n -> p h n"), sf[:, 4:8, :])
                store(QWs[:, :, o0:o0 + 512].rearrange("h p n -> p h n"), sf[:, 8:12, :])
                store(KWs[:, :, o0:o0 + 512].rearrange("h p n -> p h n"), sf[:, 12:14, :])
                b0 = (tc - (NH - 1)) * 4
                store(VWs[:, b0:b0 + 4, :], sv[:, :, 1024:1536])

        NT = 2 * NH
        stage1(0)
        for tc in range(NT):
            if tc + 1 < NT:
                stage1(tc + 1)
            stage2(tc)
        S.barrier()
        P.close()

    def phase_B1():
        P = Pool_()
        QW = P.sb("QW", [128, 4, NOWN], BF16)
        KW = P.sb("KW", [128, 2, NOWN], BF16)
        VW = P.sb("VW", [128, (NH + 1) * 4, 512], BF16)
        BM = P.sb("BM", [128, 2048], F32)
        sk = P.sb("sk_s", [128, 4], F32)
        esk = P.sb("esk", [128, 4], F32)
        ESB = P.sb("ESB", [128, 4, 128], F32)
        zer = P.sb("zerB1", [128, 128], F32)
        LG = [P.sb(f"LG{i}", [128, 2048], F32) for i in range(2)]
        PB = [P.sb(f"PB{i}", [128, 2, 2, 4, 128], BF16) for i in range(2)]
        den = P.sb("den", [128, 512], F32)
        rec = P.sb("rec", [128, 512], F32)
        yst = [P.sb(f"yst{i}", [128, 4, 128], BF16) for i in range(2)]
        ZS = P.ps("ZS", [128, 2, 2, 4, 128])
        OP = P.ps("OPs", [128, 4, 128])
        DN = P.ps("DNs", [128, 4, 128])

        load(QW[:], QWs.rearrange("h p n -> p h n"))
        load(KW[:], KWs.rearrange("h p n -> p h n"))
        load(VW[:], VWs[:, :, :])
        load(BM[:], bm_d[:, :])
        load(sk[:], sk_d[:, :])
        act(None, esk[:], sk[:], AF.Exp)
        memset("dve", zer[:], 0.0)
        for hp in range(4):
            ts("dve", ESB[:, hp, :], zer[:], esk[:, hp:hp + 1], ALU.add)

        for qi, i in enumerate(range(3, (NH + 1) * 4)):
            s = qi % 2
            qcols = slice(i * 128, (i + 1) * 128)
            for kbsel in range(2):
                ib = i - 1 + kbsel
                for h in range(8):
                    po = (h % 2) * 64
                    g = h // 4
                    hp = h // 2
                    S.op("pe", mm(ZS[:, kbsel, h % 2, hp, :], KW[po:po + 64, g, ib * 128:(ib + 1) * 128],
                                  QW[po:po + 64, hp, qcols], True, True),
                         reads=[KW.buf, QW.buf], writes=[ZS.buf])
            stt(LG[s][:], V(ZS.buf, ZS.t[:].rearrange("p a r h q -> p (a r h q)")), 0.125, BM[:], ALU.mult, ALU.add)
            for kbsel in range(2):
                lb = (NBH - 4) + i - 1 + kbsel
                act(None, V(PB[s].buf, PB[s].t[:, kbsel, :, :, :].rearrange("p r h q -> p (r h q)")),
                    LG[s][:, kbsel * 1024:(kbsel + 1) * 1024], AF.Exp, bias=kbias[:, lb:lb + 1])
            for hp in range(4):
                g = hp // 2
                prs = []
                prd = []
                for kbsel in range(2):
                    ib = i - 1 + kbsel
                    for r in range(2):
                        h = 2 * hp + r
                        var = 2 * g + r
                        prs.append((VW[:, ib, 128 * var:128 * (var + 1)], PB[s][:, kbsel, r, hp, :]))
                        prd.append((opad[:, r, :], PB[s][:, kbsel, r, hp, :]))
                pe_group(OP[:, hp, :], prs)
                pe_group(DN[:, hp, :], prd)
            tt("dve", den[:], V(DN.buf, DN.t[:].rearrange("p a q -> p (a q)")),
               V(ESB.buf, ESB.t[:].rearrange("p a q -> p (a q)")), ALU.add)
            S.op("dve", lambda e: e.reciprocal(rec.t[:], den.t[:]), reads=[den.buf], writes=[rec.buf])
            tt("dve", V(yst[s].buf, yst[s].t[:].rearrange("p a q -> p (a q)")),
               V(OP.buf, OP.t[:].rearrange("p a q -> p (a q)")), rec[:], ALU.mult)
            store(YSW[:, :, qi * 128:(qi + 1) * 128], yst[s][:])
        S.barrier()
        P.close()

    def phase_B2():
        P = Pool_()
        KTt = [P.sb(f"KTt{i}", [128, L2], BF16) for i in range(2)]
        Vt = [P.sb(f"Vt{i}", [128, NB2, 256], BF16) for i in range(2)]
        QTt = [P.sb(f"QTt{i}", [128, NOWN], BF16) for i in range(2)]
        NBUF = 4
        Eb = [P.sb(f"Eb{i}", [128, 2, 512], BF16) for i in range(NBUF)]
        Lb = [P.sb(f"Lb{i}", [128, 2, 512], BF16) for i in range(NBUF)]
        Gb = [P.sb(f"Gb{i}", [128, 2, 512], BF16) for i in range(NBUF)]
        Wb = [P.sb(f"Wb{i}", [128, 2, 512], BF16) for i in range(NBUF)]
        triC = P.sb("triC", [128, 128], BF16)
        yev = [P.sb(f"yev{i}", [128, 512], BF16) for i in range(2)]
        Z = [P.ps(f"Zp{i}", [128, 2, 512]) for i in range(2)]
        X = P.ps("Xp", [128, 2, 512])
        Y = [P.ps(f"Yp{i}", [128, 512]) for i in range(2)]
        tt("dve", triC[:], ones[:], tri[:], ALU.subtract)
        zb = P.sb("zbB2", [128, 512], BF16)
        memset("dve", zb[:], 0.0)

        def load_hp(hp):
            s = hp % 2
            hl = L2 // 2
            load(KTt[s][:, 0:hl], KTs[hp, :, 0:hl])
            load(KTt[s][:, hl:L2], KTs[hp, :, hl:L2])
            load(Vt[s][:, 0:NB2 // 2, :], VSs[hp, :, 0:NB2 // 2, :])
            load(Vt[s][:, NB2 // 2:NB2, :], VSs[hp, :, NB2 // 2:NB2, :])
            load(QTt[s][:], QTs[hp, :, :])

        gi = [0]
        load_hp(0)
        for hp in range(4):
            s = hp % 2
            if hp + 1 < 4:
                load_hp(hp + 1)
            Kt, Vv, Qt = KTt[s], Vt[s], QTt[s]
            for (qs, n, ycol) in groups:
                qc = qs - (NH - 1) * 512
                kbmax = (qs + n - 2) // 128
                kbs = list(range(kbmax, -1, -1))
                U = len(kbs)
                Yp = Y[gi[0] % 2]
                ye = yev[gi[0] % 2]
                gi[0] += 1

                def c0of(u):
                    j = kbs[u] - qs // 128
                    return 128 * j if (n == 512 and j >= 1) else 0

                def mmx(out, lhsT, rhs, start):
                    return lambda e: e.matmul(out.ap, lhsT.ap, rhs.ap, start=start, stop=True,
                                              skip_group_check=True)

                S.op("pe", [mmx(X[:, r, 0:n], ident[:], zb[:, 0:n], True) for r in range(2)],
                     reads=[ident.buf, zb.buf], writes=[X.buf])
                S.op("pe", [mmx(Yp[:, 0:n], ident[:], zb[:, 0:n], True)],
                     reads=[ident.buf, zb.buf], writes=[Yp.buf])

                def s_Z(u):
                    kb = kbs[u]
                    j = kb - qs // 128
                    c0 = c0of(u)
                    Zt = Z[u % 2]
                    for r in range(2):
                        po = 64 * r
                        prs = [(Kt[po:po + 64, kb * 128:(kb + 1) * 128], Qt[po:po + 64, qc + c0:qc + n])]
                        if j >= 0:
                            if n == 512:
                                prs.append((ident[:], dmask[:, j, c0:n]))
                            else:
                                prs.append((ident[:], dmask[:, 0, 126:128]))
                        pe_group(Zt[:, r, c0:n], prs)

                def s_EL(u):
                    kb = kbs[u]
                    b = u % NBUF
                    c0 = c0of(u)
                    act(None, Eb[b][:, :, c0:n], Z[u % 2][:, :, c0:n], AF.Exp, scale=0.125, bias=kbias[:, kb:kb + 1])
                    act(None, Lb[b][:, :, c0:n], Eb[b][:, :, c0:n], AF.Ln, bias=1.0)

                def s_A(u):
                    b = u % NBUF
                    c0 = c0of(u)
                    fns = [mmx(X[:, r, c0:n], tri[:], Lb[b][:, r, c0:n], False) for r in range(2)]
                    S.op("pe", fns, reads=[tri.buf, Lb[b].buf], writes=[X.buf])

                def s_B(u):
                    b = u % NBUF
                    c0 = c0of(u)
                    fns = [mmx(X[:, r, c0:n], triC[:], Lb[b][:, r, c0:n], False) for r in range(2)]
                    S.op("pe", fns, reads=[triC.buf, Lb[b].buf], writes=[X.buf])

                def s_G(u):
                    b = u % NBUF
                    c0 = c0of(u)
                    act(None, Gb[b][:, :, c0:n], X[:, :, c0:n], AF.Exp, scale=-1.0)
                    tt("dve", Wb[b][:, :, c0:n], Eb[b][:, :, c0:n], Gb[b][:, :, c0:n], ALU.mult)

                def s_Y(u):
                    kb = kbs[u]
                    b = u % NBUF
                    c0 = c0of(u)
                    fns = []
                    for r in range(2):
                        fns.append(mmx(Yp[:, c0:n], Vv[:, kb, 128 * r:128 * (r + 1)], Wb[b][:, r, c0:n], False))
                    S.op("pe", fns, reads=[Vv.buf, Wb[b].buf], writes=[Yp.buf])

                s_Z(0)
                if U > 1:
                    s_Z(1)
                s_EL(0)
                if U > 2:
                    s_Z(2)
                s_A(0)
                for t in range(U):
                    if t + 1 < U:
                        s_EL(t + 1)
                    s_G(t)
                    if t + 1 < U:
                        s_B(t)
                        s_A(t + 1)
                    if t + 3 < U:
                        s_Z(t + 3)
                    if t >= 1:
                        s_Y(t - 1)
                s_Y(U - 1)
                cp("dve", ye[:, 0:n], Yp[:, 0:n])
                store(YSB[:, hp, ycol:ycol + n], ye[:, 0:n])
        S.barrier()
        P.close()

    def load_w_bf16(P, W, src, nk, ncol, gvec, stg, piece):
        i = 0
        for k in range(nk):
            for c0 in range(0, ncol, piece):
                c1 = min(ncol, c0 + piece)
                st = stg[i % len(stg)]
                eng = "dve" if i % 2 == 0 else "act"
                i += 1
                load(st[:, 0:c1 - c0], src[k * 128:(k + 1) * 128, c0:c1])
                if gvec is None:
                    cp(eng, W[:, k, c0:c1], st[:, 0:c1 - c0])
                elif eng == "act":
                    act(None, W[:, k, c0:c1], st[:, 0:c1 - c0], AF.Identity, scale=gvec[:, k:k + 1])
                else:
                    ts(eng, W[:, k, c0:c1], st[:, 0:c1 - c0], gvec[:, k:k + 1], ALU.mult)

    def phase_C1():
        P = Pool_()
        WG = P.sb("WG", [128, 8, 2048], BF16)
        WSB = P.sb("WSB", [128, 4, 1024], BF16)
        WSW = P.sb("WSW", [128, 4, 1024], BF16)
        WO = P.sb("WO", [128, 8, 1024], BF16)
        stg = [P.sb(f"stgC1{i}", [128, 1024], F32) for i in range(4)]
        gm = P.sb("gm_c1", [128, 8], F32)
        xs = [P.sb(f"xsC{i}", [128, 8, 512], F32) for i in range(2)]
        sq = P.sb("sqC", [128, 8, 512], BF16)
        hT = [P.sb(f"hTC{i}", [128, 8, 512], BF16) for i in range(2)]
        lnv = P.sb("lnvC", [128, 512], F32)
        rs = P.sb("rsC", [128, 512], F32)
        ysb = [P.sb(f"ysbC{i}", [128, 4, 512], BF16) for i in range(2)]
        ysw = [P.sb(f"yswC{i}", [128, 4, 512], BF16) for i in range(2)]
        g1 = [P.sb(f"g1C{i}", [128, 512], F32) for i in range(2)]
        g2 = [P.sb(f"g2C{i}", [128, 512], F32) for i in range(2)]
        t1 = [P.sb(f"t1C{i}", [128, 512], F32) for i in range(2)]
        t2 = [P.sb(f"t2C{i}", [128, 512], F32) for i in range(2)]
        mT = P.sb("mT", [128, 8, 512], BF16)
        x1 = P.sb("x1C", [128, 8, 512], F32)
        ssps = P.ps("ssC", [128, 512])
        PS = [P.ps(f"psC{i}", [128, 512]) for i in range(7)]

        load(gm[:], gm_d[:, :])
        load_w_bf16(P, WG, w_in[:, 2304:4352], 8, 2048, gm, stg, 1024)
        load_w_bf16(P, WSB, w_sbp, 4, 1024, None, stg, 1024)
        load_w_bf16(P, WSW, w_swp, 4, 1024, None, stg, 1024)
        load_w_bf16(P, WO, w_out, 8, 1024, None, stg, 1024)
        psi = [0]

        def nextps():
            psi[0] = (psi[0] + 1) % 7
            return PS[psi[0]]

        def prologue(gi_):
            qs, n, oc = groups[gi_]
            s = gi_ % 2
            x = xs[s]
            load(x[:, :, 0:n], xT[:, qs:qs + n].rearrange("(k p) n -> p k n", p=128))
            load(ysb[s][:, :, 0:n], YSB[:, :, oc:oc + n])
            swc = qs - (H - 128)
            load(ysw[s][:, :, 0:n], YSW[:, :, swc:swc + n])
            rms_stats(P, x, n, sq, ssps, lnv, rs)
            for k in range(8):
                tt("dve" if k % 2 == 0 else "pool", hT[s][:, k, 0:n], x[:, k, 0:n], rs[:, 0:n], ALU.mult)

        def main(gi_):
            qs, n, oc = groups[gi_]
            s = gi_ % 2
            x = xs[s]
            h = hT[s]
            for o in range(8):
                pg1 = nextps()
                pe_group(pg1[:, 0:n], [(WG[:, k, o * 128:(o + 1) * 128], h[:, k, 0:n]) for k in range(8)])
                act(None, g1[o % 2][:, 0:n], pg1[:, 0:n], AF.Sigmoid)
                pg2 = nextps()
                pe_group(pg2[:, 0:n], [(WG[:, k, (8 + o) * 128:(9 + o) * 128], h[:, k, 0:n]) for k in range(8)])
                act(None, g2[o % 2][:, 0:n], pg2[:, 0:n], AF.Sigmoid)
                pa = nextps()
                pe_group(pa[:, 0:n], [(WSB[:, k, o * 128:(o + 1) * 128], ysb[s][:, k, 0:n]) for k in range(4)])
                pb = nextps()
                pe_group(pb[:, 0:n], [(WSW[:, k, o * 128:(o + 1) * 128], ysw[s][:, k, 0:n]) for k in range(4)])
                tt("dve", t1[o % 2][:, 0:n], pa[:, 0:n], g1[o % 2][:, 0:n], ALU.mult)
                tt("dve", t2[o % 2][:, 0:n], pb[:, 0:n], g2[o % 2][:, 0:n], ALU.mult)
                tt("pool", mT[:, o, 0:n], t1[o % 2][:, 0:n], t2[o % 2][:, 0:n], ALU.add)
            for o in range(8):
                ps = nextps()
                pe_group(ps[:, 0:n], [(WO[:, k, o * 128:(o + 1) * 128], mT[:, k, 0:n]) for k in range(8)])
                tt("dve", x1[:, o, 0:n], ps[:, 0:n], x[:, o, 0:n], ALU.add)
            store(X1s[:, :, oc:oc + n], x1[:, :, 0:n])

        prologue(0)
        for gi_ in range(len(groups)):
            if gi_ + 1 < len(groups):
                prologue(gi_ + 1)
            main(gi_)
        S.barrier()
        P.close()

    def phase_C2a():
        P = Pool_()
        WU = P.sb("WU", [128, 8, 2 * DFF], BF16)
        stg = [P.sb(f"stgU{i}", [128, 1408], F32) for i in range(4)]
        gf = P.sb("gf_s", [128, 8], F32)
        cw = P.sb("cw_s", [128, 44, 3], F32)
        cb = P.sb("cb_s", [128, 44], F32)
        xs = P.sb("xsU", [128, 8, 512], F32)
        sq = P.sb("sqU", [128, 8, 512], BF16)
        hT = [P.sb(f"hTU{i}", [128, 8, 512], BF16) for i in range(2)]
        lnv = P.sb("lnvU", [128, 512], F32)
        rs = P.sb("rsU", [128, 512], F32)
        yb = [P.sb(f"ybU{i}", [128, 512], F32) for i in range(4)]
        sg = [P.sb(f"sgU{i}", [128, 512], F32) for i in range(2)]
        aT = P.sb("aTU", [128, 22, 512], BF16)
        ssps = P.ps("ssU", [128, 512])
        PS = [P.ps(f"psU{i}", [128, 512]) for i in range(7)]

        load(gf[:], gf_d[:, :])
        load(cw[:], cw_d.rearrange("p (j k) -> p j k", k=3))
        load(cb[:], cb_d[:, :])
        load_w_bf16(P, WU, w_up, 8, 2 * DFF, gf, stg, 1408)
        psi = [0]

        def nextps():
            psi[0] = (psi[0] + 1) % 7
            return PS[psi[0]]

        GW = 510
        NG = (H + GW - 1) // GW

        def geo(g):
            a0 = g * GW
            w = min(GW, H - a0)
            return a0, w, w + 2

        def prologue(g):
            a0, w, n = geo(g)
            load(xs[:, :, 0:n], X1s[:, :, a0:a0 + n])
            rms_stats(P, xs, n, sq, ssps, lnv, rs)
            for k in range(8):
                tt("dve" if k % 2 == 0 else "pool", hT[g % 2][:, k, 0:n], xs[:, k, 0:n], rs[:, 0:n], ALU.mult)

        ybi = [0]

        def tiles(g):
            a0, w, n = geo(g)
            h = hT[g % 2]
            for j in range(22):
                ys = []
                for t in (j, 22 + j):
                    ps = nextps()
                    pe_group(ps[:, 0:n], [(WU[:, k, t * 128:(t + 1) * 128], h[:, k, 0:n]) for k in range(8)])
                    y = yb[ybi[0] % 4]
                    ybi[0] += 1
                    ys.append(y)
                    act(None, y[:, 0:w], ps[:, 2:n], AF.Identity, scale=cw[:, t, 2:3], bias=cb[:, t:t + 1])
                    stt(y[:, 0:w], ps[:, 1:n - 1], cw[:, t, 1:2], y[:, 0:w], ALU.mult, ALU.add)
                    stt(y[:, 0:w], ps[:, 0:w], cw[:, t, 0:1], y[:, 0:w], ALU.mult, ALU.add)
                sgt = sg[j % 2]
                act(None, sgt[:, 0:w], ys[0][:, 0:w], AF.Silu)
                tt("pool", aT[:, j, 0:w], sgt[:, 0:w], ys[1][:, 0:w], ALU.mult)
            store(ATs[:, :, 2 + a0:2 + a0 + w], aT[:, :, 0:w])

        prologue(0)
        for g in range(NG):
            if g + 1 < NG:
                prologue(g + 1)
            tiles(g)
        S.barrier()
        P.close()

    def phase_C2b():
        P = Pool_()
        WD = P.sb("WD", [128, 22, 1024], BF16)
        stg = [P.sb(f"stgD{i}", [128, 1024], F32) for i in range(4)]
        glf = P.sb("gl_s", [128, 8], F32)
        xs = [P.sb(f"xsD{i}", [128, 8, 512], F32) for i in range(2)]
        aT = [P.sb(f"aTD{i}", [128, 22, 512], BF16) for i in range(2)]
        x2 = P.sb("x2D", [128, 8, 512], F32)
        sq = P.sb("sqD", [128, 8, 512], BF16)
        lnv = P.sb("lnvD", [128, 512], F32)
        rs = P.sb("rsD", [128, 512], F32)
        ob = [P.sb(f"obD{i}", [128, 8, 512], F32) for i in range(2)]
        ssps = P.ps("ssD", [128, 512])
        PS = [P.ps(f"psD{i}", [128, 512]) for i in range(6)]

        load(glf[:], gl_d[:, :])
        load_w_bf16(P, WD, w_down, 22, 1024, None, stg, 1024)
        psi = [0]

        def nextps():
            psi[0] = (psi[0] + 1) % 6
            return PS[psi[0]]

        for gi_, (qs, n, oc) in enumerate(groups[1:]):
            s = gi_ % 2
            x = xs[s]
            load(x[:], X1s[:, :, oc:oc + n])
            load(aT[s][:], ATs[:, :, oc:oc + n])
            for o in range(8):
                ps = nextps()
                pe_group(ps[:], [(WD[:, k, o * 128:(o + 1) * 128], aT[s][:, k, :]) for k in range(22)])
                tt("dve", x2[:, o, :], ps[:], x[:, o, :], ALU.add)
            rms_stats(P, x2, 512, sq, ssps, lnv, rs)
            for o in range(8):
                stt(ob[s][:, o, :], x2[:, o, :], glf[:, o:o + 1], rs[:], ALU.mult, ALU.mult)
            store(out_d[:, :, oc - 2:oc - 2 + n].rearrange("k p n -> p k n"), ob[s][:])
        S.barrier()
        P.close()

    S.barrier()
    gtmp.close()
    ph = phases.split(",")
    if "A" in ph:
        phase_A()
    if "B1" in ph:
        phase_B1()
    if "B2" in ph:
        phase_B2()
    if "C1" in ph:
        phase_C1()
    if "C2a" in ph:
        phase_C2a()
    if "C2b" in ph:
        phase_C2b()
    gl.close()

    with ExitStack() as es:
        sems = {}
        for k in S.keys:
            nm = k if isinstance(k, str) else f"d{k[1]}"
            sems[k] = es.enter_context(nc.semaphore("s_" + nm))
        block = es.enter_context(nc.Block())

        @block.tensor
        def _(e):
            S.replay("pe", e, sems)

        @block.scalar
        def _(e):
            S.replay("act", e, sems)

        @block.vector
        def _(e):
            S.replay("dve", e, sems)

        @block.gpsimd
        def _(e):
            S.replay("pool", e, sems)

        @block.sync
        def _(e):
            S.replay("sp", e, sems)

    return nc


def _t5_bucket(dist):
    dist = np.asarray(dist, np.int32)
    max_exact = 16
    d = np.maximum(dist, 1).astype(np.float32)
    large = max_exact + (np.log(d / np.float32(max_exact)) / np.float32(np.log(128 / max_exact))
                         * np.float32(32 - max_exact)).astype(np.int32)
    large = np.minimum(large, 31)
    return np.where(dist < max_exact, dist, large)


def _consts():
    c = np.zeros((128, 2688), np.float32)
    c[:, 0:128] = 1.0
    j = np.arange(128)[:, None]
    s = np.arange(128)[None, :]
    c[:, 128:256] = (j >= s).astype(np.float32)
    c[:, 256:384] = np.eye(128, dtype=np.float32)
    q = np.arange(512)[None, :]
    for jj in range(4):
        valid = (128 * jj + j) < q
        c[:, 384 + 512 * jj:384 + 512 * (jj + 1)] = np.where(valid, 0.0, 8.0 * NEGM)
    c[:, 2432:2432 + 64] = 1.0
    c[:, 2432 + 128 + 64:2432 + 256] = 1.0
    return c


def prep_inputs(NH, x, g_mix, w_in, w_sb_proj, w_sw_proj, w_out, rel_bias, sinks,
                g_ffn, w_up, conv_w, conv_b, w_down, g_final):
    H = NH * 512
    B = x.shape[0]
    f = lambda a: np.ascontiguousarray(np.asarray(a, dtype=np.float32))
    x = f(x)
    lay8 = lambda g: f(np.asarray(g, np.float32).reshape(8, 128).T)
    rel_bias = np.asarray(rel_bias, np.float32)
    k = np.arange(128)[:, None]
    q = np.arange(128)[None, :]
    bm = np.zeros((128, 2, 2, 4, 128), np.float32)
    d0 = 128 + q - k
    d1 = q - k
    b0 = _t5_bucket(np.clip(d0, 0, 255))
    b1 = _t5_bucket(np.clip(d1, 0, 255))
    for h in range(8):
        bm[:, 0, h % 2, h // 2, :] = np.where(d0 <= 127, rel_bias[b0, h], NEGM)
        bm[:, 1, h % 2, h // 2, :] = np.where(d1 >= 0, rel_bias[b1, h], NEGM)
    sinks = np.asarray(sinks, np.float32)
    sk = np.zeros((128, 4), np.float32)
    for hp in range(4):
        sk[0:64, hp] = sinks[2 * hp]
        sk[64:128, hp] = sinks[2 * hp + 1]
    cw = np.asarray(conv_w, np.float32)
    cwl = np.ascontiguousarray(cw.reshape(3, 44, 128).transpose(2, 1, 0)).reshape(128, 132)
    cbl = f(np.asarray(conv_b, np.float32).reshape(44, 128).T)
    shared = {
        "w_in": f(w_in), "w_sbp": f(w_sb_proj), "w_swp": f(w_sw_proj), "w_out": f(w_out),
        "w_up": f(w_up), "w_down": f(w_down), "gm": lay8(g_mix), "gf": lay8(g_ffn), "gl": lay8(g_final),
        "cw": f(cwl), "cb": cbl, "bm": f(bm.reshape(128, 2048)), "sk": sk, "cst": _consts(),
    }
    in_maps = []
    for c in range(2 * B):
        b, p = c // 2, c % 2
        xl = np.zeros((2 * H, D), np.float32)
        kb = np.zeros((128, 2 * H // 128), np.float32)
        if p == 1:
            xl[:] = x[b]
        else:
            xl[H:] = x[b, :H]
            kb[:, :H // 128] = NEGM
        m = dict(shared)
        m["xT"] = np.ascontiguousarray(xl.T)
        m["kbias"] = kb
        in_maps.append(m)
    return in_maps


_NC_CACHE = {}


def run(NH, phases="A,B1,B2,C1,C2a,C2b", **inputs):
    x = np.asarray(inputs["x"])
    B = x.shape[0]
    H = NH * 512
    in_maps = prep_inputs(NH, **inputs)
    if (NH, phases) not in _NC_CACHE:
        _NC_CACHE[(NH, phases)] = build_nc(NH, phases)
    nc = _NC_CACHE[(NH, phases)]
    res = run_bass_kernel_spmd(nc, in_maps, core_ids=list(range(2 * B)))
    out = np.zeros((B, 2 * H, D), np.float32)
    for c in range(2 * B):
        b, p = c // 2, c % 2
        o = np.asarray(res.results[c]["out"]).reshape(D, H)
        out[b, p * H:(p + 1) * H, :] = o.T
    return out


def kernel(**inputs):
    return run(8, **inputs)
```

```python
import numpy as np
from contextlib import ExitStack
import concourse.bass as bass
import concourse.mybir as mybir
from concourse.bass_utils import run_bass_kernel_spmd

F32 = mybir.dt.float32
BF16 = mybir.dt.bfloat16
AF = mybir.ActivationFunctionType
ALU = mybir.AluOpType

D = 1024
KT = 8
DFF = 2816
NEGM = -100.0
ENGS = ("pe", "act", "dve", "pool", "sp")
NDMA = 24
NDMA_HW = 16


class Buf:
    __slots__ = ("w", "r", "name")

    def __init__(self, name=""):
        self.w = None
        self.r = {}
        self.name = name


class V:
    __slots__ = ("buf", "ap")

    def __init__(self, buf, ap):
        self.buf = buf
        self.ap = ap


class Tile:
    def __init__(self, handle, name=""):
        self.t = handle
        self.buf = Buf(name)

    def __getitem__(self, idx):
        return V(self.buf, self.t[idx])


class Sched:
    def __init__(self):
        self.keys = list(ENGS) + [("d", i) for i in range(NDMA)]
        self.cnt = {k: 0 for k in self.keys}
        self.seen = {e: {k: 0 for k in self.keys} for e in ENGS}
        self.q = {e: [] for e in ENGS}
        self.dma_rr = 0
        self.dma_rr2 = 0
        self.nops = 0

    def op(self, eng, fns, reads=(), writes=(), dma=False):
        if callable(fns):
            fns = [fns]
        need = {}

        def req(tok):
            if tok is None:
                return
            k, v = tok
            if need.get(k, 0) < v:
                need[k] = v

        for b in reads:
            req(b.w)
        for b in writes:
            req(b.w)
            for k, v in b.r.items():
                req((k, v))
        if dma:
            if eng == "sp":
                key = ("d", self.dma_rr)
                self.dma_rr = (self.dma_rr + 1) % NDMA_HW
            else:
                key = ("d", NDMA_HW + self.dma_rr2)
                self.dma_rr2 = (self.dma_rr2 + 1) % (NDMA - NDMA_HW)
            req((key, self.cnt[key]))
            inc = 16
        else:
            key = eng
            inc = 1
        waits = []
        seen = self.seen[eng]
        for k, v in need.items():
            if v <= 0 or seen[k] >= v:
                continue
            if k == eng and eng == "pe":
                continue
            seen[k] = v
            waits.append((k, v))
        self.cnt[key] += inc
        tok = (key, self.cnt[key])
        for b in writes:
            b.w = tok
            b.r = {}
        for b in reads:
            if b.r.get(key, 0) < tok[1]:
                b.r[key] = tok[1]
        self.q[eng].append((waits, fns, key, inc))
        self.nops += 1
        return tok

    def barrier(self):
        for e in ENGS:
            waits = []
            for k in self.keys:
                v = self.cnt[k]
                if v > 0 and self.seen[e][k] < v and k != e:
                    self.seen[e][k] = v
                    waits.append((k, v))
            if waits:
                self.q[e].append((waits, [], None, 0))

    def replay(self, eng, e, sems):
        for waits, fns, key, inc in self.q[eng]:
            for k, v in waits:
                e.wait_ge(sems[k], v)
            ins = None
            for f in fns:
                ins = f(e)
            if ins is not None and key is not None:
                ins.then_inc(sems[key], inc)


def build_nc(NH, phases="A,B1,B2,C1,C2a,C2b"):
    H = NH * 512
    L2 = 2 * H
    NB2 = L2 // 128
    NBH = H // 128
    NOWN = (NH + 1) * 512
    NY = 2 + NH * 512
    NSWB = NH * 4 + 1

    nc = bass.Bass("TRN2", target_bir_lowering=False)
    S = Sched()

    def din(name, shape):
        return nc.dram_tensor(name, shape, F32, kind="ExternalInput").ap()

    xT = din("xT", [D, L2])
    kbias_d = din("kbias", [128, NB2])
    w_in = din("w_in", [D, 4352])
    w_sbp = din("w_sbp", [512, D])
    w_swp = din("w_swp", [512, D])
    w_out = din("w_out", [D, D])
    w_up = din("w_up", [D, 2 * DFF])
    w_down = din("w_down", [DFF, D])
    gm_d = din("gm", [128, 8])
    gf_d = din("gf", [128, 8])
    gl_d = din("gl", [128, 8])
    cw_d = din("cw", [128, 44 * 3])
    cb_d = din("cb", [128, 44])
    bm_d = din("bm", [128, 2048])
    sk_d = din("sk", [128, 4])
    cst_d = din("cst", [128, 2688])
    out_d = nc.dram_tensor("out", [8, 128, H], F32, kind="ExternalOutput").ap()

    def dscr(name, shape, dt):
        return nc.dram_tensor(name, shape, dt, kind="Internal").ap()

    KTs = dscr("KTs", [4, 128, L2], BF16)
    QTs = dscr("QTs", [4, 128, NOWN], BF16)
    VSs = dscr("VSs", [4, 128, NB2, 256], BF16)
    QWs = dscr("QWs", [4, 128, NOWN], BF16)
    KWs = dscr("KWs", [2, 128, NOWN], BF16)
    VWs = dscr("VWs", [128, (NH + 1) * 4, 512], BF16)
    YSB = dscr("YSB", [128, 4, NY], BF16)
    YSW = dscr("YSW", [128, 4, NSWB * 128], BF16)
    X1s = dscr("X1s", [128, 8, NY], F32)
    ATs = dscr("ATs", [128, 22, NY], BF16)

    groups = [(H - 2, 2, 0)] + [(H + 512 * c, 512, 2 + 512 * c) for c in range(NH)]

    def mm(out, lhsT, rhs, start, stop):
        return lambda e: e.matmul(out.ap, lhsT.ap, rhs.ap, start=start, stop=stop)

    def pe_group(out, pairs, extra_reads=()):
        fns = []
        n = len(pairs)
        rd = []
        for i, (l, r) in enumerate(pairs):
            fns.append(mm(out, l, r, i == 0, i == n - 1))
            rd += [l.buf, r.buf]
        S.op("pe", fns, reads=rd + list(extra_reads), writes=[out.buf])

    def act(eng_unused, out, in_, func, scale=1.0, bias=0.0, extra_reads=()):
        def f(e):
            kw = {}
            if isinstance(bias, V):
                kw["bias"] = bias.ap
            else:
                kw["bias"] = float(bias)
            if isinstance(scale, V):
                kw["scale"] = scale.ap
            else:
                kw["scale"] = float(scale)
            return e.activation(out.ap, in_.ap, func, **kw)

        rd = [in_.buf] + [x.buf for x in (scale, bias) if isinstance(x, V)] + list(extra_reads)
        S.op("act", f, reads=rd, writes=[out.buf])

    def tt(eng, out, a, b, op):
        S.op(eng, lambda e: e.tensor_tensor(out.ap, a.ap, b.ap, op), reads=[a.buf, b.buf], writes=[out.buf])

    def ts(eng, out, a, s1, op0, s2=None, op1=None):
        def f(e):
            sc1 = s1.ap if isinstance(s1, V) else float(s1)
            if s2 is None:
                return e.tensor_scalar(out.ap, a.ap, sc1, None, op0)
            sc2 = s2.ap if isinstance(s2, V) else float(s2)
            return e.tensor_scalar(out.ap, a.ap, sc1, sc2, op0, op1)

        rd = [a.buf] + [x.buf for x in (s1, s2) if isinstance(x, V)]
        S.op(eng, f, reads=rd, writes=[out.buf])

    def stt(out, a, sc, b, op0, op1):
        def f(e):
            s = sc.ap if isinstance(sc, V) else float(sc)
            return e.scalar_tensor_tensor(out.ap, a.ap, s, b.ap, op0, op1)

        rd = [a.buf, b.buf] + ([sc.buf] if isinstance(sc, V) else [])
        S.op("dve", f, reads=rd, writes=[out.buf])

    def cp(eng, out, a):
        if eng == "act":
            act(None, out, a, AF.Copy)
        else:
            S.op(eng, lambda e: e.tensor_copy(out.ap, a.ap), reads=[a.buf], writes=[out.buf])

    def memset(eng, out, val):
        S.op(eng, lambda e: e.memset(out.ap, val), writes=[out.buf])

    def load(out, src_ap, eng="sp"):
        S.op(eng, lambda e: e.dma_start(out=out.ap, in_=src_ap), writes=[out.buf], dma=True)

    def store(dst_ap, src, eng="pool"):
        S.op(eng, lambda e: e.dma_start(out=dst_ap, in_=src.ap), reads=[src.buf], dma=True)

    class Pool_:
        def __init__(self):
            self.es = ExitStack()

        def sb(self, name, shape, dt):
            return Tile(self.es.enter_context(nc.sbuf_tensor(name, shape, dt)), name)

        def ps(self, name, shape):
            return Tile(self.es.enter_context(nc.psum_tensor(name, shape, F32)), name)

        def close(self):
            self.es.close()

    gl = Pool_()
    ones = gl.sb("ones", [128, 128], BF16)
    tri = gl.sb("tri", [128, 128], BF16)
    ident = gl.sb("ident", [128, 128], BF16)
    dmask = gl.sb("dmask", [128, 4, 512], BF16)
    opad = gl.sb("opad", [128, 2, 128], BF16)
    kbias = gl.sb("kbias_s", [128, NB2], F32)
    gtmp = Pool_()
    cstf = gtmp.sb("cstf", [128, 2688], F32)

    load(cstf[:], cst_d[:, :])
    load(kbias[:], kbias_d[:, :])
    cp("dve", ones[:], cstf[:, 0:128])
    cp("dve", tri[:], cstf[:, 128:256])
    cp("dve", ident[:], cstf[:, 256:384])
    for j in range(4):
        cp("dve", dmask[:, j, :], cstf[:, 384 + 512 * j:384 + 512 * (j + 1)])
    for j in range(2):
        cp("dve", opad[:, j, :], cstf[:, 2432 + 128 * j:2432 + 128 * (j + 1)])

    def rms_stats(P, xs, n, sq, ssps, lnv, rs):
        act(None, sq[:, :, 0:n], xs[:, :, 0:n], AF.Square)
        pe_group(ssps[:, 0:n], [(ones[:], sq[:, k, 0:n]) for k in range(8)])
        act(None, lnv[:, 0:n], ssps[:, 0:n], AF.Ln, scale=1.0 / D, bias=1e-6)
        act(None, rs[:, 0:n], lnv[:, 0:n], AF.Exp, scale=-0.5)

    def phase_A():
        P = Pool_()
        WA = P.sb("WA", [128, 8, 3328], BF16)
        stg = [P.sb(f"wstg{i}", [128, 2304], F32) for i in range(2)]
        gm = P.sb("gm_s", [128, 8], F32)
        xs = [P.sb(f"xsA{i}", [128, 8, 512], F32) for i in range(2)]
        sq = P.sb("sqA", [128, 8, 512], BF16)
        hT = [P.sb(f"hTA{i}", [128, 8, 512], BF16) for i in range(2)]
        lnv = P.sb("lnvA", [128, 512], F32)
        rs = [P.sb(f"rsA{i}", [128, 512], F32) for i in range(2)]
        stF = [P.sb(f"stF{i}", [128, 14, 512], BF16) for i in range(2)]
        stV = [P.sb(f"stV{i}", [128, 4, 1536], BF16) for i in range(2)]
        ssps = P.ps("ssA", [128, 512])
        PS = [P.ps(f"psA{i}", [128, 512]) for i in range(6)]

        load(gm[:], gm_d[:, :])
        memset("pool", WA[:, :, 1792:3328], 0.0)
        for kt in range(8):
            st = stg[kt % 2]
            load(st[:], w_in[kt * 128:(kt + 1) * 128, 0:2304])
            g = gm[:, kt:kt + 1]
            e1 = "dve"
            act(None, WA[:, kt, 0:1024], st[:, 0:1024], AF.Identity, scale=g)
            ts(e1, WA[:, kt, 1024:1536], st[:, 1536:2048], g, ALU.mult)
            for gg in range(2):
                for r in range(2):
                    ts(e1, WA[:, kt, 1536 + 128 * gg + 64 * r:1536 + 128 * gg + 64 * (r + 1)],
                       st[:, 2048 + 64 * gg:2048 + 64 * (gg + 1)], g, ALU.mult)
            for h in range(8):
                o = 1792 + 128 * h + 64 * (h % 2)
                ts(e1, WA[:, kt, o:o + 64], st[:, 1024 + 64 * h:1024 + 64 * (h + 1)], g, ALU.mult)
            for i in range(4):
                o = 2816 + 128 * i + 64 * (i % 2)
                gg = i // 2
                ts(e1, WA[:, kt, o:o + 64], st[:, 2176 + 64 * gg:2176 + 64 * (gg + 1)], g, ALU.mult)

        psi = [0]

        def nextps():
            psi[0] = (psi[0] + 1) % 6
            return PS[psi[0]]

        evi = [0]

        def evac(out, in_):
            evi[0] += 1
            cp("act" if evi[0] % 2 == 0 else "dve", out, in_)

        def stage1(tc):
            s = tc % 2
            load(xs[s][:], xT[:, tc * 512:(tc + 1) * 512].rearrange("(k p) n -> p k n", p=128))
            rms_stats(P, xs[s], 512, sq, ssps, lnv, rs[s])
            for k in range(8):
                tt("dve" if k % 2 == 0 else "pool", hT[s][:, k, :], xs[s][:, k, :], rs[s][:], ALU.mult)

        def stage2(tc):
            s = tc % 2
            own = tc >= NH - 1
            h = hT[s]
            sf = stF[s]
            sv = stV[s]

            def fm(slot, col0):
                ps = nextps()
                pe_group(ps[:], [(WA[:, k, col0:col0 + 128], h[:, k, :]) for k in range(8)])
                evac(sf[:, slot, :], ps[:])

            for hp in range(4):
                fm(hp, 512 + 128 * hp)
            if own:
                for hp in range(4):
                    fm(4 + hp, 128 * hp)
                for hp in range(4):
                    fm(8 + hp, 1024 + 128 * hp)
                for gg in range(2):
                    fm(12 + gg, 1536 + 128 * gg)
            for blk in range(4):
                for half in range(2):
                    ps = nextps()
                    pe_group(ps[:], [(h[:, k, blk * 128:(blk + 1) * 128],
                                      WA[:, k, 1792 + 512 * half:1792 + 512 * (half + 1)]) for k in range(8)])
                    evac(sv[:, blk, 512 * half:512 * (half + 1)], ps[:])
                if own:
                    ps = nextps()
                    pe_group(ps[:], [(h[:, k, blk * 128:(blk + 1) * 128], WA[:, k, 2816:3328]) for k in range(8)])
                    evac(sv[:, blk, 1024:1536], ps[:])
            c0 = tc * 512
            store(KTs[:, :, c0:c0 + 512].rearrange("h p n -> p h n"), sf[:, 0:4, :])
            for hp in range(4):
                store(VSs[hp, :, 4 * tc:4 * tc + 4, :], sv[:, :, 256 * hp:256 * (hp + 1)])
            if own:
                o0 = (tc - (NH - 1)) * 512
                store(QTs[:, :, o0:o0 + 512].rearrange("h p n -> p h n"), sf[:, 4:8, :])
                store(QWs[:, :, o0:o0 + 512].rearrange("h p n -> p h n"), sf[:, 8:12, :])
                store(KWs[:, :, o0:o0 + 512].rearrange("h p n -> p h n"), sf[:, 12:14, :])
                b0 = (tc - (NH - 1)) * 4
                store(VWs[:, b0:b0 + 4, :], sv[:, :, 1024:1536])

        NT = 2 * NH
        stage1(0)
        for tc in range(NT):
            if tc + 1 < NT:
                stage1(tc + 1)
            stage2(tc)
        S.barrier()
        P.close()

    def phase_B1():
        P = Pool_()
        QW = P.sb("QW", [128, 4, NOWN], BF16)
        KW = P.sb("KW", [128, 2, NOWN], BF16)
        VW = P.sb("VW", [128, (NH + 1) * 4, 512], BF16)
        BM = P.sb("BM", [128, 2048], F32)
        sk = P.sb("sk_s", [128, 4], F32)
        esk = P.sb("esk", [128, 4], F32)
        ESB = P.sb("ESB", [128, 4, 128], F32)
        zer = P.sb("zerB1", [128, 128], F32)
        LG = [P.sb(f"LG{i}", [128, 2048], F32) for i in range(2)]
        PB = [P.sb(f"PB{i}", [128, 2, 2, 4, 128], BF16) for i in range(2)]
        den = P.sb("den", [128, 512], F32)
        rec = P.sb("rec", [128, 512], F32)
        lden = P.sb("lden", [128, 512], F32)
        yst = [P.sb(f"yst{i}", [128, 4, 128], BF16) for i in range(2)]
        ZS = P.ps("ZS", [128, 2, 2, 4, 128])
        OP = P.ps("OPs", [128, 4, 128])
        DN = P.ps("DNs", [128, 4, 128])

        load(QW[:], QWs.rearrange("h p n -> p h n"))
        load(KW[:], KWs.rearrange("h p n -> p h n"))
        load(VW[:], VWs[:, :, :])
        load(BM[:], bm_d[:, :])
        load(sk[:], sk_d[:, :])
        act(None, esk[:], sk[:], AF.Exp)
        memset("dve", zer[:], 0.0)
        for hp in range(4):
            ts("dve", ESB[:, hp, :], zer[:], esk[:, hp:hp + 1], ALU.add)

        for qi, i in enumerate(range(3, (NH + 1) * 4)):
            s = qi % 2
            qcols = slice(i * 128, (i + 1) * 128)
            for kbsel in range(2):
                ib = i - 1 + kbsel
                for h in range(8):
                    po = (h % 2) * 64
                    g = h // 4
                    hp = h // 2
                    S.op("pe", mm(ZS[:, kbsel, h % 2, hp, :], KW[po:po + 64, g, ib * 128:(ib + 1) * 128],
                                  QW[po:po + 64, hp, qcols], True, True),
                         reads=[KW.buf, QW.buf], writes=[ZS.buf])
            stt(LG[s][:], V(ZS.buf, ZS.t[:].rearrange("p a r h q -> p (a r h q)")), 0.125, BM[:], ALU.mult, ALU.add)
            for kbsel in range(2):
                lb = (NBH - 4) + i - 1 + kbsel
                act(None, V(PB[s].buf, PB[s].t[:, kbsel, :, :, :].rearrange("p r h q -> p (r h q)")),
                    LG[s][:, kbsel * 1024:(kbsel + 1) * 1024], AF.Exp, bias=kbias[:, lb:lb + 1])
            for hp in range(4):
                g = hp // 2
                prs = []
                prd = []
                for kbsel in range(2):
                    ib = i - 1 + kbsel
                    for r in range(2):
                        h = 2 * hp + r
                        var = 2 * g + r
                        prs.append((VW[:, ib, 128 * var:128 * (var + 1)], PB[s][:, kbsel, r, hp, :]))
                        prd.append((opad[:, r, :], PB[s][:, kbsel, r, hp, :]))
                pe_group(OP[:, hp, :], prs)
                pe_group(DN[:, hp, :], prd)
            tt("dve", den[:], V(DN.buf, DN.t[:].rearrange("p a q -> p (a q)")),
               V(ESB.buf, ESB.t[:].rearrange("p a q -> p (a q)")), ALU.add)
            act(None, lden[:], den[:], AF.Ln)
            act(None, rec[:], lden[:], AF.Exp, scale=-1.0)
            tt("dve", V(yst[s].buf, yst[s].t[:].rearrange("p a q -> p (a q)")),
               V(OP.buf, OP.t[:].rearrange("p a q -> p (a q)")), rec[:], ALU.mult)
            store(YSW[:, :, qi * 128:(qi + 1) * 128], yst[s][:])
        S.barrier()
        P.close()

    def phase_B2():
        P = Pool_()
        KTt = [P.sb(f"KTt{i}", [128, L2], BF16) for i in range(2)]
        Vt = [P.sb(f"Vt{i}", [128, NB2, 256], BF16) for i in range(2)]
        QTt = [P.sb(f"QTt{i}", [128, NOWN], BF16) for i in range(2)]
        NBUF = 4
        Eb = [P.sb(f"Eb{i}", [128, 2, 512], BF16) for i in range(NBUF)]
        Lb = [P.sb(f"Lb{i}", [128, 2, 512], BF16) for i in range(NBUF)]
        Gb = [P.sb(f"Gb{i}", [128, 2, 512], BF16) for i in range(NBUF)]
        Wb = [P.sb(f"Wb{i}", [128, 2, 512], BF16) for i in range(NBUF)]
        triC = P.sb("triC", [128, 128], BF16)
        yev = [P.sb(f"yev{i}", [128, 512], BF16) for i in range(2)]
        Z = [P.ps(f"Zp{i}", [128, 2, 512]) for i in range(2)]
        X = P.ps("Xp", [128, 2, 512])
        Y = [P.ps(f"Yp{i}", [128, 512]) for i in range(2)]
        tt("dve", triC[:], ones[:], tri[:], ALU.subtract)
        zb = P.sb("zbB2", [128, 512], BF16)
        memset("dve", zb[:], 0.0)

        def load_hp(hp):
            s = hp % 2
            hl = L2 // 2
            load(KTt[s][:, 0:hl], KTs[hp, :, 0:hl])
            load(KTt[s][:, hl:L2], KTs[hp, :, hl:L2])
            load(Vt[s][:, 0:NB2 // 2, :], VSs[hp, :, 0:NB2 // 2, :])
            load(Vt[s][:, NB2 // 2:NB2, :], VSs[hp, :, NB2 // 2:NB2, :])
            load(QTt[s][:], QTs[hp, :, :])

        class Grp:
            pass

        def make_group(hp, qs, n, ycol, gidx, ubase):
            G = Grp()
            sl = hp % 2
            Kt, Vv, Qt = KTt[sl], Vt[sl], QTt[sl]
            qc = qs - (NH - 1) * 512
            kbmax = (qs + n - 2) // 128
            kbs = list(range(kbmax, -1, -1))
            U = len(kbs)
            G.U = U
            Yp = Y[gidx % 2]
            ye = yev[gidx % 2]

            def c0of(u):
                j = kbs[u] - qs // 128
                return 128 * j if (n == 512 and j >= 1) else 0

            def mmx(out, lhsT, rhs, start):
                return lambda e: e.matmul(out.ap, lhsT.ap, rhs.ap, start=start, stop=True,
                                          skip_group_check=True)

            def s_Z(u):
                kb = kbs[u]
                j = kb - qs // 128
                c0 = c0of(u)
                Zt = Z[(ubase + u) % 2]
                for r in range(2):
                    po = 64 * r
                    prs = [(Kt[po:po + 64, kb * 128:(kb + 1) * 128], Qt[po:po + 64, qc + c0:qc + n])]
                    if j >= 0:
                        if n == 512:
                            prs.append((ident[:], dmask[:, j, c0:n]))
                        else:
                            prs.append((ident[:], dmask[:, 0, 126:128]))
                    pe_group(Zt[:, r, c0:n], prs)

            def s_EL(u):
                kb = kbs[u]
                b = (ubase + u) % NBUF
                c0 = c0of(u)
                act(None, Eb[b][:, :, c0:n], Z[(ubase + u) % 2][:, :, c0:n], AF.Exp, scale=0.125,
                    bias=kbias[:, kb:kb + 1])
                act(None, Lb[b][:, :, c0:n], Eb[b][:, :, c0:n], AF.Ln, bias=1.0)

            def s_A(u):
                b = (ubase + u) % NBUF
                c0 = c0of(u)
                fns = [mmx(X[:, r, c0:n], tri[:], Lb[b][:, r, c0:n], False) for r in range(2)]
                S.op("pe", fns, reads=[tri.buf, Lb[b].buf], writes=[X.buf])

            def s_B(u):
                b = (ubase + u) % NBUF
                c0 = c0of(u)
                fns = [mmx(X[:, r, c0:n], triC[:], Lb[b][:, r, c0:n], False) for r in range(2)]
                S.op("pe", fns, reads=[triC.buf, Lb[b].buf], writes=[X.buf])

            def s_G(u):
                b = (ubase + u) % NBUF
                c0 = c0of(u)
                act(None, Gb[b][:, :, c0:n], X[:, :, c0:n], AF.Exp, scale=-1.0)
                tt("dve", Wb[b][:, :, c0:n], Eb[b][:, :, c0:n], Gb[b][:, :, c0:n], ALU.mult)

            def s_Y(u):
                kb = kbs[u]
                b = (ubase + u) % NBUF
                c0 = c0of(u)
                fns = []
                for r in range(2):
                    fns.append(mmx(Yp[:, c0:n], Vv[:, kb, 128 * r:128 * (r + 1)], Wb[b][:, r, c0:n], False))
                S.op("pe", fns, reads=[Vv.buf, Wb[b].buf], writes=[Yp.buf])

            def pre():
                s_Z(0)
                if U > 1:
                    s_Z(1)
                s_EL(0)
                if U > 2:
                    s_Z(2)

            def run(nxt):
                S.op("pe", [mmx(X[:, r, 0:n], ident[:], zb[:, 0:n], True) for r in range(2)],
                     reads=[ident.buf, zb.buf], writes=[X.buf])
                S.op("pe", [mmx(Yp[:, 0:n], ident[:], zb[:, 0:n], True)],
                     reads=[ident.buf, zb.buf], writes=[Yp.buf])
                s_A(0)
                for t in range(U):
                    if t + 1 < U:
                        s_EL(t + 1)
                    if t == U - 1 and nxt is not None:
                        nxt.pre()
                    s_G(t)
                    if t + 1 < U:
                        s_B(t)
                        s_A(t + 1)
                    if t + 3 < U:
                        s_Z(t + 3)
                    if t >= 1:
                        s_Y(t - 1)
                s_Y(U - 1)
                cp("dve", ye[:, 0:n], Yp[:, 0:n])
                store(YSB[:, hp, ycol:ycol + n], ye[:, 0:n])

            G.pre = pre
            G.run = run
            return G

        glist = []
        ub = 0
        for hp in range(4):
            for (qs, n, ycol) in groups:
                g = make_group(hp, qs, n, ycol, len(glist), ub)
                ub += g.U
                g.hp = hp
                glist.append(g)
        load_hp(0)
        load_hp(1)
        glist[0].pre()
        for i, g in enumerate(glist):
            nxt = glist[i + 1] if i + 1 < len(glist) else None
            if nxt is not None and nxt.hp != g.hp and nxt.hp + 1 < 4:
                pass
            g.run(nxt)
            if nxt is not None and nxt.hp != g.hp and nxt.hp + 1 < 4:
                load_hp(nxt.hp + 1)
        S.barrier()
        P.close()

    def load_w_bf16(P, W, src, nk, ncol, gvec, stg, piece):
        i = 0
        for k in range(nk):
            for c0 in range(0, ncol, piece):
                c1 = min(ncol, c0 + piece)
                st = stg[i % len(stg)]
                eng = "dve" if i % 2 == 0 else "act"
                i += 1
                load(st[:, 0:c1 - c0], src[k * 128:(k + 1) * 128, c0:c1])
                if gvec is None:
                    cp(eng, W[:, k, c0:c1], st[:, 0:c1 - c0])
                elif eng == "act":
                    act(None, W[:, k, c0:c1], st[:, 0:c1 - c0], AF.Identity, scale=gvec[:, k:k + 1])
                else:
                    ts(eng, W[:, k, c0:c1], st[:, 0:c1 - c0], gvec[:, k:k + 1], ALU.mult)

    def phase_C1():
        P = Pool_()
        WG = P.sb("WG", [128, 8, 2048], BF16)
        WSB = P.sb("WSB", [128, 4, 1024], BF16)
        WSW = P.sb("WSW", [128, 4, 1024], BF16)
        WO = P.sb("WO", [128, 8, 1024], BF16)
        stg = [P.sb(f"stgC1{i}", [128, 1024], F32) for i in range(4)]
        gm = P.sb("gm_c1", [128, 8], F32)
        xs = [P.sb(f"xsC{i}", [128, 8, 512], F32) for i in range(2)]
        sq = P.sb("sqC", [128, 8, 512], BF16)
        hT = [P.sb(f"hTC{i}", [128, 8, 512], BF16) for i in range(2)]
        lnv = P.sb("lnvC", [128, 512], F32)
        rs = P.sb("rsC", [128, 512], F32)
        ysb = [P.sb(f"ysbC{i}", [128, 4, 512], BF16) for i in range(2)]
        ysw = [P.sb(f"yswC{i}", [128, 4, 512], BF16) for i in range(2)]
        g1 = [P.sb(f"g1C{i}", [128, 512], F32) for i in range(2)]
        g2 = [P.sb(f"g2C{i}", [128, 512], F32) for i in range(2)]
        t1 = [P.sb(f"t1C{i}", [128, 512], F32) for i in range(2)]
        t2 = [P.sb(f"t2C{i}", [128, 512], F32) for i in range(2)]
        mT = P.sb("mT", [128, 8, 512], BF16)
        x1 = P.sb("x1C", [128, 8, 512], F32)
        ssps = P.ps("ssC", [128, 512])
        PS = [P.ps(f"psC{i}", [128, 512]) for i in range(7)]

        load(gm[:], gm_d[:, :])
        load_w_bf16(P, WG, w_in[:, 2304:4352], 8, 2048, gm, stg, 1024)
        load_w_bf16(P, WSB, w_sbp, 4, 1024, None, stg, 1024)
        load_w_bf16(P, WSW, w_swp, 4, 1024, None, stg, 1024)
        load_w_bf16(P, WO, w_out, 8, 1024, None, stg, 1024)
        psi = [0]

        def nextps():
            psi[0] = (psi[0] + 1) % 7
            return PS[psi[0]]

        def prologue(gi_):
            qs, n, oc = groups[gi_]
            s = gi_ % 2
            x = xs[s]
            load(x[:, :, 0:n], xT[:, qs:qs + n].rearrange("(k p) n -> p k n", p=128))
            load(ysb[s][:, :, 0:n], YSB[:, :, oc:oc + n])
            swc = qs - (H - 128)
            load(ysw[s][:, :, 0:n], YSW[:, :, swc:swc + n])
            rms_stats(P, x, n, sq, ssps, lnv, rs)
            for k in range(8):
                tt("dve" if k % 2 == 0 else "pool", hT[s][:, k, 0:n], x[:, k, 0:n], rs[:, 0:n], ALU.mult)

        def main(gi_):
            qs, n, oc = groups[gi_]
            s = gi_ % 2
            x = xs[s]
            h = hT[s]
            for o in range(8):
                pg1 = nextps()
                pe_group(pg1[:, 0:n], [(WG[:, k, o * 128:(o + 1) * 128], h[:, k, 0:n]) for k in range(8)])
                act(None, g1[o % 2][:, 0:n], pg1[:, 0:n], AF.Sigmoid)
                pg2 = nextps()
                pe_group(pg2[:, 0:n], [(WG[:, k, (8 + o) * 128:(9 + o) * 128], h[:, k, 0:n]) for k in range(8)])
                act(None, g2[o % 2][:, 0:n], pg2[:, 0:n], AF.Sigmoid)
                pa = nextps()
                pe_group(pa[:, 0:n], [(WSB[:, k, o * 128:(o + 1) * 128], ysb[s][:, k, 0:n]) for k in range(4)])
                pb = nextps()
                pe_group(pb[:, 0:n], [(WSW[:, k, o * 128:(o + 1) * 128], ysw[s][:, k, 0:n]) for k in range(4)])
                tt("dve", t1[o % 2][:, 0:n], pa[:, 0:n], g1[o % 2][:, 0:n], ALU.mult)
                tt("dve", t2[o % 2][:, 0:n], pb[:, 0:n], g2[o % 2][:, 0:n], ALU.mult)
                tt("pool", mT[:, o, 0:n], t1[o % 2][:, 0:n], t2[o % 2][:, 0:n], ALU.add)
            for o in range(8):
                ps = nextps()
                pe_group(ps[:, 0:n], [(WO[:, k, o * 128:(o + 1) * 128], mT[:, k, 0:n]) for k in range(8)])
                tt("dve", x1[:, o, 0:n], ps[:, 0:n], x[:, o, 0:n], ALU.add)
            store(X1s[:, :, oc:oc + n], x1[:, :, 0:n])

        prologue(0)
        for gi_ in range(len(groups)):
            if gi_ + 1 < len(groups):
                prologue(gi_ + 1)
            main(gi_)
        S.barrier()
        P.close()

    def phase_C2a():
        P = Pool_()
        WU = P.sb("WU", [128, 8, 2 * DFF], BF16)
        stg = [P.sb(f"stgU{i}", [128, 1408], F32) for i in range(4)]
        gf = P.sb("gf_s", [128, 8], F32)
        cw = P.sb("cw_s", [128, 44, 3], F32)
        cb = P.sb("cb_s", [128, 44], F32)
        xs = P.sb("xsU", [128, 8, 512], F32)
        sq = P.sb("sqU", [128, 8, 512], BF16)
        hT = [P.sb(f"hTU{i}", [128, 8, 512], BF16) for i in range(2)]
        lnv = P.sb("lnvU", [128, 512], F32)
        rs = P.sb("rsU", [128, 512], F32)
        yb = [P.sb(f"ybU{i}", [128, 512], F32) for i in range(4)]
        sg = [P.sb(f"sgU{i}", [128, 512], F32) for i in range(2)]
        aT = P.sb("aTU", [128, 22, 512], BF16)
        ssps = P.ps("ssU", [128, 512])
        PS = [P.ps(f"psU{i}", [128, 512]) for i in range(7)]

        load(gf[:], gf_d[:, :])
        load(cw[:], cw_d.rearrange("p (j k) -> p j k", k=3))
        load(cb[:], cb_d[:, :])
        load_w_bf16(P, WU, w_up, 8, 2 * DFF, gf, stg, 1408)
        psi = [0]

        def nextps():
            psi[0] = (psi[0] + 1) % 7
            return PS[psi[0]]

        GW = 510
        NG = (H + GW - 1) // GW

        def geo(g):
            a0 = g * GW
            w = min(GW, H - a0)
            return a0, w, w + 2

        def prologue(g):
            a0, w, n = geo(g)
            load(xs[:, :, 0:n], X1s[:, :, a0:a0 + n])
            rms_stats(P, xs, n, sq, ssps, lnv, rs)
            for k in range(8):
                tt("dve" if k % 2 == 0 else "pool", hT[g % 2][:, k, 0:n], xs[:, k, 0:n], rs[:, 0:n], ALU.mult)

        ybi = [0]

        def tiles(g):
            a0, w, n = geo(g)
            h = hT[g % 2]
            for j in range(22):
                ys = []
                for t in (j, 22 + j):
                    ps = nextps()
                    pe_group(ps[:, 0:n], [(WU[:, k, t * 128:(t + 1) * 128], h[:, k, 0:n]) for k in range(8)])
                    y = yb[ybi[0] % 4]
                    ybi[0] += 1
                    ys.append(y)
                    act(None, y[:, 0:w], ps[:, 2:n], AF.Identity, scale=cw[:, t, 2:3], bias=cb[:, t:t + 1])
                    stt(y[:, 0:w], ps[:, 1:n - 1], cw[:, t, 1:2], y[:, 0:w], ALU.mult, ALU.add)
                    stt(y[:, 0:w], ps[:, 0:w], cw[:, t, 0:1], y[:, 0:w], ALU.mult, ALU.add)
                sgt = sg[j % 2]
                act(None, sgt[:, 0:w], ys[0][:, 0:w], AF.Silu)
                tt("pool", aT[:, j, 0:w], sgt[:, 0:w], ys[1][:, 0:w], ALU.mult)
            store(ATs[:, :, 2 + a0:2 + a0 + w], aT[:, :, 0:w])

        prologue(0)
        for g in range(NG):
            if g + 1 < NG:
                prologue(g + 1)
            tiles(g)
        S.barrier()
        P.close()

    def phase_C2b():
        P = Pool_()
        WD = P.sb("WD", [128, 22, 1024], BF16)
        stg = [P.sb(f"stgD{i}", [128, 1024], F32) for i in range(4)]
        glf = P.sb("gl_s", [128, 8], F32)
        xs = [P.sb(f"xsD{i}", [128, 8, 512], F32) for i in range(2)]
        aT = [P.sb(f"aTD{i}", [128, 22, 512], BF16) for i in range(2)]
        x2 = P.sb("x2D", [128, 8, 512], F32)
        sq = P.sb("sqD", [128, 8, 512], BF16)
        lnv = P.sb("lnvD", [128, 512], F32)
        rs = P.sb("rsD", [128, 512], F32)
        ob = [P.sb(f"obD{i}", [128, 8, 512], F32) for i in range(2)]
        ssps = P.ps("ssD", [128, 512])
        PS = [P.ps(f"psD{i}", [128, 512]) for i in range(6)]

        load(glf[:], gl_d[:, :])
        load_w_bf16(P, WD, w_down, 22, 1024, None, stg, 1024)
        psi = [0]

        def nextps():
            psi[0] = (psi[0] + 1) % 6
            return PS[psi[0]]

        for gi_, (qs, n, oc) in enumerate(groups[1:]):
            s = gi_ % 2
            x = xs[s]
            load(x[:], X1s[:, :, oc:oc + n])
            load(aT[s][:], ATs[:, :, oc:oc + n])
            for o in range(8):
                ps = nextps()
                pe_group(ps[:], [(WD[:, k, o * 128:(o + 1) * 128], aT[s][:, k, :]) for k in range(22)])
                tt("dve", x2[:, o, :], ps[:], x[:, o, :], ALU.add)
            rms_stats(P, x2, 512, sq, ssps, lnv, rs)
            for o in range(8):
                stt(ob[s][:, o, :], x2[:, o, :], glf[:, o:o + 1], rs[:], ALU.mult, ALU.mult)
            store(out_d[:, :, oc - 2:oc - 2 + n].rearrange("k p n -> p k n"), ob[s][:])
        S.barrier()
        P.close()

    S.barrier()
    gtmp.close()
    ph = phases.split(",")
    if "A" in ph:
        phase_A()
    if "B1" in ph:
        phase_B1()
    if "B2" in ph:
        phase_B2()
    if "C1" in ph:
        phase_C1()
    if "C2a" in ph:
        phase_C2a()
    if "C2b" in ph:
        phase_C2b()
    gl.close()

    with ExitStack() as es:
        sems = {}
        for k in S.keys:
            nm = k if isinstance(k, str) else f"d{k[1]}"
            sems[k] = es.enter_context(nc.semaphore("s_" + nm))
        block = es.enter_context(nc.Block())

        @block.tensor
        def _(e):
            S.replay("pe", e, sems)

        @block.scalar
        def _(e):
            S.replay("act", e, sems)

        @block.vector
        def _(e):
            S.replay("dve", e, sems)

        @block.gpsimd
        def _(e):
            S.replay("pool", e, sems)

        @block.sync
        def _(e):
            S.replay("sp", e, sems)

    return nc


def _t5_bucket(dist):
    dist = np.asarray(dist, np.int32)
    max_exact = 16
    d = np.maximum(dist, 1).astype(np.float32)
    large = max_exact + (np.log(d / np.float32(max_exact)) / np.float32(np.log(128 / max_exact))
                         * np.float32(32 - max_exact)).astype(np.int32)
    large = np.minimum(large, 31)
    return np.where(dist < max_exact, dist, large)


def _consts():
    c = np.zeros((128, 2688), np.float32)
    c[:, 0:128] = 1.0
    j = np.arange(128)[:, None]
    s = np.arange(128)[None, :]
    c[:, 128:256] = (j >= s).astype(np.float32)
    c[:, 256:384] = np.eye(128, dtype=np.float32)
    q = np.arange(512)[None, :]
    for jj in range(4):
        valid = (128 * jj + j) < q
        c[:, 384 + 512 * jj:384 + 512 * (jj + 1)] = np.where(valid, 0.0, 8.0 * NEGM)
    c[:, 2432:2432 + 64] = 1.0
    c[:, 2432 + 128 + 64:2432 + 256] = 1.0
    return c


def prep_inputs(NH, x, g_mix, w_in, w_sb_proj, w_sw_proj, w_out, rel_bias, sinks,
                g_ffn, w_up, conv_w, conv_b, w_down, g_final):
    H = NH * 512
    B = x.shape[0]
    f = lambda a: np.ascontiguousarray(np.asarray(a, dtype=np.float32))
    x = f(x)
    lay8 = lambda g: f(np.asarray(g, np.float32).reshape(8, 128).T)
    rel_bias = np.asarray(rel_bias, np.float32)
    k = np.arange(128)[:, None]
    q = np.arange(128)[None, :]
    bm = np.zeros((128, 2, 2, 4, 128), np.float32)
    d0 = 128 + q - k
    d1 = q - k
    b0 = _t5_bucket(np.clip(d0, 0, 255))
    b1 = _t5_bucket(np.clip(d1, 0, 255))
    for h in range(8):
        bm[:, 0, h % 2, h // 2, :] = np.where(d0 <= 127, rel_bias[b0, h], NEGM)
        bm[:, 1, h % 2, h // 2, :] = np.where(d1 >= 0, rel_bias[b1, h], NEGM)
    sinks = np.asarray(sinks, np.float32)
    sk = np.zeros((128, 4), np.float32)
    for hp in range(4):
        sk[0:64, hp] = sinks[2 * hp]
        sk[64:128, hp] = sinks[2 * hp + 1]
    cw = np.asarray(conv_w, np.float32)
    cwl = np.ascontiguousarray(cw.reshape(3, 44, 128).transpose(2, 1, 0)).reshape(128, 132)
    cbl = f(np.asarray(conv_b, np.float32).reshape(44, 128).T)
    shared = {
        "w_in": f(w_in), "w_sbp": f(w_sb_proj), "w_swp": f(w_sw_proj), "w_out": f(w_out),
        "w_up": f(w_up), "w_down": f(w_down), "gm": lay8(g_mix), "gf": lay8(g_ffn), "gl": lay8(g_final),
        "cw": f(cwl), "cb": cbl, "bm": f(bm.reshape(128, 2048)), "sk": sk, "cst": _consts(),
    }
    in_maps = []
    for c in range(2 * B):
        b, p = c // 2, c % 2
        xl = np.zeros((2 * H, D), np.float32)
        kb = np.zeros((128, 2 * H // 128), np.float32)
        if p == 1:
            xl[:] = x[b]
        else:
            xl[H:] = x[b, :H]
            kb[:, :H // 128] = NEGM
        m = dict(shared)
        m["xT"] = np.ascontiguousarray(xl.T)
        m["kbias"] = kb
        in_maps.append(m)
    return in_maps


_NC_CACHE = {}


def run(NH, phases="A,B1,B2,C1,C2a,C2b", **inputs):
    x = np.asarray(inputs["x"])
    B = x.shape[0]
    H = NH * 512
    in_maps = prep_inputs(NH, **inputs)
    if (NH, phases) not in _NC_CACHE:
        _NC_CACHE[(NH, phases)] = build_nc(NH, phases)
    nc = _NC_CACHE[(NH, phases)]
    res = run_bass_kernel_spmd(nc, in_maps, core_ids=list(range(2 * B)))
    out = np.zeros((B, 2 * H, D), np.float32)
    for c in range(2 * B):
        b, p = c // 2, c % 2
        o = np.asarray(res.results[c]["out"]).reshape(D, H)
        out[b, p * H:(p + 1) * H, :] = o.T
    return out


def kernel(**inputs):
    return run(8, **inputs)
```

```python
import numpy as np
from contextlib import ExitStack
import concourse.bass as bass
import concourse.mybir as mybir
from concourse.bass_utils import run_bass_kernel_spmd

F32 = mybir.dt.float32
BF16 = mybir.dt.bfloat16
AF = mybir.ActivationFunctionType
ALU = mybir.AluOpType

D = 1024
KT = 8
DFF = 2816
NEGM = -100.0
ENGS = ("pe", "act", "dve", "pool", "sp")
NDMA = 24
NDMA_HW = 16


class Buf:
    __slots__ = ("w", "r", "name")

    def __init__(self, name=""):
        self.w = None
        self.r = {}
        self.name = name


class V:
    __slots__ = ("buf", "ap")

    def __init__(self, buf, ap):
        self.buf = buf
        self.ap = ap


class Tile:
    def __init__(self, handle, name=""):
        self.t = handle
        self.buf = Buf(name)

    def __getitem__(self, idx):
        return V(self.buf, self.t[idx])


class Sched:
    def __init__(self):
        self.keys = list(ENGS) + [("d", i) for i in range(NDMA)]
        self.cnt = {k: 0 for k in self.keys}
        self.seen = {e: {k: 0 for k in self.keys} for e in ENGS}
        self.q = {e: [] for e in ENGS}
        self.dma_rr = 0
        self.dma_rr2 = 0
        self.nops = 0

    def op(self, eng, fns, reads=(), writes=(), dma=False):
        if callable(fns):
            fns = [fns]
        need = {}

        def req(tok):
            if tok is None:
                return
            k, v = tok
            if need.get(k, 0) < v:
                need[k] = v

        for b in reads:
            req(b.w)
        for b in writes:
            req(b.w)
            for k, v in b.r.items():
                req((k, v))
        if dma:
            if eng == "sp":
                key = ("d", self.dma_rr)
                self.dma_rr = (self.dma_rr + 1) % NDMA_HW
            else:
                key = ("d", NDMA_HW + self.dma_rr2)
                self.dma_rr2 = (self.dma_rr2 + 1) % (NDMA - NDMA_HW)
            req((key, self.cnt[key]))
            inc = 16
        else:
            key = eng
            inc = 1
        waits = []
        seen = self.seen[eng]
        for k, v in need.items():
            if v <= 0 or seen[k] >= v:
                continue
            if k == eng and eng == "pe":
                continue
            seen[k] = v
            waits.append((k, v))
        self.cnt[key] += inc
        tok = (key, self.cnt[key])
        for b in writes:
            b.w = tok
            b.r = {}
        for b in reads:
            if b.r.get(key, 0) < tok[1]:
                b.r[key] = tok[1]
        self.q[eng].append((waits, fns, key, inc))
        self.nops += 1
        return tok

    def barrier(self):
        for e in ENGS:
            waits = []
            for k in self.keys:
                v = self.cnt[k]
                if v > 0 and self.seen[e][k] < v and k != e:
                    self.seen[e][k] = v
                    waits.append((k, v))
            if waits:
                self.q[e].append((waits, [], None, 0))

    def replay(self, eng, e, sems):
        for waits, fns, key, inc in self.q[eng]:
            for k, v in waits:
                e.wait_ge(sems[k], v)
            ins = None
            for f in fns:
                ins = f(e)
            if ins is not None and key is not None:
                ins.then_inc(sems[key], inc)


def build_nc(NH, phases="A,B1,B2,C1,C2a,C2b"):
    H = NH * 512
    L2 = 2 * H
    NB2 = L2 // 128
    NBH = H // 128
    NOWN = (NH + 1) * 512
    NY = 2 + NH * 512
    NSWB = NH * 4 + 1

    nc = bass.Bass("TRN2", target_bir_lowering=False)
    S = Sched()

    def din(name, shape):
        return nc.dram_tensor(name, shape, F32, kind="ExternalInput").ap()

    xT = din("xT", [D, L2])
    kbias_d = din("kbias", [128, NB2])
    w_in = din("w_in", [D, 4352])
    w_sbp = din("w_sbp", [512, D])
    w_swp = din("w_swp", [512, D])
    w_out = din("w_out", [D, D])
    w_up = din("w_up", [D, 2 * DFF])
    w_down = din("w_down", [DFF, D])
    gm_d = din("gm", [128, 8])
    gf_d = din("gf", [128, 8])
    gl_d = din("gl", [128, 8])
    cw_d = din("cw", [128, 44 * 3])
    cb_d = din("cb", [128, 44])
    bm_d = din("bm", [128, 2048])
    sk_d = din("sk", [128, 4])
    cst_d = din("cst", [128, 2688])
    out_d = nc.dram_tensor("out", [8, 128, H], F32, kind="ExternalOutput").ap()

    def dscr(name, shape, dt):
        return nc.dram_tensor(name, shape, dt, kind="Internal").ap()

    KTs = dscr("KTs", [4, 128, L2], BF16)
    QTs = dscr("QTs", [4, 128, NOWN], BF16)
    VSs = dscr("VSs", [4, 128, NB2, 256], BF16)
    QWs = dscr("QWs", [4, 128, NOWN], BF16)
    KWs = dscr("KWs", [2, 128, NOWN], BF16)
    VWs = dscr("VWs", [128, (NH + 1) * 4, 512], BF16)
    YSB = dscr("YSB", [128, 4, NY], BF16)
    YSW = dscr("YSW", [128, 4, NSWB * 128], BF16)
    X1s = dscr("X1s", [128, 8, NY], F32)
    ATs = dscr("ATs", [128, 22, NY], BF16)

    groups = [(H - 2, 2, 0)] + [(H + 512 * c, 512, 2 + 512 * c) for c in range(NH)]

    def mm(out, lhsT, rhs, start, stop):
        return lambda e: e.matmul(out.ap, lhsT.ap, rhs.ap, start=start, stop=stop)

    def pe_group(out, pairs, extra_reads=()):
        fns = []
        n = len(pairs)
        rd = []
        for i, (l, r) in enumerate(pairs):
            fns.append(mm(out, l, r, i == 0, i == n - 1))
            rd += [l.buf, r.buf]
        S.op("pe", fns, reads=rd + list(extra_reads), writes=[out.buf])

    def act(eng_unused, out, in_, func, scale=1.0, bias=0.0, extra_reads=()):
        def f(e):
            kw = {}
            if isinstance(bias, V):
                kw["bias"] = bias.ap
            else:
                kw["bias"] = float(bias)
            if isinstance(scale, V):
                kw["scale"] = scale.ap
            else:
                kw["scale"] = float(scale)
            return e.activation(out.ap, in_.ap, func, **kw)

        rd = [in_.buf] + [x.buf for x in (scale, bias) if isinstance(x, V)] + list(extra_reads)
        S.op("act", f, reads=rd, writes=[out.buf])

    def tt(eng, out, a, b, op):
        S.op(eng, lambda e: e.tensor_tensor(out.ap, a.ap, b.ap, op), reads=[a.buf, b.buf], writes=[out.buf])

    def ts(eng, out, a, s1, op0, s2=None, op1=None):
        def f(e):
            sc1 = s1.ap if isinstance(s1, V) else float(s1)
            if s2 is None:
                return e.tensor_scalar(out.ap, a.ap, sc1, None, op0)
            sc2 = s2.ap if isinstance(s2, V) else float(s2)
            return e.tensor_scalar(out.ap, a.ap, sc1, sc2, op0, op1)

        rd = [a.buf] + [x.buf for x in (s1, s2) if isinstance(x, V)]
        S.op(eng, f, reads=rd, writes=[out.buf])

    def stt(out, a, sc, b, op0, op1):
        def f(e):
            s = sc.ap if isinstance(sc, V) else float(sc)
            return e.scalar_tensor_tensor(out.ap, a.ap, s, b.ap, op0, op1)

        rd = [a.buf, b.buf] + ([sc.buf] if isinstance(sc, V) else [])
        S.op("dve", f, reads=rd, writes=[out.buf])

    def cp(eng, out, a):
        if eng == "act":
            act(None, out, a, AF.Copy)
        else:
            S.op(eng, lambda e: e.tensor_copy(out.ap, a.ap), reads=[a.buf], writes=[out.buf])

    def memset(eng, out, val):
        S.op(eng, lambda e: e.memset(out.ap, val), writes=[out.buf])

    def load(out, src_ap, eng="sp"):
        S.op(eng, lambda e: e.dma_start(out=out.ap, in_=src_ap), writes=[out.buf], dma=True)

    def store(dst_ap, src, eng="pool"):
        S.op(eng, lambda e: e.dma_start(out=dst_ap, in_=src.ap), reads=[src.buf], dma=True)

    class Pool_:
        def __init__(self):
            self.es = ExitStack()

        def sb(self, name, shape, dt):
            return Tile(self.es.enter_context(nc.sbuf_tensor(name, shape, dt)), name)

        def ps(self, name, shape):
            return Tile(self.es.enter_context(nc.psum_tensor(name, shape, F32)), name)

        def close(self):
            self.es.close()

    gl = Pool_()
    ones = gl.sb("ones", [128, 128], BF16)
    tri = gl.sb("tri", [128, 128], BF16)
    ident = gl.sb("ident", [128, 128], BF16)
    dmask = gl.sb("dmask", [128, 4, 512], BF16)
    opad = gl.sb("opad", [128, 2, 128], BF16)
    kbias = gl.sb("kbias_s", [128, NB2], F32)
    gtmp = Pool_()
    cstf = gtmp.sb("cstf", [128, 2688], F32)

    load(cstf[:], cst_d[:, :])
    load(kbias[:], kbias_d[:, :])
    cp("dve", ones[:], cstf[:, 0:128])
    cp("dve", tri[:], cstf[:, 128:256])
    cp("dve", ident[:], cstf[:, 256:384])
    for j in range(4):
        cp("dve", dmask[:, j, :], cstf[:, 384 + 512 * j:384 + 512 * (j + 1)])
    for j in range(2):
        cp("dve", opad[:, j, :], cstf[:, 2432 + 128 * j:2432 + 128 * (j + 1)])

    def rms_stats(P, xs, n, sq, ssps, lnv, rs):
        act(None, sq[:, :, 0:n], xs[:, :, 0:n], AF.Square)
        pe_group(ssps[:, 0:n], [(ones[:], sq[:, k, 0:n]) for k in range(8)])
        act(None, lnv[:, 0:n], ssps[:, 0:n], AF.Ln, scale=1.0 / D, bias=1e-6)
        act(None, rs[:, 0:n], lnv[:, 0:n], AF.Exp, scale=-0.5)

    def phase_A():
        P = Pool_()
        WA = P.sb("WA", [128, 8, 3328], BF16)
        stg = [P.sb(f"wstg{i}", [128, 2304], F32) for i in range(2)]
        gm = P.sb("gm_s", [128, 8], F32)
        xs = [P.sb(f"xsA{i}", [128, 8, 512], F32) for i in range(2)]
        sq = P.sb("sqA", [128, 8, 512], BF16)
        hT = [P.sb(f"hTA{i}", [128, 8, 512], BF16) for i in range(2)]
        lnv = P.sb("lnvA", [128, 512], F32)
        rs = [P.sb(f"rsA{i}", [128, 512], F32) for i in range(2)]
        stF = [P.sb(f"stF{i}", [128, 14, 512], BF16) for i in range(2)]
        stV = [P.sb(f"stV{i}", [128, 4, 1536], BF16) for i in range(2)]
        ssps = P.ps("ssA", [128, 512])
        PS = [P.ps(f"psA{i}", [128, 512]) for i in range(6)]

        load(gm[:], gm_d[:, :])
        memset("pool", WA[:, :, 1792:3328], 0.0)
        for kt in range(8):
            st = stg[kt % 2]
            load(st[:], w_in[kt * 128:(kt + 1) * 128, 0:2304])
            g = gm[:, kt:kt + 1]
            e1 = "dve"
            act(None, WA[:, kt, 0:1024], st[:, 0:1024], AF.Identity, scale=g)
            ts(e1, WA[:, kt, 1024:1536], st[:, 1536:2048], g, ALU.mult)
            for gg in range(2):
                for r in range(2):
                    ts(e1, WA[:, kt, 1536 + 128 * gg + 64 * r:1536 + 128 * gg + 64 * (r + 1)],
                       st[:, 2048 + 64 * gg:2048 + 64 * (gg + 1)], g, ALU.mult)
            for h in range(8):
                o = 1792 + 128 * h + 64 * (h % 2)
                ts(e1, WA[:, kt, o:o + 64], st[:, 1024 + 64 * h:1024 + 64 * (h + 1)], g, ALU.mult)
            for i in range(4):
                o = 2816 + 128 * i + 64 * (i % 2)
                gg = i // 2
                ts(e1, WA[:, kt, o:o + 64], st[:, 2176 + 64 * gg:2176 + 64 * (gg + 1)], g, ALU.mult)

        psi = [0]

        def nextps():
            psi[0] = (psi[0] + 1) % 6
            return PS[psi[0]]

        evi = [0]

        def evac(out, in_):
            evi[0] += 1
            cp("act" if evi[0] % 2 == 0 else "dve", out, in_)

        def stage1(tc):
            s = tc % 2
            load(xs[s][:], xT[:, tc * 512:(tc + 1) * 512].rearrange("(k p) n -> p k n", p=128))
            rms_stats(P, xs[s], 512, sq, ssps, lnv, rs[s])
            for k in range(8):
                tt("dve" if k % 2 == 0 else "pool", hT[s][:, k, :], xs[s][:, k, :], rs[s][:], ALU.mult)

        def stage2(tc):
            s = tc % 2
            own = tc >= NH - 1
            h = hT[s]
            sf = stF[s]
            sv = stV[s]

            def fm(slot, col0):
                ps = nextps()
                pe_group(ps[:], [(WA[:, k, col0:col0 + 128], h[:, k, :]) for k in range(8)])
                evac(sf[:, slot, :], ps[:])

            for hp in range(4):
                fm(hp, 512 + 128 * hp)
            if own:
                for hp in range(4):
                    fm(4 + hp, 128 * hp)
                for hp in range(4):
                    fm(8 + hp, 1024 + 128 * hp)
                for gg in range(2):
                    fm(12 + gg, 1536 + 128 * gg)
            for blk in range(4):
                for half in range(2):
                    ps = nextps()
                    pe_group(ps[:], [(h[:, k, blk * 128:(blk + 1) * 128],
                                      WA[:, k, 1792 + 512 * half:1792 + 512 * (half + 1)]) for k in range(8)])
                    evac(sv[:, blk, 512 * half:512 * (half + 1)], ps[:])
                if own:
                    ps = nextps()
                    pe_group(ps[:], [(h[:, k, blk * 128:(blk + 1) * 128], WA[:, k, 2816:3328]) for k in range(8)])
                    evac(sv[:, blk, 1024:1536], ps[:])
            c0 = tc * 512
            store(KTs[:, :, c0:c0 + 512].rearrange("h p n -> p h n"), sf[:, 0:4, :])
            for hp in range(4):
                store(VSs[hp, :, 4 * tc:4 * tc + 4, :], sv[:, :, 256 * hp:256 * (hp + 1)])
            if own:
                o0 = (tc - (NH - 1)) * 512
                store(QTs[:, :, o0:o0 + 512].rearrange("h p n -> p h n"), sf[:, 4:8, :])
                store(QWs[:, :, o0:o0 + 512].rearrange("h p n -> p h n"), sf[:, 8:12, :])
                store(KWs[:, :, o0:o0 + 512].rearrange("h p n -> p h n"), sf[:, 12:14, :])
                b0 = (tc - (NH - 1)) * 4
                store(VWs[:, b0:b0 + 4, :], sv[:, :, 1024:1536])

        NT = 2 * NH
        stage1(0)
        for tc in range(NT):
            if tc + 1 < NT:
                stage1(tc + 1)
            stage2(tc)
        S.barrier()
        P.close()

    def phase_B1():
        P = Pool_()
        QW = P.sb("QW", [128, 4, NOWN], BF16)
        KW = P.sb("KW", [128, 2, NOWN], BF16)
        VW = P.sb("VW", [128, (NH + 1) * 4, 512], BF16)
        BM = P.sb("BM", [128, 2048], F32)
        sk = P.sb("sk_s", [128, 4], F32)
        esk = P.sb("esk", [128, 4], F32)
        ESB = P.sb("ESB", [128, 4, 128], F32)
        zer = P.sb("zerB1", [128, 128], F32)
        LG = [P.sb(f"LG{i}", [128, 2048], F32) for i in range(2)]
        PB = [P.sb(f"PB{i}", [128, 2, 2, 4, 128], BF16) for i in range(2)]
        den = P.sb("den", [128, 512], F32)
        rec = P.sb("rec", [128, 512], F32)
        lden = P.sb("lden", [128, 512], F32)
        yst = [P.sb(f"yst{i}", [128, 4, 128], BF16) for i in range(2)]
        ZS = P.ps("ZS", [128, 2, 2, 4, 128])
        OP = P.ps("OPs", [128, 4, 128])
        DN = P.ps("DNs", [128, 4, 128])

        load(QW[:], QWs.rearrange("h p n -> p h n"))
        load(KW[:], KWs.rearrange("h p n -> p h n"))
        load(VW[:], VWs[:, :, :])
        load(BM[:], bm_d[:, :])
        load(sk[:], sk_d[:, :])
        act(None, esk[:], sk[:], AF.Exp)
        memset("dve", zer[:], 0.0)
        for hp in range(4):
            ts("dve", ESB[:, hp, :], zer[:], esk[:, hp:hp + 1], ALU.add)

        blocks = list(enumerate(range(3, (NH + 1) * 4)))

        def zstage(qi, i):
            s = qi % 2
            qcols = slice(i * 128, (i + 1) * 128)
            for kbsel in range(2):
                ib = i - 1 + kbsel
                for h in range(8):
                    po = (h % 2) * 64
                    g = h // 4
                    hp = h // 2
                    S.op("pe", mm(ZS[:, kbsel, h % 2, hp, :], KW[po:po + 64, g, ib * 128:(ib + 1) * 128],
                                  QW[po:po + 64, hp, qcols], True, True),
                         reads=[KW.buf, QW.buf], writes=[ZS.buf])
            stt(LG[s][:], V(ZS.buf, ZS.t[:].rearrange("p a r h q -> p (a r h q)")), 0.125, BM[:], ALU.mult, ALU.add)
            for kbsel in range(2):
                lb = (NBH - 4) + i - 1 + kbsel
                act(None, V(PB[s].buf, PB[s].t[:, kbsel, :, :, :].rearrange("p r h q -> p (r h q)")),
                    LG[s][:, kbsel * 1024:(kbsel + 1) * 1024], AF.Exp, bias=kbias[:, lb:lb + 1])

        def pvstage(qi, i):
            s = qi % 2
            for hp in range(4):
                g = hp // 2
                prs = []
                prd = []
                for kbsel in range(2):
                    ib = i - 1 + kbsel
                    for r in range(2):
                        var = 2 * g + r
                        prs.append((VW[:, ib, 128 * var:128 * (var + 1)], PB[s][:, kbsel, r, hp, :]))
                        prd.append((opad[:, r, :], PB[s][:, kbsel, r, hp, :]))
                pe_group(OP[:, hp, :], prs)
                pe_group(DN[:, hp, :], prd)
            tt("dve", den[:], V(DN.buf, DN.t[:].rearrange("p a q -> p (a q)")),
               V(ESB.buf, ESB.t[:].rearrange("p a q -> p (a q)")), ALU.add)
            act(None, lden[:], den[:], AF.Ln)
            act(None, rec[:], lden[:], AF.Exp, scale=-1.0)
            tt("dve", V(yst[s].buf, yst[s].t[:].rearrange("p a q -> p (a q)")),
               V(OP.buf, OP.t[:].rearrange("p a q -> p (a q)")), rec[:], ALU.mult)
            store(YSW[:, :, qi * 128:(qi + 1) * 128], yst[s][:])

        zstage(*blocks[0])
        for bi, (qi, i) in enumerate(blocks):
            if bi + 1 < len(blocks):
                zstage(*blocks[bi + 1])
            pvstage(qi, i)
        S.barrier()
        P.close()

    def phase_B2():
        P = Pool_()
        KTt = [P.sb(f"KTt{i}", [128, L2], BF16) for i in range(2)]
        Vt = [P.sb(f"Vt{i}", [128, NB2, 256], BF16) for i in range(2)]
        QTt = [P.sb(f"QTt{i}", [128, NOWN], BF16) for i in range(2)]
        NBUF = 4
        Eb = [P.sb(f"Eb{i}", [128, 2, 512], BF16) for i in range(NBUF)]
        Lb = [P.sb(f"Lb{i}", [128, 2, 512], BF16) for i in range(NBUF)]
        Gb = [P.sb(f"Gb{i}", [128, 2, 512], BF16) for i in range(NBUF)]
        Wb = [P.sb(f"Wb{i}", [128, 2, 512], BF16) for i in range(NBUF)]
        triC = P.sb("triC", [128, 128], BF16)
        yev = [P.sb(f"yev{i}", [128, 512], BF16) for i in range(2)]
        Z = [P.ps(f"Zp{i}", [128, 2, 512]) for i in range(2)]
        X = P.ps("Xp", [128, 2, 512])
        Y = [P.ps(f"Yp{i}", [128, 512]) for i in range(2)]
        tt("dve", triC[:], ones[:], tri[:], ALU.subtract)
        zb = P.sb("zbB2", [128, 512], BF16)
        memset("dve", zb[:], 0.0)

        def load_hp(hp):
            s = hp % 2
            hl = L2 // 2
            load(KTt[s][:, 0:hl], KTs[hp, :, 0:hl])
            load(KTt[s][:, hl:L2], KTs[hp, :, hl:L2])
            load(Vt[s][:, 0:NB2 // 2, :], VSs[hp, :, 0:NB2 // 2, :])
            load(Vt[s][:, NB2 // 2:NB2, :], VSs[hp, :, NB2 // 2:NB2, :])
            load(QTt[s][:], QTs[hp, :, :])

        class Grp:
            pass

        def make_group(hp, qs, n, ycol, gidx, ubase):
            G = Grp()
            sl = hp % 2
            Kt, Vv, Qt = KTt[sl], Vt[sl], QTt[sl]
            qc = qs - (NH - 1) * 512
            kbmax = (qs + n - 2) // 128
            kbs = list(range(kbmax, -1, -1))
            U = len(kbs)
            G.U = U
            Yp = Y[gidx % 2]
            ye = yev[gidx % 2]

            def c0of(u):
                j = kbs[u] - qs // 128
                return 128 * j if (n == 512 and j >= 1) else 0

            def mmx(out, lhsT, rhs, start):
                return lambda e: e.matmul(out.ap, lhsT.ap, rhs.ap, start=start, stop=True,
                                          skip_group_check=True)

            def s_Z(u):
                kb = kbs[u]
                j = kb - qs // 128
                c0 = c0of(u)
                Zt = Z[(ubase + u) % 2]
                for r in range(2):
                    po = 64 * r
                    prs = [(Kt[po:po + 64, kb * 128:(kb + 1) * 128], Qt[po:po + 64, qc + c0:qc + n])]
                    if j >= 0:
                        if n == 512:
                            prs.append((ident[:], dmask[:, j, c0:n]))
                        else:
                            prs.append((ident[:], dmask[:, 0, 126:128]))
                    pe_group(Zt[:, r, c0:n], prs)

            def s_EL(u):
                kb = kbs[u]
                b = (ubase + u) % NBUF
                c0 = c0of(u)
                act(None, Eb[b][:, :, c0:n], Z[(ubase + u) % 2][:, :, c0:n], AF.Exp, scale=0.125,
                    bias=kbias[:, kb:kb + 1])
                act(None, Lb[b][:, :, c0:n], Eb[b][:, :, c0:n], AF.Ln, bias=1.0)

            def s_A(u):
                b = (ubase + u) % NBUF
                c0 = c0of(u)
                fns = [mmx(X[:, r, c0:n], tri[:], Lb[b][:, r, c0:n], False) for r in range(2)]
                S.op("pe", fns, reads=[tri.buf, Lb[b].buf], writes=[X.buf])

            def s_B(u):
                b = (ubase + u) % NBUF
                c0 = c0of(u)
                fns = [mmx(X[:, r, c0:n], triC[:], Lb[b][:, r, c0:n], False) for r in range(2)]
                S.op("pe", fns, reads=[triC.buf, Lb[b].buf], writes=[X.buf])

            def s_G(u):
                b = (ubase + u) % NBUF
                c0 = c0of(u)
                act(None, Gb[b][:, :, c0:n], X[:, :, c0:n], AF.Exp, scale=-1.0)
                tt("dve", Wb[b][:, :, c0:n], Eb[b][:, :, c0:n], Gb[b][:, :, c0:n], ALU.mult)

            def s_Y(u):
                kb = kbs[u]
                b = (ubase + u) % NBUF
                c0 = c0of(u)
                fns = []
                for r in range(2):
                    fns.append(mmx(Yp[:, c0:n], Vv[:, kb, 128 * r:128 * (r + 1)], Wb[b][:, r, c0:n], False))
                S.op("pe", fns, reads=[Vv.buf, Wb[b].buf], writes=[Yp.buf])

            def pre():
                s_Z(0)
                if U > 1:
                    s_Z(1)
                s_EL(0)
                if U > 2:
                    s_Z(2)

            def run(nxt):
                S.op("pe", [mmx(X[:, r, 0:n], ident[:], zb[:, 0:n], True) for r in range(2)],
                     reads=[ident.buf, zb.buf], writes=[X.buf])
                S.op("pe", [mmx(Yp[:, 0:n], ident[:], zb[:, 0:n], True)],
                     reads=[ident.buf, zb.buf], writes=[Yp.buf])
                s_A(0)
                for t in range(U):
                    if t + 1 < U:
                        s_EL(t + 1)
                    if t == U - 1 and nxt is not None:
                        nxt.pre()
                    s_G(t)
                    if t + 1 < U:
                        s_B(t)
                        s_A(t + 1)
                    if t + 3 < U:
                        s_Z(t + 3)
                    if t >= 1:
                        s_Y(t - 1)
                s_Y(U - 1)
                cp("dve", ye[:, 0:n], Yp[:, 0:n])
                store(YSB[:, hp, ycol:ycol + n], ye[:, 0:n])

            G.pre = pre
            G.run = run
            return G

        glist = []
        ub = 0
        for hp in range(4):
            for (qs, n, ycol) in groups:
                g = make_group(hp, qs, n, ycol, len(glist), ub)
                ub += g.U
                g.hp = hp
                glist.append(g)
        load_hp(0)
        load_hp(1)
        glist[0].pre()
        for i, g in enumerate(glist):
            nxt = glist[i + 1] if i + 1 < len(glist) else None
            if nxt is not None and nxt.hp != g.hp and nxt.hp + 1 < 4:
                pass
            g.run(nxt)
            if nxt is not None and nxt.hp != g.hp and nxt.hp + 1 < 4:
                load_hp(nxt.hp + 1)
        S.barrier()
        P.close()

    def load_w_bf16(P, W, src, nk, ncol, gvec, stg, piece):
        i = 0
        for k in range(nk):
            for c0 in range(0, ncol, piece):
                c1 = min(ncol, c0 + piece)
                st = stg[i % len(stg)]
                eng = "dve" if i % 2 == 0 else "act"
                i += 1
                load(st[:, 0:c1 - c0], src[k * 128:(k + 1) * 128, c0:c1])
                if gvec is None:
                    cp(eng, W[:, k, c0:c1], st[:, 0:c1 - c0])
                elif eng == "act":
                    act(None, W[:, k, c0:c1], st[:, 0:c1 - c0], AF.Identity, scale=gvec[:, k:k + 1])
                else:
                    ts(eng, W[:, k, c0:c1], st[:, 0:c1 - c0], gvec[:, k:k + 1], ALU.mult)

    def phase_C1():
        P = Pool_()
        WG = P.sb("WG", [128, 8, 2048], BF16)
        WSB = P.sb("WSB", [128, 4, 1024], BF16)
        WSW = P.sb("WSW", [128, 4, 1024], BF16)
        WO = P.sb("WO", [128, 8, 1024], BF16)
        stg = [P.sb(f"stgC1{i}", [128, 1024], F32) for i in range(4)]
        gm = P.sb("gm_c1", [128, 8], F32)
        xs = [P.sb(f"xsC{i}", [128, 8, 512], F32) for i in range(2)]
        sq = P.sb("sqC", [128, 8, 512], BF16)
        hT = [P.sb(f"hTC{i}", [128, 8, 512], BF16) for i in range(2)]
        lnv = P.sb("lnvC", [128, 512], F32)
        rs = P.sb("rsC", [128, 512], F32)
        ysb = [P.sb(f"ysbC{i}", [128, 4, 512], BF16) for i in range(2)]
        ysw = [P.sb(f"yswC{i}", [128, 4, 512], BF16) for i in range(2)]
        g1 = [P.sb(f"g1C{i}", [128, 512], F32) for i in range(2)]
        g2 = [P.sb(f"g2C{i}", [128, 512], F32) for i in range(2)]
        t1 = [P.sb(f"t1C{i}", [128, 512], F32) for i in range(2)]
        t2 = [P.sb(f"t2C{i}", [128, 512], F32) for i in range(2)]
        mT = P.sb("mT", [128, 8, 512], BF16)
        x1 = P.sb("x1C", [128, 8, 512], F32)
        ssps = P.ps("ssC", [128, 512])
        PS = [P.ps(f"psC{i}", [128, 512]) for i in range(7)]

        load(gm[:], gm_d[:, :])
        load_w_bf16(P, WG, w_in[:, 2304:4352], 8, 2048, gm, stg, 1024)
        load_w_bf16(P, WSB, w_sbp, 4, 1024, None, stg, 1024)
        load_w_bf16(P, WSW, w_swp, 4, 1024, None, stg, 1024)
        load_w_bf16(P, WO, w_out, 8, 1024, None, stg, 1024)
        psi = [0]

        def nextps():
            psi[0] = (psi[0] + 1) % 7
            return PS[psi[0]]

        def prologue(gi_):
            qs, n, oc = groups[gi_]
            s = gi_ % 2
            x = xs[s]
            load(x[:, :, 0:n], xT[:, qs:qs + n].rearrange("(k p) n -> p k n", p=128))
            load(ysb[s][:, :, 0:n], YSB[:, :, oc:oc + n])
            swc = qs - (H - 128)
            load(ysw[s][:, :, 0:n], YSW[:, :, swc:swc + n])
            rms_stats(P, x, n, sq, ssps, lnv, rs)
            for k in range(8):
                tt("dve" if k % 2 == 0 else "pool", hT[s][:, k, 0:n], x[:, k, 0:n], rs[:, 0:n], ALU.mult)

        def main(gi_):
            qs, n, oc = groups[gi_]
            s = gi_ % 2
            x = xs[s]
            h = hT[s]
            for o in range(8):
                pg1 = nextps()
                pe_group(pg1[:, 0:n], [(WG[:, k, o * 128:(o + 1) * 128], h[:, k, 0:n]) for k in range(8)])
                act(None, g1[o % 2][:, 0:n], pg1[:, 0:n], AF.Sigmoid)
                pg2 = nextps()
                pe_group(pg2[:, 0:n], [(WG[:, k, (8 + o) * 128:(9 + o) * 128], h[:, k, 0:n]) for k in range(8)])
                act(None, g2[o % 2][:, 0:n], pg2[:, 0:n], AF.Sigmoid)
                pa = nextps()
                pe_group(pa[:, 0:n], [(WSB[:, k, o * 128:(o + 1) * 128], ysb[s][:, k, 0:n]) for k in range(4)])
                pb = nextps()
                pe_group(pb[:, 0:n], [(WSW[:, k, o * 128:(o + 1) * 128], ysw[s][:, k, 0:n]) for k in range(4)])
                tt("dve", t1[o % 2][:, 0:n], pa[:, 0:n], g1[o % 2][:, 0:n], ALU.mult)
                tt("dve", t2[o % 2][:, 0:n], pb[:, 0:n], g2[o % 2][:, 0:n], ALU.mult)
                tt("pool", mT[:, o, 0:n], t1[o % 2][:, 0:n], t2[o % 2][:, 0:n], ALU.add)
            for o in range(8):
                ps = nextps()
                pe_group(ps[:, 0:n], [(WO[:, k, o * 128:(o + 1) * 128], mT[:, k, 0:n]) for k in range(8)])
                tt("dve", x1[:, o, 0:n], ps[:, 0:n], x[:, o, 0:n], ALU.add)
            store(X1s[:, :, oc:oc + n], x1[:, :, 0:n])

        prologue(0)
        for gi_ in range(len(groups)):
            if gi_ + 1 < len(groups):
                prologue(gi_ + 1)
            main(gi_)
        S.barrier()
        P.close()

    def phase_C2a():
        P = Pool_()
        WU = P.sb("WU", [128, 8, 2 * DFF], BF16)
        stg = [P.sb(f"stgU{i}", [128, 1408], F32) for i in range(4)]
        gf = P.sb("gf_s", [128, 8], F32)
        cw = P.sb("cw_s", [128, 44, 3], F32)
        cb = P.sb("cb_s", [128, 44], F32)
        xs = P.sb("xsU", [128, 8, 512], F32)
        sq = P.sb("sqU", [128, 8, 512], BF16)
        hT = [P.sb(f"hTU{i}", [128, 8, 512], BF16) for i in range(2)]
        lnv = P.sb("lnvU", [128, 512], F32)
        rs = P.sb("rsU", [128, 512], F32)
        yb = [P.sb(f"ybU{i}", [128, 512], F32) for i in range(4)]
        sg = [P.sb(f"sgU{i}", [128, 512], F32) for i in range(2)]
        aT = P.sb("aTU", [128, 22, 512], BF16)
        ssps = P.ps("ssU", [128, 512])
        PS = [P.ps(f"psU{i}", [128, 512]) for i in range(7)]

        load(gf[:], gf_d[:, :])
        load(cw[:], cw_d.rearrange("p (j k) -> p j k", k=3))
        load(cb[:], cb_d[:, :])
        load_w_bf16(P, WU, w_up, 8, 2 * DFF, gf, stg, 1408)
        psi = [0]

        def nextps():
            psi[0] = (psi[0] + 1) % 7
            return PS[psi[0]]

        GW = 510
        NG = (H + GW - 1) // GW

        def geo(g):
            a0 = g * GW
            w = min(GW, H - a0)
            return a0, w, w + 2

        def prologue(g):
            a0, w, n = geo(g)
            load(xs[:, :, 0:n], X1s[:, :, a0:a0 + n])
            rms_stats(P, xs, n, sq, ssps, lnv, rs)
            for k in range(8):
                tt("dve" if k % 2 == 0 else "pool", hT[g % 2][:, k, 0:n], xs[:, k, 0:n], rs[:, 0:n], ALU.mult)

        ybi = [0]

        def tiles(g):
            a0, w, n = geo(g)
            h = hT[g % 2]
            for j in range(22):
                ys = []
                for t in (j, 22 + j):
                    ps = nextps()
                    pe_group(ps[:, 0:n], [(WU[:, k, t * 128:(t + 1) * 128], h[:, k, 0:n]) for k in range(8)])
                    y = yb[ybi[0] % 4]
                    ybi[0] += 1
                    ys.append(y)
                    act(None, y[:, 0:w], ps[:, 2:n], AF.Identity, scale=cw[:, t, 2:3], bias=cb[:, t:t + 1])
                    stt(y[:, 0:w], ps[:, 1:n - 1], cw[:, t, 1:2], y[:, 0:w], ALU.mult, ALU.add)
                    stt(y[:, 0:w], ps[:, 0:w], cw[:, t, 0:1], y[:, 0:w], ALU.mult, ALU.add)
                sgt = sg[j % 2]
                act(None, sgt[:, 0:w], ys[0][:, 0:w], AF.Silu)
                tt("pool", aT[:, j, 0:w], sgt[:, 0:w], ys[1][:, 0:w], ALU.mult)
            store(ATs[:, :, 2 + a0:2 + a0 + w], aT[:, :, 0:w])

        prologue(0)
        for g in range(NG):
            if g + 1 < NG:
                prologue(g + 1)
            tiles(g)
        S.barrier()
        P.close()

    def phase_C2b():
        P = Pool_()
        WD = P.sb("WD", [128, 22, 1024], BF16)
        stg = [P.sb(f"stgD{i}", [128, 1024], F32) for i in range(4)]
        glf = P.sb("gl_s", [128, 8], F32)
        xs = [P.sb(f"xsD{i}", [128, 8, 512], F32) for i in range(2)]
        aT = [P.sb(f"aTD{i}", [128, 22, 512], BF16) for i in range(2)]
        x2 = P.sb("x2D", [128, 8, 512], F32)
        sq = P.sb("sqD", [128, 8, 512], BF16)
        lnv = P.sb("lnvD", [128, 512], F32)
        rs = P.sb("rsD", [128, 512], F32)
        ob = [P.sb(f"obD{i}", [128, 8, 512], F32) for i in range(2)]
        ssps = P.ps("ssD", [128, 512])
        PS = [P.ps(f"psD{i}", [128, 512]) for i in range(6)]

        load(glf[:], gl_d[:, :])
        load_w_bf16(P, WD, w_down, 22, 1024, None, stg, 1024)
        psi = [0]

        def nextps():
            psi[0] = (psi[0] + 1) % 6
            return PS[psi[0]]

        for gi_, (qs, n, oc) in enumerate(groups[1:]):
            s = gi_ % 2
            x = xs[s]
            load(x[:], X1s[:, :, oc:oc + n])
            load(aT[s][:], ATs[:, :, oc:oc + n])
            for o in range(8):
                ps = nextps()
                pe_group(ps[:], [(WD[:, k, o * 128:(o + 1) * 128], aT[s][:, k, :]) for k in range(22)])
                tt("dve", x2[:, o, :], ps[:], x[:, o, :], ALU.add)
            rms_stats(P, x2, 512, sq, ssps, lnv, rs)
            for o in range(8):
                stt(ob[s][:, o, :], x2[:, o, :], glf[:, o:o + 1], rs[:], ALU.mult, ALU.mult)
            store(out_d[:, :, oc - 2:oc - 2 + n].rearrange("k p n -> p k n"), ob[s][:])
        S.barrier()
        P.close()

    S.barrier()
    gtmp.close()
    ph = phases.split(",")
    if "A" in ph:
        phase_A()
    if "B1" in ph:
        phase_B1()
    if "B2" in ph:
        phase_B2()
    if "C1" in ph:
        phase_C1()
    if "C2a" in ph:
        phase_C2a()
    if "C2b" in ph:
        phase_C2b()
    gl.close()

    with ExitStack() as es:
        sems = {}
        for k in S.keys:
            nm = k if isinstance(k, str) else f"d{k[1]}"
            sems[k] = es.enter_context(nc.semaphore("s_" + nm))
        block = es.enter_context(nc.Block())

        @block.tensor
        def _(e):
            S.replay("pe", e, sems)

        @block.scalar
        def _(e):
            S.replay("act", e, sems)

        @block.vector
        def _(e):
            S.replay("dve", e, sems)

        @block.gpsimd
        def _(e):
            S.replay("pool", e, sems)

        @block.sync
        def _(e):
            S.replay("sp", e, sems)

    return nc


def _t5_bucket(dist):
    dist = np.asarray(dist, np.int32)
    max_exact = 16
    d = np.maximum(dist, 1).astype(np.float32)
    large = max_exact + (np.log(d / np.float32(max_exact)) / np.float32(np.log(128 / max_exact))
                         * np.float32(32 - max_exact)).astype(np.int32)
    large = np.minimum(large, 31)
    return np.where(dist < max_exact, dist, large)


def _consts():
    c = np.zeros((128, 2688), np.float32)
    c[:, 0:128] = 1.0
    j = np.arange(128)[:, None]
    s = np.arange(128)[None, :]
    c[:, 128:256] = (j >= s).astype(np.float32)
    c[:, 256:384] = np.eye(128, dtype=np.float32)
    q = np.arange(512)[None, :]
    for jj in range(4):
        valid = (128 * jj + j) < q
        c[:, 384 + 512 * jj:384 + 512 * (jj + 1)] = np.where(valid, 0.0, 8.0 * NEGM)
    c[:, 2432:2432 + 64] = 1.0
    c[:, 2432 + 128 + 64:2432 + 256] = 1.0
    return c


def prep_inputs(NH, x, g_mix, w_in, w_sb_proj, w_sw_proj, w_out, rel_bias, sinks,
                g_ffn, w_up, conv_w, conv_b, w_down, g_final):
    H = NH * 512
    B = x.shape[0]
    f = lambda a: np.ascontiguousarray(np.asarray(a, dtype=np.float32))
    x = f(x)
    lay8 = lambda g: f(np.asarray(g, np.float32).reshape(8, 128).T)
    rel_bias = np.asarray(rel_bias, np.float32)
    k = np.arange(128)[:, None]
    q = np.arange(128)[None, :]
    bm = np.zeros((128, 2, 2, 4, 128), np.float32)
    d0 = 128 + q - k
    d1 = q - k
    b0 = _t5_bucket(np.clip(d0, 0, 255))
    b1 = _t5_bucket(np.clip(d1, 0, 255))
    for h in range(8):
        bm[:, 0, h % 2, h // 2, :] = np.where(d0 <= 127, rel_bias[b0, h], NEGM)
        bm[:, 1, h % 2, h // 2, :] = np.where(d1 >= 0, rel_bias[b1, h], NEGM)
    sinks = np.asarray(sinks, np.float32)
    sk = np.zeros((128, 4), np.float32)
    for hp in range(4):
        sk[0:64, hp] = sinks[2 * hp]
        sk[64:128, hp] = sinks[2 * hp + 1]
    cw = np.asarray(conv_w, np.float32)
    cwl = np.ascontiguousarray(cw.reshape(3, 44, 128).transpose(2, 1, 0)).reshape(128, 132)
    cbl = f(np.asarray(conv_b, np.float32).reshape(44, 128).T)
    shared = {
        "w_in": f(w_in), "w_sbp": f(w_sb_proj), "w_swp": f(w_sw_proj), "w_out": f(w_out),
        "w_up": f(w_up), "w_down": f(w_down), "gm": lay8(g_mix), "gf": lay8(g_ffn), "gl": lay8(g_final),
        "cw": f(cwl), "cb": cbl, "bm": f(bm.reshape(128, 2048)), "sk": sk, "cst": _consts(),
    }
    in_maps = []
    for c in range(2 * B):
        b, p = c // 2, c % 2
        xl = np.zeros((2 * H, D), np.float32)
        kb = np.zeros((128, 2 * H // 128), np.float32)
        if p == 1:
            xl[:] = x[b]
        else:
            xl[H:] = x[b, :H]
            kb[:, :H // 128] = NEGM
        m = dict(shared)
        m["xT"] = np.ascontiguousarray(xl.T)
        m["kbias"] = kb
        in_maps.append(m)
    return in_maps


_NC_CACHE = {}


def run(NH, phases="A,B1,B2,C1,C2a,C2b", **inputs):
    x = np.asarray(inputs["x"])
    B = x.shape[0]
    H = NH * 512
    in_maps = prep_inputs(NH, **inputs)
    if (NH, phases) not in _NC_CACHE:
        _NC_CACHE[(NH, phases)] = build_nc(NH, phases)
    nc = _NC_CACHE[(NH, phases)]
    res = run_bass_kernel_spmd(nc, in_maps, core_ids=list(range(2 * B)))
    out = np.zeros((B, 2 * H, D), np.float32)
    for c in range(2 * B):
        b, p = c // 2, c % 2
        o = np.asarray(res.results[c]["out"]).reshape(D, H)
        out[b, p * H:(p + 1) * H, :] = o.T
    return out


def kernel(**inputs):
    return run(8, **inputs)
```

```python
import numpy as np
from contextlib import ExitStack
import concourse.bass as bass
import concourse.mybir as mybir
from concourse.bass_utils import run_bass_kernel_spmd

F32 = mybir.dt.float32
BF16 = mybir.dt.bfloat16
AF = mybir.ActivationFunctionType
ALU = mybir.AluOpType

D = 1024
KT = 8
DFF = 2816
NEGM = -100.0
ENGS = ("pe", "act", "dve", "pool", "sp")
NDMA = 24
NDMA_HW = 16


class Buf:
    __slots__ = ("w", "r", "name")

    def __init__(self, name=""):
        self.w = None
        self.r = {}
        self.name = name


class V:
    __slots__ = ("buf", "ap")

    def __init__(self, buf, ap):
        self.buf = buf
        self.ap = ap


class Tile:
    def __init__(self, handle, name=""):
        self.t = handle
        self.buf = Buf(name)

    def __getitem__(self, idx):
        return V(self.buf, self.t[idx])


class Sched:
    def __init__(self):
        self.keys = list(ENGS) + [("d", i) for i in range(NDMA)]
        self.cnt = {k: 0 for k in self.keys}
        self.seen = {e: {k: 0 for k in self.keys} for e in ENGS}
        self.q = {e: [] for e in ENGS}
        self.dma_rr = 0
        self.dma_rr2 = 0
        self.nops = 0

    def op(self, eng, fns, reads=(), writes=(), dma=False):
        if callable(fns):
            fns = [fns]
        need = {}

        def req(tok):
            if tok is None:
                return
            k, v = tok
            if need.get(k, 0) < v:
                need[k] = v

        for b in reads:
            req(b.w)
        for b in writes:
            req(b.w)
            for k, v in b.r.items():
                req((k, v))
        if dma:
            if eng == "sp":
                key = ("d", self.dma_rr)
                self.dma_rr = (self.dma_rr + 1) % NDMA_HW
            else:
                key = ("d", NDMA_HW + self.dma_rr2)
                self.dma_rr2 = (self.dma_rr2 + 1) % (NDMA - NDMA_HW)
            req((key, self.cnt[key]))
            inc = 16
        else:
            key = eng
            inc = 1
        waits = []
        seen = self.seen[eng]
        for k, v in need.items():
            if v <= 0 or seen[k] >= v:
                continue
            if k == eng and eng == "pe":
                continue
            seen[k] = v
            waits.append((k, v))
        self.cnt[key] += inc
        tok = (key, self.cnt[key])
        for b in writes:
            b.w = tok
            b.r = {}
        for b in reads:
            if b.r.get(key, 0) < tok[1]:
                b.r[key] = tok[1]
        self.q[eng].append((waits, fns, key, inc))
        self.nops += 1
        return tok

    def barrier(self):
        for e in ENGS:
            waits = []
            for k in self.keys:
                v = self.cnt[k]
                if v > 0 and self.seen[e][k] < v and k != e:
                    self.seen[e][k] = v
                    waits.append((k, v))
            if waits:
                self.q[e].append((waits, [], None, 0))

    def replay(self, eng, e, sems):
        for waits, fns, key, inc in self.q[eng]:
            for k, v in waits:
                e.wait_ge(sems[k], v)
            ins = None
            for f in fns:
                ins = f(e)
            if ins is not None and key is not None:
                ins.then_inc(sems[key], inc)


def build_nc(NH, phases="A,B1,B2,C1,C2a,C2b"):
    H = NH * 512
    L2 = 2 * H
    NB2 = L2 // 128
    NBH = H // 128
    NOWN = (NH + 1) * 512
    NY = 2 + NH * 512
    NSWB = NH * 4 + 1

    nc = bass.Bass("TRN2", target_bir_lowering=False)
    S = Sched()

    def din(name, shape):
        return nc.dram_tensor(name, shape, F32, kind="ExternalInput").ap()

    xT = din("xT", [D, L2])
    kbias_d = din("kbias", [128, NB2])
    w_in = din("w_in", [D, 4352])
    w_sbp = din("w_sbp", [512, D])
    w_swp = din("w_swp", [512, D])
    w_out = din("w_out", [D, D])
    w_up = din("w_up", [D, 2 * DFF])
    w_down = din("w_down", [DFF, D])
    gm_d = din("gm", [128, 8])
    gf_d = din("gf", [128, 8])
    gl_d = din("gl", [128, 8])
    cw_d = din("cw", [128, 44 * 3])
    cb_d = din("cb", [128, 44])
    bm_d = din("bm", [128, 2048])
    sk_d = din("sk", [128, 4])
    cst_d = din("cst", [128, 2688])
    out_d = nc.dram_tensor("out", [8, 128, H], F32, kind="ExternalOutput").ap()

    def dscr(name, shape, dt):
        return nc.dram_tensor(name, shape, dt, kind="Internal").ap()

    KTs = dscr("KTs", [4, 128, L2], BF16)
    QTs = dscr("QTs", [4, 128, NOWN], BF16)
    VSs = dscr("VSs", [4, 128, NB2, 256], BF16)
    QWs = dscr("QWs", [4, 128, NOWN], BF16)
    KWs = dscr("KWs", [2, 128, NOWN], BF16)
    VWs = dscr("VWs", [128, (NH + 1) * 4, 512], BF16)
    YSB = dscr("YSB", [128, 4, NY], BF16)
    YSW = dscr("YSW", [128, 4, NSWB * 128], BF16)
    X1s = dscr("X1s", [128, 8, NY], F32)
    ATs = dscr("ATs", [128, 22, NY], BF16)

    groups = [(H - 2, 2, 0)] + [(H + 512 * c, 512, 2 + 512 * c) for c in range(NH)]

    def mm(out, lhsT, rhs, start, stop):
        return lambda e: e.matmul(out.ap, lhsT.ap, rhs.ap, start=start, stop=stop)

    def pe_group(out, pairs, extra_reads=()):
        fns = []
        n = len(pairs)
        rd = []
        for i, (l, r) in enumerate(pairs):
            fns.append(mm(out, l, r, i == 0, i == n - 1))
            rd += [l.buf, r.buf]
        S.op("pe", fns, reads=rd + list(extra_reads), writes=[out.buf])

    def act(eng_unused, out, in_, func, scale=1.0, bias=0.0, extra_reads=()):
        def f(e):
            kw = {}
            if isinstance(bias, V):
                kw["bias"] = bias.ap
            else:
                kw["bias"] = float(bias)
            if isinstance(scale, V):
                kw["scale"] = scale.ap
            else:
                kw["scale"] = float(scale)
            return e.activation(out.ap, in_.ap, func, **kw)

        rd = [in_.buf] + [x.buf for x in (scale, bias) if isinstance(x, V)] + list(extra_reads)
        S.op("act", f, reads=rd, writes=[out.buf])

    def tt(eng, out, a, b, op):
        S.op(eng, lambda e: e.tensor_tensor(out.ap, a.ap, b.ap, op), reads=[a.buf, b.buf], writes=[out.buf])

    def ts(eng, out, a, s1, op0, s2=None, op1=None):
        def f(e):
            sc1 = s1.ap if isinstance(s1, V) else float(s1)
            if s2 is None:
                return e.tensor_scalar(out.ap, a.ap, sc1, None, op0)
            sc2 = s2.ap if isinstance(s2, V) else float(s2)
            return e.tensor_scalar(out.ap, a.ap, sc1, sc2, op0, op1)

        rd = [a.buf] + [x.buf for x in (s1, s2) if isinstance(x, V)]
        S.op(eng, f, reads=rd, writes=[out.buf])

    def stt(out, a, sc, b, op0, op1):
        def f(e):
            s = sc.ap if isinstance(sc, V) else float(sc)
            return e.scalar_tensor_tensor(out.ap, a.ap, s, b.ap, op0, op1)

        rd = [a.buf, b.buf] + ([sc.buf] if isinstance(sc, V) else [])
        S.op("dve", f, reads=rd, writes=[out.buf])

    def cp(eng, out, a):
        if eng == "act":
            act(None, out, a, AF.Copy)
        else:
            S.op(eng, lambda e: e.tensor_copy(out.ap, a.ap), reads=[a.buf], writes=[out.buf])

    def memset(eng, out, val):
        S.op(eng, lambda e: e.memset(out.ap, val), writes=[out.buf])

    def load(out, src_ap, eng="sp"):
        S.op(eng, lambda e: e.dma_start(out=out.ap, in_=src_ap), writes=[out.buf], dma=True)

    def store(dst_ap, src, eng="pool"):
        S.op(eng, lambda e: e.dma_start(out=dst_ap, in_=src.ap), reads=[src.buf], dma=True)

    class Pool_:
        def __init__(self):
            self.es = ExitStack()

        def sb(self, name, shape, dt):
            return Tile(self.es.enter_context(nc.sbuf_tensor(name, shape, dt)), name)

        def ps(self, name, shape):
            return Tile(self.es.enter_context(nc.psum_tensor(name, shape, F32)), name)

        def close(self):
            self.es.close()

    gl = Pool_()
    ones = gl.sb("ones", [128, 128], BF16)
    tri = gl.sb("tri", [128, 128], BF16)
    ident = gl.sb("ident", [128, 128], BF16)
    dmask = gl.sb("dmask", [128, 4, 512], BF16)
    opad = gl.sb("opad", [128, 2, 128], BF16)
    kbias = gl.sb("kbias_s", [128, NB2], F32)
    gtmp = Pool_()
    cstf = gtmp.sb("cstf", [128, 2688], F32)

    load(cstf[:], cst_d[:, :])
    load(kbias[:], kbias_d[:, :])
    cp("dve", ones[:], cstf[:, 0:128])
    cp("dve", tri[:], cstf[:, 128:256])
    cp("dve", ident[:], cstf[:, 256:384])
    for j in range(4):
        cp("dve", dmask[:, j, :], cstf[:, 384 + 512 * j:384 + 512 * (j + 1)])
    for j in range(2):
        cp("dve", opad[:, j, :], cstf[:, 2432 + 128 * j:2432 + 128 * (j + 1)])

    def rms_stats(P, xs, n, sq, ssps, lnv, rs):
        act(None, sq[:, :, 0:n], xs[:, :, 0:n], AF.Square)
        pe_group(ssps[:, 0:n], [(ones[:], sq[:, k, 0:n]) for k in range(8)])
        act(None, lnv[:, 0:n], ssps[:, 0:n], AF.Ln, scale=1.0 / D, bias=1e-6)
        act(None, rs[:, 0:n], lnv[:, 0:n], AF.Exp, scale=-0.5)

    def phase_A():
        P = Pool_()
        WA = P.sb("WA", [128, 8, 3328], BF16)
        stg = [P.sb(f"wstg{i}", [128, 2304], F32) for i in range(2)]
        gm = P.sb("gm_s", [128, 8], F32)
        xs = [P.sb(f"xsA{i}", [128, 8, 512], F32) for i in range(2)]
        sq = P.sb("sqA", [128, 8, 512], BF16)
        hT = [P.sb(f"hTA{i}", [128, 8, 512], BF16) for i in range(2)]
        lnv = P.sb("lnvA", [128, 512], F32)
        rs = [P.sb(f"rsA{i}", [128, 512], F32) for i in range(2)]
        stF = [P.sb(f"stF{i}", [128, 14, 512], BF16) for i in range(2)]
        stV = [P.sb(f"stV{i}", [128, 4, 1536], BF16) for i in range(2)]
        ssps = P.ps("ssA", [128, 512])
        PS = [P.ps(f"psA{i}", [128, 512]) for i in range(6)]

        load(gm[:], gm_d[:, :])
        memset("pool", WA[:, :, 1792:3328], 0.0)
        for i in range(2):
            memset("pool", stV[i][:, :, 0:1024], 0.0)
        for kt in range(8):
            st = stg[kt % 2]
            load(st[:], w_in[kt * 128:(kt + 1) * 128, 0:2304])
            g = gm[:, kt:kt + 1]
            e1 = "dve"
            act(None, WA[:, kt, 0:1024], st[:, 0:1024], AF.Identity, scale=g)
            ts(e1, WA[:, kt, 1024:1536], st[:, 1536:2048], g, ALU.mult)
            for gg in range(2):
                for r in range(2):
                    ts(e1, WA[:, kt, 1536 + 128 * gg + 64 * r:1536 + 128 * gg + 64 * (r + 1)],
                       st[:, 2048 + 64 * gg:2048 + 64 * (gg + 1)], g, ALU.mult)
            ts(e1, WA[:, kt, 1792:2304], st[:, 1024:1536], g, ALU.mult)
            for i in range(4):
                o = 2816 + 128 * i + 64 * (i % 2)
                gg = i // 2
                ts(e1, WA[:, kt, o:o + 64], st[:, 2176 + 64 * gg:2176 + 64 * (gg + 1)], g, ALU.mult)

        psi = [0]

        def nextps():
            psi[0] = (psi[0] + 1) % 6
            return PS[psi[0]]

        evi = [0]

        def evac(out, in_):
            evi[0] += 1
            cp("act" if evi[0] % 2 == 0 else "dve", out, in_)

        def stage1(tc):
            s = tc % 2
            load(xs[s][:], xT[:, tc * 512:(tc + 1) * 512].rearrange("(k p) n -> p k n", p=128))
            rms_stats(P, xs[s], 512, sq, ssps, lnv, rs[s])
            for k in range(8):
                tt("dve" if k % 2 == 0 else "pool", hT[s][:, k, :], xs[s][:, k, :], rs[s][:], ALU.mult)

        def stage2(tc):
            s = tc % 2
            own = tc >= NH - 1
            h = hT[s]
            sf = stF[s]
            sv = stV[s]

            def fm(slot, col0):
                ps = nextps()
                pe_group(ps[:], [(WA[:, k, col0:col0 + 128], h[:, k, :]) for k in range(8)])
                evac(sf[:, slot, :], ps[:])

            for hp in range(4):
                fm(hp, 512 + 128 * hp)
            if own:
                for hp in range(4):
                    fm(4 + hp, 128 * hp)
                for hp in range(4):
                    fm(8 + hp, 1024 + 128 * hp)
                for gg in range(2):
                    fm(12 + gg, 1536 + 128 * gg)
            for blk in range(4):
                ps = nextps()
                pe_group(ps[:], [(h[:, k, blk * 128:(blk + 1) * 128], WA[:, k, 1792:2304]) for k in range(8)])
                dst = sv.t[:, blk, 0:1024].rearrange("p (hp r c) -> p hp r c", r=2, c=128)
                src = ps.t[:, 0:512].rearrange("p (hp r c) -> p hp r c", r=2, c=64)
                evac(V(sv.buf, dst[:, :, 0, 0:64]), V(ps.buf, src[:, :, 0, :]))
                evac(V(sv.buf, dst[:, :, 1, 64:128]), V(ps.buf, src[:, :, 1, :]))
                if own:
                    ps = nextps()
                    pe_group(ps[:], [(h[:, k, blk * 128:(blk + 1) * 128], WA[:, k, 2816:3328]) for k in range(8)])
                    evac(sv[:, blk, 1024:1536], ps[:])
            c0 = tc * 512
            store(KTs[:, :, c0:c0 + 512].rearrange("h p n -> p h n"), sf[:, 0:4, :])
            for hp in range(4):
                store(VSs[hp, :, 4 * tc:4 * tc + 4, :], sv[:, :, 256 * hp:256 * (hp + 1)])
            if own:
                o0 = (tc - (NH - 1)) * 512
                store(QTs[:, :, o0:o0 + 512].rearrange("h p n -> p h n"), sf[:, 4:8, :])
                store(QWs[:, :, o0:o0 + 512].rearrange("h p n -> p h n"), sf[:, 8:12, :])
                store(KWs[:, :, o0:o0 + 512].rearrange("h p n -> p h n"), sf[:, 12:14, :])
                b0 = (tc - (NH - 1)) * 4
                store(VWs[:, b0:b0 + 4, :], sv[:, :, 1024:1536])

        NT = 2 * NH
        stage1(0)
        for tc in range(NT):
            if tc + 1 < NT:
                stage1(tc + 1)
            stage2(tc)
        S.barrier()
        P.close()

    def phase_B1():
        P = Pool_()
        QW = P.sb("QW", [128, 4, NOWN], BF16)
        KW = P.sb("KW", [128, 2, NOWN], BF16)
        VW = P.sb("VW", [128, (NH + 1) * 4, 512], BF16)
        BM = P.sb("BM", [128, 2048], F32)
        sk = P.sb("sk_s", [128, 4], F32)
        esk = P.sb("esk", [128, 4], F32)
        ESB = P.sb("ESB", [128, 4, 128], F32)
        zer = P.sb("zerB1", [128, 128], F32)
        LG = [P.sb(f"LG{i}", [128, 2048], F32) for i in range(2)]
        PB = [P.sb(f"PB{i}", [128, 2, 2, 4, 128], BF16) for i in range(2)]
        den = P.sb("den", [128, 512], F32)
        rec = P.sb("rec", [128, 512], F32)
        lden = P.sb("lden", [128, 512], F32)
        yst = [P.sb(f"yst{i}", [128, 4, 128], BF16) for i in range(2)]
        ZS = P.ps("ZS", [128, 2, 2, 4, 128])
        OP = P.ps("OPs", [128, 4, 128])
        DN = P.ps("DNs", [128, 4, 128])

        load(QW[:], QWs.rearrange("h p n -> p h n"))
        load(KW[:], KWs.rearrange("h p n -> p h n"))
        load(VW[:], VWs[:, :, :])
        load(BM[:], bm_d[:, :])
        load(sk[:], sk_d[:, :])
        act(None, esk[:], sk[:], AF.Exp)
        memset("dve", zer[:], 0.0)
        for hp in range(4):
            ts("dve", ESB[:, hp, :], zer[:], esk[:, hp:hp + 1], ALU.add)

        blocks = list(enumerate(range(3, (NH + 1) * 4)))

        def zstage(qi, i):
            s = qi % 2
            qcols = slice(i * 128, (i + 1) * 128)
            for kbsel in range(2):
                ib = i - 1 + kbsel
                for h in range(8):
                    po = (h % 2) * 64
                    g = h // 4
                    hp = h // 2
                    S.op("pe", mm(ZS[:, kbsel, h % 2, hp, :], KW[po:po + 64, g, ib * 128:(ib + 1) * 128],
                                  QW[po:po + 64, hp, qcols], True, True),
                         reads=[KW.buf, QW.buf], writes=[ZS.buf])
            stt(LG[s][:], V(ZS.buf, ZS.t[:].rearrange("p a r h q -> p (a r h q)")), 0.125, BM[:], ALU.mult, ALU.add)
            for kbsel in range(2):
                lb = (NBH - 4) + i - 1 + kbsel
                act(None, V(PB[s].buf, PB[s].t[:, kbsel, :, :, :].rearrange("p r h q -> p (r h q)")),
                    LG[s][:, kbsel * 1024:(kbsel + 1) * 1024], AF.Exp, bias=kbias[:, lb:lb + 1])

        def pvstage(qi, i):
            s = qi % 2
            for hp in range(4):
                g = hp // 2
                prs = []
                prd = []
                for kbsel in range(2):
                    ib = i - 1 + kbsel
                    for r in range(2):
                        var = 2 * g + r
                        prs.append((VW[:, ib, 128 * var:128 * (var + 1)], PB[s][:, kbsel, r, hp, :]))
                        prd.append((opad[:, r, :], PB[s][:, kbsel, r, hp, :]))
                pe_group(OP[:, hp, :], prs)
                pe_group(DN[:, hp, :], prd)
            tt("dve", den[:], V(DN.buf, DN.t[:].rearrange("p a q -> p (a q)")),
               V(ESB.buf, ESB.t[:].rearrange("p a q -> p (a q)")), ALU.add)
            act(None, lden[:], den[:], AF.Ln)
            act(None, rec[:], lden[:], AF.Exp, scale=-1.0)
            tt("dve", V(yst[s].buf, yst[s].t[:].rearrange("p a q -> p (a q)")),
               V(OP.buf, OP.t[:].rearrange("p a q -> p (a q)")), rec[:], ALU.mult)
            store(YSW[:, :, qi * 128:(qi + 1) * 128], yst[s][:])

        zstage(*blocks[0])
        for bi, (qi, i) in enumerate(blocks):
            if bi + 1 < len(blocks):
                zstage(*blocks[bi + 1])
            pvstage(qi, i)
        S.barrier()
        P.close()

    def phase_B2():
        P = Pool_()
        KTt = [P.sb(f"KTt{i}", [128, L2], BF16) for i in range(2)]
        Vt = [P.sb(f"Vt{i}", [128, NB2, 256], BF16) for i in range(2)]
        QTt = [P.sb(f"QTt{i}", [128, NOWN], BF16) for i in range(2)]
        NBUF = 4
        Eb = [P.sb(f"Eb{i}", [128, 2, 512], BF16) for i in range(NBUF)]
        Lb = [P.sb(f"Lb{i}", [128, 2, 512], BF16) for i in range(NBUF)]
        Gb = [P.sb(f"Gb{i}", [128, 2, 512], BF16) for i in range(NBUF)]
        Wb = [P.sb(f"Wb{i}", [128, 2, 512], BF16) for i in range(NBUF)]
        triC = P.sb("triC", [128, 128], BF16)
        yev = [P.sb(f"yev{i}", [128, 512], BF16) for i in range(2)]
        Z = [P.ps(f"Zp{i}", [128, 2, 512]) for i in range(2)]
        X = P.ps("Xp", [128, 2, 512])
        Y = [P.ps(f"Yp{i}", [128, 512]) for i in range(2)]
        tt("dve", triC[:], ones[:], tri[:], ALU.subtract)
        zb = P.sb("zbB2", [128, 512], BF16)
        memset("dve", zb[:], 0.0)

        def load_hp(hp):
            s = hp % 2
            hl = L2 // 2
            load(KTt[s][:, 0:hl], KTs[hp, :, 0:hl])
            load(KTt[s][:, hl:L2], KTs[hp, :, hl:L2])
            load(Vt[s][:, 0:NB2 // 2, :], VSs[hp, :, 0:NB2 // 2, :])
            load(Vt[s][:, NB2 // 2:NB2, :], VSs[hp, :, NB2 // 2:NB2, :])
            load(QTt[s][:], QTs[hp, :, :])

        class Grp:
            pass

        def make_group(hp, qs, n, ycol, gidx, ubase):
            G = Grp()
            sl = hp % 2
            Kt, Vv, Qt = KTt[sl], Vt[sl], QTt[sl]
            qc = qs - (NH - 1) * 512
            kbmax = (qs + n - 2) // 128
            kbs = list(range(kbmax, -1, -1))
            U = len(kbs)
            G.U = U
            Yp = Y[gidx % 2]
            ye = yev[gidx % 2]

            def c0of(u):
                j = kbs[u] - qs // 128
                return 128 * j if (n == 512 and j >= 1) else 0

            def mmx(out, lhsT, rhs, start):
                return lambda e: e.matmul(out.ap, lhsT.ap, rhs.ap, start=start, stop=True,
                                          skip_group_check=True)

            def s_Z(u):
                kb = kbs[u]
                j = kb - qs // 128
                c0 = c0of(u)
                Zt = Z[(ubase + u) % 2]
                for r in range(2):
                    po = 64 * r
                    prs = [(Kt[po:po + 64, kb * 128:(kb + 1) * 128], Qt[po:po + 64, qc + c0:qc + n])]
                    if j >= 0:
                        if n == 512:
                            prs.append((ident[:], dmask[:, j, c0:n]))
                        else:
                            prs.append((ident[:], dmask[:, 0, 126:128]))
                    pe_group(Zt[:, r, c0:n], prs)

            def s_EL(u):
                kb = kbs[u]
                b = (ubase + u) % NBUF
                c0 = c0of(u)
                act(None, Eb[b][:, :, c0:n], Z[(ubase + u) % 2][:, :, c0:n], AF.Exp, scale=0.125,
                    bias=kbias[:, kb:kb + 1])
                act(None, Lb[b][:, :, c0:n], Eb[b][:, :, c0:n], AF.Ln, bias=1.0)

            def s_A(u):
                b = (ubase + u) % NBUF
                c0 = c0of(u)
                fns = [mmx(X[:, r, c0:n], tri[:], Lb[b][:, r, c0:n], False) for r in range(2)]
                S.op("pe", fns, reads=[tri.buf, Lb[b].buf], writes=[X.buf])

            def s_B(u):
                b = (ubase + u) % NBUF
                c0 = c0of(u)
                fns = [mmx(X[:, r, c0:n], triC[:], Lb[b][:, r, c0:n], False) for r in range(2)]
                S.op("pe", fns, reads=[triC.buf, Lb[b].buf], writes=[X.buf])

            def s_G(u):
                b = (ubase + u) % NBUF
                c0 = c0of(u)
                act(None, Gb[b][:, :, c0:n], X[:, :, c0:n], AF.Exp, scale=-1.0)
                tt("dve", Wb[b][:, :, c0:n], Eb[b][:, :, c0:n], Gb[b][:, :, c0:n], ALU.mult)

            def s_Y(u):
                kb = kbs[u]
                b = (ubase + u) % NBUF
                c0 = c0of(u)
                fns = []
                for r in range(2):
                    fns.append(mmx(Yp[:, c0:n], Vv[:, kb, 128 * r:128 * (r + 1)], Wb[b][:, r, c0:n], False))
                S.op("pe", fns, reads=[Vv.buf, Wb[b].buf], writes=[Yp.buf])

            def pre():
                s_Z(0)
                if U > 1:
                    s_Z(1)
                s_EL(0)
                if U > 2:
                    s_Z(2)

            def run(nxt):
                S.op("pe", [mmx(X[:, r, 0:n], ident[:], zb[:, 0:n], True) for r in range(2)],
                     reads=[ident.buf, zb.buf], writes=[X.buf])
                S.op("pe", [mmx(Yp[:, 0:n], ident[:], zb[:, 0:n], True)],
                     reads=[ident.buf, zb.buf], writes=[Yp.buf])
                s_A(0)
                for t in range(U):
                    if t + 1 < U:
                        s_EL(t + 1)
                    if t == U - 1 and nxt is not None:
                        nxt.pre()
                    s_G(t)
                    if t + 1 < U:
                        s_B(t)
                        s_A(t + 1)
                    if t + 3 < U:
                        s_Z(t + 3)
                    if t >= 1:
                        s_Y(t - 1)
                s_Y(U - 1)
                cp("dve", ye[:, 0:n], Yp[:, 0:n])
                store(YSB[:, hp, ycol:ycol + n], ye[:, 0:n])

            G.pre = pre
            G.run = run
            return G

        glist = []
        ub = 0
        for hp in range(4):
            for (qs, n, ycol) in groups:
                g = make_group(hp, qs, n, ycol, len(glist), ub)
                ub += g.U
                g.hp = hp
                glist.append(g)
        load_hp(0)
        load_hp(1)
        glist[0].pre()
        for i, g in enumerate(glist):
            nxt = glist[i + 1] if i + 1 < len(glist) else None
            if nxt is not None and nxt.hp != g.hp and nxt.hp + 1 < 4:
                pass
            g.run(nxt)
            if nxt is not None and nxt.hp != g.hp and nxt.hp + 1 < 4:
                load_hp(nxt.hp + 1)
        S.barrier()
        P.close()

    def load_w_bf16(P, W, src, nk, ncol, gvec, stg, piece):
        i = 0
        for k in range(nk):
            for c0 in range(0, ncol, piece):
                c1 = min(ncol, c0 + piece)
                st = stg[i % len(stg)]
                eng = "dve" if i % 2 == 0 else "act"
                i += 1
                load(st[:, 0:c1 - c0], src[k * 128:(k + 1) * 128, c0:c1])
                if gvec is None:
                    cp(eng, W[:, k, c0:c1], st[:, 0:c1 - c0])
                elif eng == "act":
                    act(None, W[:, k, c0:c1], st[:, 0:c1 - c0], AF.Identity, scale=gvec[:, k:k + 1])
                else:
                    ts(eng, W[:, k, c0:c1], st[:, 0:c1 - c0], gvec[:, k:k + 1], ALU.mult)

    def phase_C1():
        P = Pool_()
        WG = P.sb("WG", [128, 8, 2048], BF16)
        WSB = P.sb("WSB", [128, 4, 1024], BF16)
        WSW = P.sb("WSW", [128, 4, 1024], BF16)
        WO = P.sb("WO", [128, 8, 1024], BF16)
        stg = [P.sb(f"stgC1{i}", [128, 1024], F32) for i in range(4)]
        gm = P.sb("gm_c1", [128, 8], F32)
        xs = [P.sb(f"xsC{i}", [128, 8, 512], F32) for i in range(2)]
        sq = P.sb("sqC", [128, 8, 512], BF16)
        hT = [P.sb(f"hTC{i}", [128, 8, 512], BF16) for i in range(2)]
        lnv = P.sb("lnvC", [128, 512], F32)
        rs = P.sb("rsC", [128, 512], F32)
        ysb = [P.sb(f"ysbC{i}", [128, 4, 512], BF16) for i in range(2)]
        ysw = [P.sb(f"yswC{i}", [128, 4, 512], BF16) for i in range(2)]
        g1 = [P.sb(f"g1C{i}", [128, 512], F32) for i in range(2)]
        g2 = [P.sb(f"g2C{i}", [128, 512], F32) for i in range(2)]
        t1 = [P.sb(f"t1C{i}", [128, 512], F32) for i in range(2)]
        t2 = [P.sb(f"t2C{i}", [128, 512], F32) for i in range(2)]
        mT = P.sb("mT", [128, 8, 512], BF16)
        x1 = P.sb("x1C", [128, 8, 512], F32)
        ssps = P.ps("ssC", [128, 512])
        PS = [P.ps(f"psC{i}", [128, 512]) for i in range(7)]

        load(gm[:], gm_d[:, :])
        load_w_bf16(P, WG, w_in[:, 2304:4352], 8, 2048, gm, stg, 1024)
        load_w_bf16(P, WSB, w_sbp, 4, 1024, None, stg, 1024)
        load_w_bf16(P, WSW, w_swp, 4, 1024, None, stg, 1024)
        load_w_bf16(P, WO, w_out, 8, 1024, None, stg, 1024)
        psi = [0]

        def nextps():
            psi[0] = (psi[0] + 1) % 7
            return PS[psi[0]]

        def prologue(gi_):
            qs, n, oc = groups[gi_]
            s = gi_ % 2
            x = xs[s]
            load(x[:, :, 0:n], xT[:, qs:qs + n].rearrange("(k p) n -> p k n", p=128))
            load(ysb[s][:, :, 0:n], YSB[:, :, oc:oc + n])
            swc = qs - (H - 128)
            load(ysw[s][:, :, 0:n], YSW[:, :, swc:swc + n])
            rms_stats(P, x, n, sq, ssps, lnv, rs)
            for k in range(8):
                tt("dve" if k % 2 == 0 else "pool", hT[s][:, k, 0:n], x[:, k, 0:n], rs[:, 0:n], ALU.mult)

        def main(gi_):
            qs, n, oc = groups[gi_]
            s = gi_ % 2
            x = xs[s]
            h = hT[s]
            for o in range(8):
                pg1 = nextps()
                pe_group(pg1[:, 0:n], [(WG[:, k, o * 128:(o + 1) * 128], h[:, k, 0:n]) for k in range(8)])
                act(None, g1[o % 2][:, 0:n], pg1[:, 0:n], AF.Sigmoid)
                pg2 = nextps()
                pe_group(pg2[:, 0:n], [(WG[:, k, (8 + o) * 128:(9 + o) * 128], h[:, k, 0:n]) for k in range(8)])
                act(None, g2[o % 2][:, 0:n], pg2[:, 0:n], AF.Sigmoid)
                pa = nextps()
                pe_group(pa[:, 0:n], [(WSB[:, k, o * 128:(o + 1) * 128], ysb[s][:, k, 0:n]) for k in range(4)])
                pb = nextps()
                pe_group(pb[:, 0:n], [(WSW[:, k, o * 128:(o + 1) * 128], ysw[s][:, k, 0:n]) for k in range(4)])
                tt("dve", t1[o % 2][:, 0:n], pa[:, 0:n], g1[o % 2][:, 0:n], ALU.mult)
                tt("dve", t2[o % 2][:, 0:n], pb[:, 0:n], g2[o % 2][:, 0:n], ALU.mult)
                tt("pool", mT[:, o, 0:n], t1[o % 2][:, 0:n], t2[o % 2][:, 0:n], ALU.add)
            for o in range(8):
                ps = nextps()
                pe_group(ps[:, 0:n], [(WO[:, k, o * 128:(o + 1) * 128], mT[:, k, 0:n]) for k in range(8)])
                tt("dve", x1[:, o, 0:n], ps[:, 0:n], x[:, o, 0:n], ALU.add)
            store(X1s[:, :, oc:oc + n], x1[:, :, 0:n])

        prologue(0)
        for gi_ in range(len(groups)):
            if gi_ + 1 < len(groups):
                prologue(gi_ + 1)
            main(gi_)
        S.barrier()
        P.close()

    def phase_C2a():
        P = Pool_()
        WU = P.sb("WU", [128, 8, 2 * DFF], BF16)
        stg = [P.sb(f"stgU{i}", [128, 1408], F32) for i in range(4)]
        gf = P.sb("gf_s", [128, 8], F32)
        cw = P.sb("cw_s", [128, 44, 3], F32)
        cb = P.sb("cb_s", [128, 44], F32)
        xs = P.sb("xsU", [128, 8, 512], F32)
        sq = P.sb("sqU", [128, 8, 512], BF16)
        hT = [P.sb(f"hTU{i}", [128, 8, 512], BF16) for i in range(2)]
        lnv = P.sb("lnvU", [128, 512], F32)
        rs = P.sb("rsU", [128, 512], F32)
        yb = [P.sb(f"ybU{i}", [128, 512], F32) for i in range(4)]
        sg = [P.sb(f"sgU{i}", [128, 512], F32) for i in range(2)]
        aT = P.sb("aTU", [128, 22, 512], BF16)
        ssps = P.ps("ssU", [128, 512])
        PS = [P.ps(f"psU{i}", [128, 512]) for i in range(7)]

        load(gf[:], gf_d[:, :])
        load(cw[:], cw_d.rearrange("p (j k) -> p j k", k=3))
        load(cb[:], cb_d[:, :])
        load_w_bf16(P, WU, w_up, 8, 2 * DFF, gf, stg, 1408)
        psi = [0]

        def nextps():
            psi[0] = (psi[0] + 1) % 7
            return PS[psi[0]]

        GW = 510
        NG = (H + GW - 1) // GW

        def geo(g):
            a0 = g * GW
            w = min(GW, H - a0)
            return a0, w, w + 2

        def prologue(g):
            a0, w, n = geo(g)
            load(xs[:, :, 0:n], X1s[:, :, a0:a0 + n])
            rms_stats(P, xs, n, sq, ssps, lnv, rs)
            for k in range(8):
                tt("dve" if k % 2 == 0 else "pool", hT[g % 2][:, k, 0:n], xs[:, k, 0:n], rs[:, 0:n], ALU.mult)

        ybi = [0]

        def tiles(g):
            a0, w, n = geo(g)
            h = hT[g % 2]
            for j in range(22):
                ys = []
                for t in (j, 22 + j):
                    ps = nextps()
                    pe_group(ps[:, 0:n], [(WU[:, k, t * 128:(t + 1) * 128], h[:, k, 0:n]) for k in range(8)])
                    y = yb[ybi[0] % 4]
                    ybi[0] += 1
                    ys.append(y)
                    act(None, y[:, 0:w], ps[:, 2:n], AF.Identity, scale=cw[:, t, 2:3], bias=cb[:, t:t + 1])
                    stt(y[:, 0:w], ps[:, 1:n - 1], cw[:, t, 1:2], y[:, 0:w], ALU.mult, ALU.add)
                    stt(y[:, 0:w], ps[:, 0:w], cw[:, t, 0:1], y[:, 0:w], ALU.mult, ALU.add)
                sgt = sg[j % 2]
                act(None, sgt[:, 0:w], ys[0][:, 0:w], AF.Silu)
                tt("pool", aT[:, j, 0:w], sgt[:, 0:w], ys[1][:, 0:w], ALU.mult)
            store(ATs[:, :, 2 + a0:2 + a0 + w], aT[:, :, 0:w])

        prologue(0)
        for g in range(NG):
            if g + 1 < NG:
                prologue(g + 1)
            tiles(g)
        S.barrier()
        P.close()

    def phase_C2b():
        P = Pool_()
        WD = P.sb("WD", [128, 22, 1024], BF16)
        stg = [P.sb(f"stgD{i}", [128, 1024], F32) for i in range(4)]
        glf = P.sb("gl_s", [128, 8], F32)
        xs = [P.sb(f"xsD{i}", [128, 8, 512], F32) for i in range(2)]
        aT = [P.sb(f"aTD{i}", [128, 22, 512], BF16) for i in range(2)]
        x2 = P.sb("x2D", [128, 8, 512], F32)
        sq = P.sb("sqD", [128, 8, 512], BF16)
        lnv = P.sb("lnvD", [128, 512], F32)
        rs = P.sb("rsD", [128, 512], F32)
        ob = [P.sb(f"obD{i}", [128, 8, 512], F32) for i in range(2)]
        ssps = P.ps("ssD", [128, 512])
        PS = [P.ps(f"psD{i}", [128, 512]) for i in range(6)]

        load(glf[:], gl_d[:, :])
        load_w_bf16(P, WD, w_down, 22, 1024, None, stg, 1024)
        psi = [0]

        def nextps():
            psi[0] = (psi[0] + 1) % 6
            return PS[psi[0]]

        for gi_, (qs, n, oc) in enumerate(groups[1:]):
            s = gi_ % 2
            x = xs[s]
            load(x[:], X1s[:, :, oc:oc + n])
            load(aT[s][:], ATs[:, :, oc:oc + n])
            for o in range(8):
                ps = nextps()
                pe_group(ps[:], [(WD[:, k, o * 128:(o + 1) * 128], aT[s][:, k, :]) for k in range(22)])
                tt("dve", x2[:, o, :], ps[:], x[:, o, :], ALU.add)
            rms_stats(P, x2, 512, sq, ssps, lnv, rs)
            for o in range(8):
                stt(ob[s][:, o, :], x2[:, o, :], glf[:, o:o + 1], rs[:], ALU.mult, ALU.mult)
            store(out_d[:, :, oc - 2:oc - 2 + n].rearrange("k p n -> p k n"), ob[s][:])
        S.barrier()
        P.close()

    S.barrier()
    gtmp.close()
    ph = phases.split(",")
    if "A" in ph:
        phase_A()
    if "B1" in ph:
        phase_B1()
    if "B2" in ph:
        phase_B2()
    if "C1" in ph:
        phase_C1()
    if "C2a" in ph:
        phase_C2a()
    if "C2b" in ph:
        phase_C2b()
    gl.close()

    with ExitStack() as es:
        sems = {}
        for k in S.keys:
            nm = k if isinstance(k, str) else f"d{k[1]}"
            sems[k] = es.enter_context(nc.semaphore("s_" + nm))
        block = es.enter_context(nc.Block())

        @block.tensor
        def _(e):
            S.replay("pe", e, sems)

        @block.scalar
        def _(e):
            S.replay("act", e, sems)

        @block.vector
        def _(e):
            S.replay("dve", e, sems)

        @block.gpsimd
        def _(e):
            S.replay("pool", e, sems)

        @block.sync
        def _(e):
            S.replay("sp", e, sems)

    return nc


def _t5_bucket(dist):
    dist = np.asarray(dist, np.int32)
    max_exact = 16
    d = np.maximum(dist, 1).astype(np.float32)
    large = max_exact + (np.log(d / np.float32(max_exact)) / np.float32(np.log(128 / max_exact))
                         * np.float32(32 - max_exact)).astype(np.int32)
    large = np.minimum(large, 31)
    return np.where(dist < max_exact, dist, large)


def _consts():
    c = np.zeros((128, 2688), np.float32)
    c[:, 0:128] = 1.0
    j = np.arange(128)[:, None]
    s = np.arange(128)[None, :]
    c[:, 128:256] = (j >= s).astype(np.float32)
    c[:, 256:384] = np.eye(128, dtype=np.float32)
    q = np.arange(512)[None, :]
    for jj in range(4):
        valid = (128 * jj + j) < q
        c[:, 384 + 512 * jj:384 + 512 * (jj + 1)] = np.where(valid, 0.0, 8.0 * NEGM)
    c[:, 2432:2432 + 64] = 1.0
    c[:, 2432 + 128 + 64:2432 + 256] = 1.0
    return c


def prep_inputs(NH, x, g_mix, w_in, w_sb_proj, w_sw_proj, w_out, rel_bias, sinks,
                g_ffn, w_up, conv_w, conv_b, w_down, g_final):
    H = NH * 512
    B = x.shape[0]
    f = lambda a: np.ascontiguousarray(np.asarray(a, dtype=np.float32))
    x = f(x)
    lay8 = lambda g: f(np.asarray(g, np.float32).reshape(8, 128).T)
    rel_bias = np.asarray(rel_bias, np.float32)
    k = np.arange(128)[:, None]
    q = np.arange(128)[None, :]
    bm = np.zeros((128, 2, 2, 4, 128), np.float32)
    d0 = 128 + q - k
    d1 = q - k
    b0 = _t5_bucket(np.clip(d0, 0, 255))
    b1 = _t5_bucket(np.clip(d1, 0, 255))
    for h in range(8):
        bm[:, 0, h % 2, h // 2, :] = np.where(d0 <= 127, rel_bias[b0, h], NEGM)
        bm[:, 1, h % 2, h // 2, :] = np.where(d1 >= 0, rel_bias[b1, h], NEGM)
    sinks = np.asarray(sinks, np.float32)
    sk = np.zeros((128, 4), np.float32)
    for hp in range(4):
        sk[0:64, hp] = sinks[2 * hp]
        sk[64:128, hp] = sinks[2 * hp + 1]
    cw = np.asarray(conv_w, np.float32)
    cwl = np.ascontiguousarray(cw.reshape(3, 44, 128).transpose(2, 1, 0)).reshape(128, 132)
    cbl = f(np.asarray(conv_b, np.float32).reshape(44, 128).T)
    shared = {
        "w_in": f(w_in), "w_sbp": f(w_sb_proj), "w_swp": f(w_sw_proj), "w_out": f(w_out),
        "w_up": f(w_up), "w_down": f(w_down), "gm": lay8(g_mix), "gf": lay8(g_ffn), "gl": lay8(g_final),
        "cw": f(cwl), "cb": cbl, "bm": f(bm.reshape(128, 2048)), "sk": sk, "cst": _consts(),
    }
    in_maps = []
    for c in range(2 * B):
        b, p = c // 2, c % 2
        xl = np.zeros((2 * H, D), np.float32)
        kb = np.zeros((128, 2 * H // 128), np.float32)
        if p == 1:
            xl[:] = x[b]
        else:
            xl[H:] = x[b, :H]
            kb[:, :H // 128] = NEGM
        m = dict(shared)
        m["xT"] = np.ascontiguousarray(xl.T)
        m["kbias"] = kb
        in_maps.append(m)
    return in_maps


_NC_CACHE = {}


def run(NH, phases="A,B1,B2,C1,C2a,C2b", **inputs):
    x = np.asarray(inputs["x"])
    B = x.shape[0]
    H = NH * 512
    in_maps = prep_inputs(NH, **inputs)
    if (NH, phases) not in _NC_CACHE:
        _NC_CACHE[(NH, phases)] = build_nc(NH, phases)
    nc = _NC_CACHE[(NH, phases)]
    res = run_bass_kernel_spmd(nc, in_maps, core_ids=list(range(2 * B)))
    out = np.zeros((B, 2 * H, D), np.float32)
    for c in range(2 * B):
        b, p = c // 2, c % 2
        o = np.asarray(res.results[c]["out"]).reshape(D, H)
        out[b, p * H:(p + 1) * H, :] = o.T
    return out


def kernel(**inputs):
    return run(8, **inputs)
```
